# Optimizing a Trainium2 kernel written in Bass

```python
import jax, jax.numpy as jnp
from jax import lax
import numpy as np

D_MODEL = 1024
BATCH = 8
SEQ = 2048
DEPTH = 2

HEAD_DIM = 64
N_ATTN_HEADS = 8
ATTN_WIDTH = N_ATTN_HEADS * HEAD_DIM
DILATED_PATTERNS = ((128, 1), (512, 4), (2048, 16))
ROPE_THETA = 10000.0
NEG_INF = -1e30

N_DELTA_HEADS = 4
DELTA_DK = 128
DELTA_DV = 128
DELTA_K_WIDTH = N_DELTA_HEADS * DELTA_DK
DELTA_V_WIDTH = N_DELTA_HEADS * DELTA_DV
DELTA_CONV = 4
DELTA_CHUNK = 64

MIX_WIDTH = ATTN_WIDTH + DELTA_V_WIDTH
IN_COLS = 3 * ATTN_WIDTH + 2 * DELTA_K_WIDTH + 2 * DELTA_V_WIDTH + 2 * N_DELTA_HEADS

D_FF = 2816
FFN_CONV = 3
EPS = 1e-6

kernel_name = 'hybrid_dilated_swa_gated_deltanet_convglu'


def rms_norm(x, w):
    xf = x.astype(jnp.float32)
    y = xf * lax.rsqrt(jnp.mean(xf * xf, axis=-1, keepdims=True) + EPS)
    return (y * w.astype(jnp.float32)).astype(x.dtype)


def rope_tables(seq, dim):
    inv = 1.0 / (ROPE_THETA ** (jnp.arange(0, dim, 2, dtype=jnp.float32) / dim))
    ang = jnp.arange(seq, dtype=jnp.float32)[:, None] * inv[None, :]
    return jnp.cos(ang), jnp.sin(ang)


def apply_rope(x, cos, sin):
    x1, x2 = jnp.split(x, 2, axis=-1)
    c = cos[None, :, None, :]
    s = sin[None, :, None, :]
    return jnp.concatenate([x1 * c - x2 * s, x1 * s + x2 * c], axis=-1)


def causal_dwconv(x, w):
    K, C = w.shape
    return lax.conv_general_dilated(
        x, w[:, None, :].astype(x.dtype), window_strides=(1,), padding=[(K - 1, 0)],
        dimension_numbers=('NWC', 'WIO', 'NWC'), feature_group_count=C)


def dilated_branch(q, k, v, window, dilation):
    B, S, H, D = q.shape
    n = window // dilation
    L = S // dilation
    nb = -(-L // n)
    Lp = nb * n

    def split(t):
        return t.reshape(B, L, dilation, H, D).transpose(0, 2, 3, 1, 4)

    qs = jnp.pad(split(q), ((0, 0), (0, 0), (0, 0), (0, Lp - L), (0, 0)))
    qs = qs.reshape(B, dilation, H, nb, n, D)

    def kv_blocks(t):
        tp = jnp.pad(split(t), ((0, 0), (0, 0), (0, 0), (n, Lp - L), (0, 0)))
        prev = tp[:, :, :, :Lp].reshape(B, dilation, H, nb, n, D)
        cur = tp[:, :, :, n:].reshape(B, dilation, H, nb, n, D)
        return jnp.concatenate([prev, cur], axis=4)

    kb = kv_blocks(k)
    vb = kv_blocks(v)
    s = jnp.einsum('brhnqd,brhnkd->brhnqk', qs, kb) * (D ** -0.5)
    a = jnp.arange(n)[:, None]
    c = jnp.arange(2 * n)[None, :]
    key_pos = (jnp.arange(nb)[:, None, None] - 1) * n + c[None]
    valid = ((c >= a) & (c <= a + n))[None] & (key_pos >= 0)
    s = jnp.where(valid, s, NEG_INF)
    m = jnp.max(s, axis=-1, keepdims=True)
    e = jnp.exp(s - m)
    l = jnp.sum(e, axis=-1, keepdims=True)
    o = jnp.einsum('brhnqk,brhnkd->brhnqd', e, vb) / l
    lse = (m + jnp.log(l))[..., 0]

    o = o.reshape(B, dilation, H, Lp, D)[:, :, :, :L]
    o = o.transpose(0, 3, 1, 2, 4).reshape(B, S, H, D)
    lse = lse.reshape(B, dilation, H, Lp)[:, :, :, :L]
    lse = lse.transpose(0, 3, 1, 2).reshape(B, S, H)
    return o, lse


def dilated_attention(q, k, v):
    outs, lses = [], []
    for window, dilation in DILATED_PATTERNS:
        o, lse = dilated_branch(q, k, v, window, dilation)
        outs.append(o)
        lses.append(lse)
    wts = jax.nn.softmax(jnp.stack(lses, axis=0), axis=0)
    return jnp.sum(wts[..., None] * jnp.stack(outs, axis=0), axis=0)


def gated_delta_rule(q, k, v, beta, g):
    B, S, H, K = q.shape
    V = v.shape[-1]
    C = DELTA_CHUNK
    N = S // C
    q = q * lax.rsqrt(jnp.sum(q * q, axis=-1, keepdims=True) + EPS) * (K ** -0.5)
    k = k * lax.rsqrt(jnp.sum(k * k, axis=-1, keepdims=True) + EPS)

    def chunk(t):
        return jnp.moveaxis(t.reshape((B, N, C, H) + t.shape[3:]), 3, 1)

    q, k, v, beta, g = chunk(q), chunk(k), chunk(v), chunk(beta), chunk(g)
    g = jnp.cumsum(g, axis=-1)
    tril = jnp.tril(jnp.ones((C, C), dtype=bool))
    strict = jnp.tril(jnp.ones((C, C), dtype=bool), -1)
    decay = jnp.exp(jnp.where(tril, g[..., :, None] - g[..., None, :], -jnp.inf))
    kb = k * beta[..., None]
    A = jnp.where(strict, jnp.einsum('bhnik,bhnjk->bhnij', kb, k) * decay, 0.0)
    eye = jnp.eye(C, dtype=A.dtype)
    T = lax.linalg.triangular_solve(A + eye, jnp.broadcast_to(eye, A.shape), left_side=True,
                                    lower=True, unit_diagonal=True)
    u = jnp.einsum('bhnij,bhnjv->bhniv', T, v * beta[..., None])
    w = jnp.einsum('bhnij,bhnjk->bhnik', T, kb * jnp.exp(g)[..., None])
    qk = jnp.einsum('bhnik,bhnjk->bhnij', q, k) * decay
    q_dec = q * jnp.exp(g)[..., None]
    k_dec = k * jnp.exp(g[..., -1:] - g)[..., None]
    g_last = jnp.exp(g[..., -1])

    def step(state, xs):
        q_i, k_i, u_i, w_i, qk_i, gl = xs
        v_new = u_i - jnp.einsum('bhck,bhkv->bhcv', w_i, state)
        o = jnp.einsum('bhck,bhkv->bhcv', q_i, state) + jnp.einsum('bhij,bhjv->bhiv', qk_i, v_new)
        state = state * gl[..., None, None] + jnp.einsum('bhck,bhcv->bhkv', k_i, v_new)
        return state, o

    xs = (jnp.moveaxis(q_dec, 2, 0), jnp.moveaxis(k_dec, 2, 0), jnp.moveaxis(u, 2, 0),
          jnp.moveaxis(w, 2, 0), jnp.moveaxis(qk, 2, 0), jnp.moveaxis(g_last, 2, 0))
    _, o = lax.scan(step, jnp.zeros((B, H, K, V), jnp.float32), xs)
    return jnp.transpose(o, (1, 0, 3, 2, 4)).reshape(B, S, H, V)


def hybrid_mixer(h, w_in, dn_conv_w, dn_a_log, dn_dt_bias, dn_norm_w, w_out, cos, sin):
    B, S, _ = h.shape
    proj = h @ w_in
    cuts = [ATTN_WIDTH, 2 * ATTN_WIDTH, 3 * ATTN_WIDTH,
            3 * ATTN_WIDTH + 2 * DELTA_K_WIDTH + DELTA_V_WIDTH,
            3 * ATTN_WIDTH + 2 * DELTA_K_WIDTH + 2 * DELTA_V_WIDTH,
            3 * ATTN_WIDTH + 2 * DELTA_K_WIDTH + 2 * DELTA_V_WIDTH + N_DELTA_HEADS]
    aq, ak, av, dn_qkv, dn_z, dn_b, dn_a = jnp.split(proj, cuts, axis=-1)

    aq = apply_rope(aq.reshape(B, S, N_ATTN_HEADS, HEAD_DIM).astype(jnp.float32), cos, sin)
    ak = apply_rope(ak.reshape(B, S, N_ATTN_HEADS, HEAD_DIM).astype(jnp.float32), cos, sin)
    av = av.reshape(B, S, N_ATTN_HEADS, HEAD_DIM).astype(jnp.float32)
    attn_out = dilated_attention(aq, ak, av).reshape(B, S, ATTN_WIDTH).astype(h.dtype)

    qkv = jax.nn.silu(causal_dwconv(dn_qkv, dn_conv_w)).astype(jnp.float32)
    dq, dk, dv = jnp.split(qkv, [DELTA_K_WIDTH, 2 * DELTA_K_WIDTH], axis=-1)
    beta = jax.nn.sigmoid(dn_b.astype(jnp.float32))
    g = -jnp.exp(dn_a_log.astype(jnp.float32)) * jax.nn.softplus(
        dn_a.astype(jnp.float32) + dn_dt_bias.astype(jnp.float32))
    o = gated_delta_rule(dq.reshape(B, S, N_DELTA_HEADS, DELTA_DK),
                         dk.reshape(B, S, N_DELTA_HEADS, DELTA_DK),
                         dv.reshape(B, S, N_DELTA_HEADS, DELTA_DV), beta, g)
    z = dn_z.reshape(B, S, N_DELTA_HEADS, DELTA_DV).astype(jnp.float32)
    dn_out = (rms_norm(o, dn_norm_w) * jax.nn.silu(z)).reshape(B, S, DELTA_V_WIDTH).astype(h.dtype)

    return jnp.concatenate([attn_out, dn_out], axis=-1) @ w_out


def conv_glu_ffn(h, ffn_w_in, ffn_conv_w, ffn_conv_b, ffn_w_out):
    u = causal_dwconv(h @ ffn_w_in, ffn_conv_w) + ffn_conv_b
    gate, up = jnp.split(u, 2, axis=-1)
    return (jax.nn.gelu(gate, approximate=True) * up) @ ffn_w_out


def setup_inputs(seed: int = 0) -> dict:
    key = jax.random.key(seed)
    ks = jax.random.split(key, 16)
    f32 = jnp.float32
    nrm = lambda k, shape, scale: jax.random.normal(k, shape, f32) * scale
    dt = jnp.exp(jax.random.uniform(ks[4], (DEPTH, N_DELTA_HEADS), f32) *
                 (jnp.log(0.1) - jnp.log(0.001)) + jnp.log(0.001))
    return {
        'x': nrm(ks[0], (BATCH, SEQ, D_MODEL), 1.0),
        'w_in': nrm(ks[1], (DEPTH, D_MODEL, IN_COLS), D_MODEL ** -0.5),
        'dn_conv_w': nrm(ks[2], (DEPTH, DELTA_CONV, 2 * DELTA_K_WIDTH + DELTA_V_WIDTH), DELTA_CONV ** -0.5),
        'dn_a_log': jnp.log(jax.random.uniform(ks[3], (DEPTH, N_DELTA_HEADS), f32, 1.0, 16.0)),
        'dn_dt_bias': dt + jnp.log(-jnp.expm1(-dt)),
        'dn_norm_w': 1.0 + nrm(ks[5], (DEPTH, DELTA_DV), 0.05),
        'w_out': nrm(ks[6], (DEPTH, MIX_WIDTH, D_MODEL), MIX_WIDTH ** -0.5),
        'ffn_w_in': nrm(ks[7], (DEPTH, D_MODEL, 2 * D_FF), D_MODEL ** -0.5),
        'ffn_conv_w': nrm(ks[8], (DEPTH, FFN_CONV, 2 * D_FF), FFN_CONV ** -0.5),
        'ffn_conv_b': nrm(ks[9], (DEPTH, 2 * D_FF), 0.01),
        'ffn_w_out': nrm(ks[10], (DEPTH, D_FF, D_MODEL), D_FF ** -0.5),
        'norm_pre_mix': 1.0 + nrm(ks[11], (DEPTH, D_MODEL), 0.05),
        'norm_post_mix': 1.0 + nrm(ks[12], (DEPTH, D_MODEL), 0.05),
        'norm_pre_ffn': 1.0 + nrm(ks[13], (DEPTH, D_MODEL), 0.05),
        'norm_post_ffn': 1.0 + nrm(ks[14], (DEPTH, D_MODEL), 0.05),
    }


def reference(x, w_in, dn_conv_w, dn_a_log, dn_dt_bias, dn_norm_w, w_out, ffn_w_in, ffn_conv_w,
              ffn_conv_b, ffn_w_out, norm_pre_mix, norm_post_mix, norm_pre_ffn, norm_post_ffn):
    cos, sin = rope_tables(x.shape[1], HEAD_DIM)
    for l in range(DEPTH):
        h = rms_norm(x, norm_pre_mix[l])
        mix = hybrid_mixer(h, w_in[l], dn_conv_w[l], dn_a_log[l], dn_dt_bias[l], dn_norm_w[l],
                           w_out[l], cos, sin)
        x = x + rms_norm(mix, norm_post_mix[l]).astype(x.dtype)
        h = rms_norm(x, norm_pre_ffn[l])
        f = conv_glu_ffn(h, ffn_w_in[l], ffn_conv_w[l], ffn_conv_b[l], ffn_w_out[l])
        x = x + rms_norm(f, norm_post_ffn[l]).astype(x.dtype)
    return x
```

```python
import numpy as np
import ml_dtypes
from contextlib import ExitStack
import concourse.bass as bass
import concourse.mybir as mybir
from concourse.bass_utils import run_bass_kernel_spmd

F32 = mybir.dt.float32
BF16 = mybir.dt.bfloat16
AF = mybir.ActivationFunctionType
ALU = mybir.AluOpType

S = 2048
D = 1024
NT = 16
KC = 8
DEPTH = 2
DFF = 2816
NFC = 22
EPS = 1e-6
N_CORES = 8


class Res:
    __slots__ = ("w", "r", "excl", "name")

    def __init__(self, name="", excl=False):
        self.w = None
        self.r = []
        self.excl = excl
        self.name = name


def run_interleaved(gens):
    active = list(gens)
    while active:
        for g in list(active):
            try:
                next(g)
            except StopIteration:
                active.remove(g)


def run_pipelined(items):
    pending = sorted(items, key=lambda it: it[0])
    active = []
    step = 0
    while pending or active:
        while pending and pending[0][0] <= step:
            _, rate, g = pending.pop(0)
            active.append((rate, g))
        for it in list(active):
            rate, g = it
            for _ in range(rate):
                try:
                    next(g)
                except StopIteration:
                    active.remove(it)
                    break
        step += 1


class Sched:
    ENG = ("pe", "act", "dve", "pool", "sp")

    def __init__(self, nc, es):
        self.nc = nc
        self.es = es
        self.sems = {}
        self.cnt = {}
        for e in self.ENG:
            self.sems[e] = es.enter_context(nc.semaphore("sem_" + e))
            self.cnt[e] = 0
        self.waited = {e: {} for e in self.ENG}
        self.stream = {e: [] for e in self.ENG}
        self.nwaits = 0

    def dma_sem(self, name):
        self.sems[name] = self.es.enter_context(self.nc.semaphore("dq_" + name))
        self.cnt[name] = 0
        return name

    def _collect(self, eng, reads, writes):
        need = {}

        def add(tok, kind):
            if tok is None:
                return
            key, val = tok
            if key == eng:
                if eng in ("pe", "sp"):
                    return
            if self.waited[eng].get(key, 0) >= val:
                return
            if need.get(key, 0) < val:
                need[key] = val

        for res in reads:
            add(res.w, "raw")
            if res.excl:
                for t in res.r:
                    if t[0] != eng:
                        add(t, "war")
        for res in writes:
            add(res.w, "waw")
            for t in res.r:
                add(t, "war")
        for k, v in need.items():
            self.waited[eng][k] = v
        return list(need.items())

    def _commit(self, tok, reads, writes):
        for res in reads:
            res.r = [t for t in res.r if t[0] != tok[0]] + [tok]
        for res in writes:
            res.w = tok
            res.r = []

    def op(self, eng, fn, reads=(), writes=()):
        waits = self._collect(eng, reads, writes)
        self.cnt[eng] += 1
        tok = (eng, self.cnt[eng])
        sems = self.sems
        semh = sems[eng]
        self.nwaits += len(waits)

        def emit(h, waits=waits, fn=fn, semh=semh):
            for k, v in waits:
                h.wait_ge(sems[k], v)
            fn(h).then_inc(semh, 1)

        self.stream[eng].append(emit)
        self._commit(tok, reads, writes)
        return tok

    def dma(self, eng, semkey, fn, reads=(), writes=()):
        waits = self._collect(eng, reads, writes)
        self.cnt[semkey] += 16
        tok = (semkey, self.cnt[semkey])
        sems = self.sems
        semh = sems[semkey]

        def emit(h, waits=waits, fn=fn, semh=semh):
            for k, v in waits:
                h.wait_ge(sems[k], v)
            fn(h).then_inc(semh, 16)

        self.stream[eng].append(emit)
        self._commit(tok, reads, writes)
        return tok

    def wait_all(self, eng, toks):
        need = {}
        for key, val in toks:
            if need.get(key, 0) < val:
                need[key] = val
        waits = list(need.items())
        sems = self.sems

        def emit(h, waits=waits):
            for k, v in waits:
                h.wait_ge(sems[k], v)

        self.stream[eng].append(emit)

    def barrier(self):
        snap = dict(self.cnt)
        for e in self.ENG:
            waits = [(k, v) for k, v in snap.items() if v > 0 and k != e and self.waited[e].get(k, 0) < v]
            for k, v in waits:
                self.waited[e][k] = v
            if e in ("act", "dve", "pool") and snap[e] > 0:
                waits.append((e, snap[e]))
                self.waited[e][e] = snap[e]
            sems = self.sems

            def emit(h, waits=waits):
                for k, v in waits:
                    h.wait_ge(sems[k], v)

            self.stream[e].append(emit)

    def emit_all(self):
        nc = self.nc
        streams = self.stream
        self.stream = {e: [] for e in self.ENG}
        self._emit(streams)

    def _emit(self, streams):
        nc = self.nc
        self_stream = streams
        with nc.Block() as block:
            @block.tensor
            def _(h):
                for f in self_stream["pe"]:
                    f(h)

            @block.scalar
            def _(h):
                for f in self_stream["act"]:
                    f(h)

            @block.vector
            def _(h):
                for f in self_stream["dve"]:
                    f(h)

            @block.gpsimd
            def _(h):
                for f in self_stream["pool"]:
                    f(h)

            @block.sync
            def _(h):
                for f in self_stream["sp"]:
                    f(h)


def build_program(depth=DEPTH, stage=99, dbg=False):
    nc = bass.Bass("TRN2", target_bir_lowering=False)
    es = ExitStack()
    sc = Sched(nc, es)

    def dram_in(name, shape, dt=F32):
        return nc.dram_tensor(name, list(shape), dt, kind="ExternalInput").ap()

    def dram_out(name, shape, dt=F32):
        return nc.dram_tensor(name, list(shape), dt, kind="ExternalOutput").ap()

    uid = [0]

    def sb(e, name, shape, dt):
        uid[0] += 1
        return e.enter_context(nc.sbuf_tensor("sb%d_%s" % (uid[0], name), list(shape), dt))

    x_d = dram_in("x", [S, D])
    y_d = dram_out("y", [S, D])
    w_att_d = dram_in("w_att", [depth, 4, 128, KC, 640])
    ident_d = dram_in("ident", [128, 128])
    ropec_d = dram_in("rope_c", [128, S])
    ropes_d = dram_in("rope_s", [128, S])
    amask_d = dram_in("amask", [128, 1024])
    npre_d = dram_in("npre", [128, depth, 2, KC])
    w_dn_d = dram_in("w_dn", [depth, 4, 128, KC, 512])
    w_ba_d = dram_in("w_ba", [depth, 128, KC, 8])
    dn_cw_d = dram_in("dn_cw", [128, depth, 4, 3, 4])
    dn_hp_d = dram_in("dn_hp", [128, depth, 2, 4])
    dn_nwT_d = dram_in("dn_nwT", [depth, 128, 1])
    dn_c_d = dram_in("dn_c", [128, 8, 128])
    w_o_d = dram_in("w_o", [depth, 128, KC, D])
    npost_d = dram_in("npost", [128, depth, 2, D])
    w_f1_d = dram_in("w_f1", [depth, NFC, 128, KC, 256])
    w_f2_d = dram_in("w_f2", [depth, 128, NFC, D])
    fcw_d = dram_in("fcw", [128, depth, 2 * NFC, 4])
    dbg_d = {}
    if dbg:
        dbg_d["hT"] = dram_out("dbg_hT", [128, KC, S], BF16)
        dbg_d["catA"] = dram_out("dbg_catA", [128, 4, S], BF16)
        dbg_d["catD"] = dram_out("dbg_catD", [128, 4, S], BF16)
        dbg_d["xmid"] = dram_out("dbg_xmid", [S, D], F32)

    x_sb = sb(es, "x_sb", [128, NT, D], F32)
    hT_box = [None]
    ident = sb(es, "ident", [128, 128], BF16)
    npre = sb(es, "npre", [128, depth, 2, KC], F32)
    R_x = [Res("x%d" % t) for t in range(NT)]
    R_hT = [Res("hT%d" % t) for t in range(NT)]
    R_const = Res("const")

    psum = [es.enter_context(nc.psum_tensor("ps%d" % i, [128, 512], F32)) for i in range(8)]
    R_ps = [Res("ps%d" % i, excl=True) for i in range(8)]

    q_out = sc.dma_sem("out")
    q_ld = {"sp": [sc.dma_sem("ld%d" % i) for i in range(4)],
            "pool": [sc.dma_sem("ldp%d" % i) for i in range(3)]}
    ld_rr = {"sp": 0, "pool": 0}

    def phase_load(out_ap, in_ap, res, eng="sp"):
        qs = q_ld[eng]
        q = qs[ld_rr[eng] % len(qs)]
        ld_rr[eng] += 1
        return sc.dma(eng, q, lambda h: h.dma_start(out=out_ap, in_=in_ap), writes=[res])

    def end_phase():
        sc.barrier()
        sc.emit_all()

    xv = x_d.rearrange("(t p) d -> p t d", p=128)
    q_x = sc.dma_sem("xin")
    for t4 in range(4):
        sc.dma("sp", q_x, lambda h, t4=t4: h.dma_start(out=x_sb[:, t4 * 4:(t4 + 1) * 4, :], in_=xv[:, t4 * 4:(t4 + 1) * 4, :]),
               writes=R_x[t4 * 4:(t4 + 1) * 4])
    for r in R_x:
        r.w = (q_x, sc.cnt[q_x])
    phase_load(ident[:], ident_d, R_const, eng="pool")
    phase_load(npre[:], npre_d, R_const)
    end_phase()

    def prenorm(l, which, hT, tiles, R_hTl):
        e = ExitStack()
        ss_all = sb(e, "ss_all", [128, NT], F32)
        rstd_all = sb(e, "rstd_all", [128, NT], F32)
        junk = sb(e, "junk", [128, D], BF16)
        hb = [sb(e, "hb%d" % i, [128, D], BF16) for i in range(2)]
        R_hb = [Res("hb%d" % i) for i in range(2)]
        R_ss = Res("ss")
        sc.op("dve", lambda h: h.memset(ss_all[:], 0.0), writes=[R_ss])
        for t in tiles:
            sc.op("act", lambda h, t=t: h.activation(out=junk[:], in_=x_sb[:, t, :], func=AF.Square,
                                                     accum_out=ss_all[:, t:t + 1]),
                  reads=[R_x[t]], writes=[R_ss])
        sc.op("dve", lambda h: h.tensor_scalar(out=rstd_all[:], in0=ss_all[:], scalar1=1.0 / D, scalar2=EPS,
                                               op0=ALU.mult, op1=ALU.add), reads=[R_ss], writes=[R_ss])
        sc.op("act", lambda h: h.activation(out=rstd_all[:], in_=rstd_all[:], func=AF.Sqrt),
              reads=[R_ss], writes=[R_ss])
        sc.op("dve", lambda h: h.reciprocal(out=rstd_all[:], in_=rstd_all[:]), reads=[R_ss], writes=[R_ss])
        for ti, t in enumerate(tiles):
            b = t % 2
            pb = 6 + (t % 2)
            sc.op("act", lambda h, t=t, b=b: h.activation(out=hb[b][:], in_=x_sb[:, t, :], func=AF.Copy,
                                                          scale=rstd_all[:, t:t + 1]),
                  reads=[R_x[t], R_ss], writes=[R_hb[b]])
            pst = psum[pb][:].bitcast(BF16)
            for k in range(KC):
                sc.op("pe", lambda h, k=k, b=b, pst=pst: h.transpose(pst[:, k * 128:(k + 1) * 128],
                                                                     hb[b][:, k * 128:(k + 1) * 128], ident[:]),
                      reads=[R_hb[b], R_const], writes=[R_ps[pb]])
            wv = npre[:, l, which, :].unsqueeze(2).to_broadcast([128, KC, 128])
            sc.op("dve", lambda h, ti=ti, pst=pst, wv=wv: h.tensor_tensor(
                out=hT[:, :, ti * 128:(ti + 1) * 128], in0=pst.rearrange("p (k c) -> p k c", k=KC), in1=wv, op=ALU.mult),
                reads=[R_ps[pb], R_const], writes=[R_hTl[ti]])
        end_phase()
        e.close()

    BR = ((1, 16), (4, 4), (16, 1))

    def attention_phase(l, catA, R_catA):
        hT = hT_box[0]
        e = ExitStack()
        ropec = sb(e, "ropec", [128, S], F32)
        ropes = sb(e, "ropes", [128, S], F32)
        amask = sb(e, "amask", [128, 1024], BF16)
        R_tab = Res("tab")
        phase_load(ropec[:], ropec_d, R_tab)
        phase_load(ropes[:], ropes_d, R_tab)
        phase_load(amask[:], amask_d, R_tab, eng="pool")
        w_att = [sb(e, "w_att%d" % i, [128, KC, 640], BF16) for i in range(2)]
        R_watt = [Res("watt%d" % i) for i in range(2)]
        q_watt = [sc.dma_sem("watt%d_%d" % (l, i)) for i in range(2)]
        qT = sb(e, "qT", [128, S], BF16)
        kTe = sb(e, "kTe", [128, S], BF16)
        kTo = sb(e, "kTo", [128, S], BF16)
        R_qT, R_kT = Res("qT"), Res("kT")
        vT = sb(e, "vT", [128, S], BF16)
        R_vT = Res("vT")
        rtmp = [sb(e, "rtmp%d" % i, [128, 2, 512], F32) for i in range(2)]
        R_rtmp = [Res("rtmp%d" % i) for i in range(2)]
        VX = sb(e, "vext", [128, 16, 2, 128], BF16)
        R_vext = Res("vext")
        pT = [sb(e, "pT%d" % i, [128, 512], BF16) for i in range(6)]
        R_pT = [Res("pT%d" % i) for i in range(6)]
        acc = sb(e, "acc", [128, 2, S], F32)
        R_acc = [Res("acc0"), Res("acc1")]

        sc.op("pool", lambda h: h.memset(kTe[64:128, :], 0.0), writes=[R_kT])
        sc.op("pool", lambda h: h.memset(kTo[0:64, :], 0.0), writes=[R_kT])
        sc.op("pool", lambda h: h.memset(VX[:, :, :, 64:128], 1.0), writes=[R_vext])

        def load_watt(hp):
            i = hp % 2
            sc.dma("pool", q_watt[i], lambda h, i=i: h.dma_start(out=w_att[i][:], in_=w_att_d[l, hp]),
                   writes=[R_watt[i]])

        def attention_unit(hp):
            W = w_att[hp % 2]
            RW = R_watt[hp % 2]
            for qk in range(2):
                for g in range(4):
                    pa, pb = (0, 1) if g % 2 == 0 else (2, 3)
                    for ci, pbk in ((0, pa), (1, pb)):
                        c0 = qk * 256 + ci * 128
                        for k in range(KC):
                            sc.op("pe", lambda h, k=k, c0=c0, pbk=pbk, g=g: h.matmul(
                                psum[pbk][:], lhsT=W[:, k, c0:c0 + 128], rhs=hT[:, k, g * 512:(g + 1) * 512],
                                start=(k == 0), stop=(k == KC - 1)),
                                reads=[RW] + R_hT[g * 4:(g + 1) * 4], writes=[R_ps[pbk]])
                    rb = g % 2
                    sc.op("dve", lambda h, pa=pa, g=g, rb=rb: h.tensor_tensor(
                        out=rtmp[rb][:, 0, :], in0=psum[pa][:], in1=ropec[:, g * 512:(g + 1) * 512], op=ALU.mult),
                        reads=[R_ps[pa], R_tab], writes=[R_rtmp[rb]])
                    sc.op("dve", lambda h, pb=pb, g=g, rb=rb: h.tensor_tensor(
                        out=rtmp[rb][:, 1, :], in0=psum[pb][:], in1=ropes[:, g * 512:(g + 1) * 512], op=ALU.mult),
                        reads=[R_ps[pb], R_tab], writes=[R_rtmp[rb]])
                    if qk == 0:
                        sc.op("pool", lambda h, g=g, rb=rb: h.tensor_tensor(
                            out=qT[:, g * 512:(g + 1) * 512], in0=rtmp[rb][:, 0, :], in1=rtmp[rb][:, 1, :],
                            op=ALU.add), reads=[R_rtmp[rb]], writes=[R_qT])
                    else:
                        sc.op("pool", lambda h, g=g, rb=rb: h.tensor_tensor(
                            out=kTe[0:64, g * 512:(g + 1) * 512], in0=rtmp[rb][0:64, 0, :],
                            in1=rtmp[rb][0:64, 1, :], op=ALU.add), reads=[R_rtmp[rb]], writes=[R_kT])
                        sc.op("pool", lambda h, g=g, rb=rb: h.tensor_tensor(
                            out=kTo[64:128, g * 512:(g + 1) * 512], in0=rtmp[rb][64:128, 0, :],
                            in1=rtmp[rb][64:128, 1, :], op=ALU.add), reads=[R_rtmp[rb]], writes=[R_kT])

            for g in range(4):
                pbk = g % 2
                for k in range(KC):
                    sc.op("pe", lambda h, k=k, pbk=pbk, g=g: h.matmul(
                        psum[pbk][:], lhsT=W[:, k, 512:640], rhs=hT[:, k, g * 512:(g + 1) * 512],
                        start=(k == 0), stop=(k == KC - 1)),
                        reads=[RW] + R_hT[g * 4:(g + 1) * 4], writes=[R_ps[pbk]])
                sc.op("act", lambda h, pbk=pbk, g=g: h.activation(out=vT[:, g * 512:(g + 1) * 512], in_=psum[pbk][:], func=AF.Copy),
                      reads=[R_ps[pbk]], writes=[R_vT])
            kTs = (kTe, kTo)

            def head_task(hh, bi, d, nb):
                kT_h = kTs[hh]
                sbank = (0, 1) if hh == 0 else (2, 3)
                obank = (4, 5) if hh == 0 else (6, 7)
                pTs = pT[hh * 3:hh * 3 + 3]
                RpTs = R_pT[hh * 3:hh * 3 + 3]
                steps = []
                oi = 0
                for r in range(d):
                    for ob in range(0, nb, 4):
                        nq = min(4, nb - ob)
                        for half in range(0, nq, 2):
                            steps.append((r, ob, nq, half, min(2, nq - half), oi, half + 2 >= nq))
                        oi += 1
                pending = None

                def emit_qk(i, st):
                    r, ob, nq, half, n2, oi, last = st
                    pss = sbank[i % 2]
                    pti = i % 3
                    for jj in range(n2):
                        bq = ob + half + jj
                        q0 = r + d * 128 * bq
                        qsl = qT[:, q0:q0 + d * 127 + 1:d]
                        if bq > 0:
                            k0 = r + d * 128 * (bq - 1)
                            sc.op("pe", lambda h, jj=jj, k0=k0, qsl=qsl, pss=pss: h.matmul(
                                psum[pss][:, jj * 256:jj * 256 + 128],
                                lhsT=kT_h[:, k0:k0 + d * 127 + 1:d], rhs=qsl, start=True, stop=True),
                                reads=[R_kT, R_qT], writes=[R_ps[pss]])
                        k0 = r + d * 128 * bq
                        sc.op("pe", lambda h, jj=jj, k0=k0, qsl=qsl, pss=pss: h.matmul(
                            psum[pss][:, jj * 256 + 128:jj * 256 + 256],
                            lhsT=kT_h[:, k0:k0 + d * 127 + 1:d], rhs=qsl, start=True, stop=True),
                            reads=[R_kT, R_qT], writes=[R_ps[pss]])
                    c_lo = 128 if (ob + half == 0) else 0
                    c_hi = n2 * 256
                    sc.op("act", lambda h, pss=pss, pti=pti, c_lo=c_lo, c_hi=c_hi: h.activation(
                        out=pTs[pti][:, c_lo:c_hi], in_=psum[pss][:, c_lo:c_hi], func=AF.Exp, scale=0.125),
                        reads=[R_ps[pss]], writes=[RpTs[pti]])
                    sc.op("dve", lambda h, pti=pti, c_lo=c_lo, c_hi=c_hi: h.tensor_tensor(
                        out=pTs[pti][:, c_lo:c_hi], in0=pTs[pti][:, c_lo:c_hi], in1=amask[:, c_lo:c_hi],
                        op=ALU.mult), reads=[RpTs[pti], R_tab], writes=[RpTs[pti]])

                def emit_pv(i, st):
                    r, ob, nq, half, n2, oi, last = st
                    pti = i % 3
                    pso = obank[oi % 2]
                    for jj in range(n2):
                        bq = ob + half + jj
                        oc = (half + jj) * 128
                        if bq > 0:
                            sc.op("pe", lambda h, jj=jj, bq=bq, oc=oc, pso=pso, pti=pti, r=r: h.matmul(
                                psum[pso][:, oc:oc + 128], lhsT=VX[:, r * nb + bq - 1, hh, :],
                                rhs=pTs[pti][:, jj * 256:jj * 256 + 128], start=True, stop=False),
                                reads=[R_vext, RpTs[pti]], writes=[R_ps[pso]])
                        sc.op("pe", lambda h, jj=jj, bq=bq, oc=oc, pso=pso, pti=pti, r=r: h.matmul(
                            psum[pso][:, oc:oc + 128], lhsT=VX[:, r * nb + bq, hh, :],
                            rhs=pTs[pti][:, jj * 256 + 128:jj * 256 + 256], start=(bq == 0), stop=True),
                            reads=[R_vext, RpTs[pti]], writes=[R_ps[pso]])
                    if last:
                        t0 = r + d * 128 * ob
                        n = nq * 128
                        dst = acc[:, hh, t0:t0 + d * (n - 1) + 1:d]
                        if bi == 0:
                            sc.op("act", lambda h, dst=dst, pso=pso, n=n: h.activation(
                                out=dst, in_=psum[pso][:, 0:n], func=AF.Copy),
                                reads=[R_ps[pso]], writes=[R_acc[hh]])
                        else:
                            sc.op("dve", lambda h, dst=dst, pso=pso, n=n: h.tensor_tensor(
                                out=dst, in0=psum[pso][:, 0:n], in1=dst, op=ALU.add),
                                reads=[R_ps[pso], R_acc[hh]], writes=[R_acc[hh]])

                for i, st in enumerate(steps):
                    emit_qk(i, st)
                    if pending is not None:
                        emit_pv(*pending)
                    pending = (i, st)
                    yield
                emit_pv(*pending)
                yield

            for bi, (d, nb) in enumerate(BR):
                for q4 in range(4):
                    pbk = q4 % 2
                    pvb = psum[pbk][:].bitcast(BF16)
                    for j in range(4):
                        blk = q4 * 4 + j
                        r, kb = blk // nb, blk % nb
                        t0 = r + d * 128 * kb
                        sc.op("pe", lambda h, j=j, t0=t0, pvb=pvb, d=d: h.transpose(
                            pvb[:, j * 128:(j + 1) * 128], vT[:, t0:t0 + d * 127 + 1:d], ident[:]),
                            reads=[R_vT, R_const], writes=[R_ps[pbk]])
                    sc.op("act", lambda h, q4=q4, pvb=pvb: h.activation(
                        out=VX[:, q4 * 4:(q4 + 1) * 4, :, 0:64],
                        in_=pvb[:, 0:512].rearrange("p (j h c) -> p j h c", j=4, h=2), func=AF.Copy),
                        reads=[R_ps[pbk]], writes=[R_vext])
                run_interleaved([head_task(0, bi, d, nb), head_task(1, bi, d, nb)])
            for hh in range(2):
                for g in range(4):
                    rb = g % 2
                    rlv = rtmp[rb][0:64, 0, :]
                    sc.op("act", lambda h, hh=hh, g=g, rlv=rlv: h.activation(
                        out=rlv, in_=acc[64:128, hh, g * 512:(g + 1) * 512], func=AF.Ln),
                        reads=[R_acc[hh]], writes=[R_rtmp[rb]])
                    sc.op("act", lambda h, rlv=rlv: h.activation(out=rlv, in_=rlv, func=AF.Exp, scale=-1.0),
                          reads=[R_rtmp[rb]], writes=[R_rtmp[rb]])
                    sc.op("pool", lambda h, hh=hh, g=g, rlv=rlv: h.tensor_tensor(
                        out=catA[64 * hh:64 * hh + 64, hp, g * 512:(g + 1) * 512],
                        in0=acc[0:64, hh, g * 512:(g + 1) * 512], in1=rlv, op=ALU.mult),
                        reads=[R_acc[hh], R_rtmp[rb]], writes=[R_catA[hp]])

        load_watt(0)
        for hp in range(4):
            if hp < 3:
                load_watt(hp + 1)
            attention_unit(hp)
        end_phase()
        e.close()


    def deltanet_phase(l, catD, R_catD):
        hT = hT_box[0]
        e = ExitStack()
        C = sb(e, "dn_c", [128, 8, 128], F32)
        M1, M2, maskS, maskTu, Mblk, Mc0, Mc1, identf = [C[:, i, :] for i in range(8)]
        R_c = Res("dnc")
        phase_load(C[:], dn_c_d, R_c)
        cw = sb(e, "dn_cw", [128, 4, 3, 4], F32)
        hp = sb(e, "dn_hp", [128, 2, 4], F32)
        wba = sb(e, "w_ba", [128, KC, 8], BF16)
        R_c2 = Res("dnc2")
        phase_load(cw[:], dn_cw_d[:, l], R_c2)
        phase_load(hp[:], dn_hp_d[:, l], R_c2)
        R_wba = Res("wba")
        phase_load(wba[:], w_ba_d[l], R_wba, eng="pool")
        onesb = sb(e, "onesb", [128, 1], BF16)
        sc.op("pool", lambda h: h.memset(onesb[:], 1.0), writes=[R_c2])

        Wd = sb(e, "w_dn", [128, KC, 512], BF16)
        R_Wd = Res("wdn")
        q_wdn = sc.dma_sem("wdn_%d" % l)

        def load_wdn(hd):
            sc.dma("pool", q_wdn, lambda h: h.dma_start(out=Wd[:], in_=w_dn_d[l, hd]), writes=[R_Wd])

        SC = sb(e, "dn_sc", [128, 8, NT, 4], F32)
        beta, xa, graw, egc, ekd, tmpa = [SC[:, i] for i in range(6)]
        egl = SC[:, 6:8]
        R_sc = Res("dnsc")
        expA = sb(e, "expA", [128, 4], F32)
        pb = psum[0]
        for t in range(NT):
            for k in range(KC):
                sc.op("pe", lambda h, t=t, k=k: h.matmul(pb[:, t * 8:(t + 1) * 8], lhsT=hT[:, k, t * 128:(t + 1) * 128],
                                                         rhs=wba[:, k, :], start=(k == 0), stop=(k == KC - 1)),
                      reads=[R_hT[t], R_wba], writes=[R_ps[0]])
        bav = pb[:, 0:128].rearrange("p (t c) -> p t c", c=8)
        sc.op("act", lambda h: h.activation(out=beta, in_=bav[:, :, 0:4], func=AF.Sigmoid),
              reads=[R_ps[0]], writes=[R_sc])
        sc.op("dve", lambda h: h.tensor_tensor(out=xa, in0=bav[:, :, 4:8],
                                               in1=hp[:, 1, :].unsqueeze(1).to_broadcast([128, NT, 4]), op=ALU.add),
              reads=[R_ps[0], R_c2], writes=[R_sc])
        sc.op("act", lambda h: h.activation(out=xa, in_=xa, func=AF.Exp), reads=[R_sc], writes=[R_sc])
        sc.op("act", lambda h: h.activation(out=xa, in_=xa, func=AF.Ln, bias=1.0), reads=[R_sc], writes=[R_sc])
        sc.op("act", lambda h: h.activation(out=expA[:], in_=hp[:, 0, :], func=AF.Exp), reads=[R_c2], writes=[R_sc])
        sc.op("dve", lambda h: h.scalar_tensor_tensor(out=graw, in0=xa, scalar=-1.0,
                                                      in1=expA[:].unsqueeze(1).to_broadcast([128, NT, 4]),
                                                      op0=ALU.mult, op1=ALU.mult), reads=[R_sc], writes=[R_sc])
        grf = graw.rearrange("p t c -> p (t c)")
        pg = psum[1]
        for i, Mx in enumerate((M1, Mblk, Mc0, Mc1)):
            sc.op("pe", lambda h, i=i, Mx=Mx: h.matmul(pg[:, i * 64:(i + 1) * 64], lhsT=Mx, rhs=grf, start=True, stop=True),
                  reads=[R_c, R_sc], writes=[R_ps[1]])
        pgv = pg[:, 0:256].rearrange("p (i t c) -> p i t c", i=4, c=4)
        sc.op("act", lambda h: h.activation(out=egc, in_=pgv[:, 0], func=AF.Exp), reads=[R_ps[1]], writes=[R_sc])
        sc.op("dve", lambda h: h.tensor_copy(out=tmpa, in_=pgv[:, 0]), reads=[R_ps[1]], writes=[R_sc])
        sc.op("dve", lambda h: h.tensor_tensor(out=ekd, in0=pgv[:, 1], in1=tmpa, op=ALU.subtract),
              reads=[R_ps[1], R_sc], writes=[R_sc])
        sc.op("act", lambda h: h.activation(out=ekd, in_=ekd, func=AF.Exp), reads=[R_sc], writes=[R_sc])
        sc.op("act", lambda h: h.activation(out=SC[:, 6:8].rearrange("p a t c -> p (a t c)"), in_=pg[:, 128:256], func=AF.Exp),
              reads=[R_ps[1]], writes=[R_sc])

        xpad = [sb(e, "xpad%d" % i, [128, 3 + 512], BF16) for i in range(2)]
        R_xpad = [Res("xpad%d" % i) for i in range(2)]
        dg = sb(e, "dgw", [128, 12, 128], BF16)
        R_dg = Res("dg")
        csT = [sb(e, "csT%d" % i, [128, S], BF16) for i in range(3)]
        R_csT = [Res("csT%d" % i) for i in range(3)]
        sqt = sb(e, "sq", [128, S], BF16)
        sq = sqt[:]
        R_sqall = [Res("sq")]
        zw2 = [sb(e, "zw%d" % i, [128, NT, 128], BF16) for i in range(2)]
        R_zw2 = [Res("zw%d" % i) for i in range(2)]
        HS = sb(e, "dn_hs", [128, 6, NT], F32)
        R_hs = Res("hs")
        NH = 2
        GMh, R_GMh, ABh, R_ABh = [], [], [], []
        for hf_ in range(2):
            gmb = [sb(e, "gm%d_%d" % (hf_, i), [128, NH, 128], F32) for i in range(6)]
            rgm = [Res("gm%d_%d" % (hf_, i)) for i in range(6)]
            GMh.append([gmb[0], gmb[1], gmb[2], gmb[3], gmb[0], gmb[4], gmb[1], gmb[5], gmb[2]])
            R_GMh.append([rgm[0], rgm[1], rgm[2], rgm[3], rgm[0], rgm[4], rgm[1], rgm[5], rgm[2]])
            ABh.append([sb(e, "ab%d_%d" % (hf_, i), [128, NH, 128], BF16) for i in range(4)])
            R_ABh.append([Res("ab%d_%d" % (hf_, i)) for i in range(4)])
        HB = [[[sb(e, "hb%d_%d_%d" % (p, hf_, i), [128, NH, 128], BF16) for i in range(7)] for hf_ in range(2)]
              for p in range(2)]
        R_HB = [[[Res("hb%d_%d_%d" % (p, hf_, i)) for i in range(7)] for hf_ in range(2)] for p in range(2)]
        Sf = sb(e, "Sf", [128, 128], F32)
        Sb = sb(e, "Sb", [128, 128], BF16)
        R_S = Res("S")
        vnew = sb(e, "vnew", [128, 128], BF16)
        R_vnew = Res("vnew")
        p2s = [sb(e, "p2s%d" % i, [128, 128], F32) for i in range(2)]
        R_p2s = [Res("p2s%d" % i) for i in range(2)]
        osb = sb(e, "osb", [128, 4, 128], F32)
        R_osb = Res("osb")
        otk = sb(e, "otk", [128, 4, 128], BF16)
        R_otk = Res("otk")
        oss = sb(e, "oss", [128, 8], F32)
        R_oss = Res("oss")
        junk = sb(e, "junkd", [128, 128], BF16)
        sc.op("pool", lambda h: h.memset(oss[:], 0.0), writes=[R_oss])
        for p in range(2):
            for hf_ in range(2):
                sc.op("pool", lambda h, p=p, hf_=hf_: h.memset(HB[p][hf_][4][:], 0.0), writes=[R_HB[p][hf_][4]])
                sc.op("pool", lambda h, p=p, hf_=hf_: h.memset(HB[p][hf_][5][:], 0.0), writes=[R_HB[p][hf_][5]])

        load_wdn(0)

        def stage_C(hd):
            zw = zw2[hd % 2]
            R_zw = R_zw2[hd % 2]
            for cc in range(3):
                for j in range(4):
                    sc.op("dve", lambda h, cc=cc, j=j: h.tensor_scalar(
                        out=dg[:, cc * 4 + j, :], in0=ident[:], scalar1=cw[:, hd, cc, j:j + 1], scalar2=None, op0=ALU.mult),
                        reads=[R_const, R_c2], writes=[R_dg])
            yield
            for cc in range(3):
                for g in range(4):
                    pj = g % 2
                    pc = 2 + (g % 2)
                    xb = xpad[(cc * 4 + g) % 2]
                    Rxb = R_xpad[(cc * 4 + g) % 2]
                    xprev = xpad[(cc * 4 + g + 1) % 2]
                    Rxprev = R_xpad[(cc * 4 + g + 1) % 2]
                    for k in range(KC):
                        sc.op("pe", lambda h, k=k, cc=cc, g=g, pj=pj: h.matmul(
                            psum[pj][:], lhsT=Wd[:, k, cc * 128:(cc + 1) * 128], rhs=hT[:, k, g * 512:(g + 1) * 512],
                            start=(k == 0), stop=(k == KC - 1)), reads=[R_Wd] + R_hT[g * 4:(g + 1) * 4], writes=[R_ps[pj]])
                    sc.op("act", lambda h, pj=pj, xb=xb: h.activation(out=xb[:, 3:515], in_=psum[pj][:], func=AF.Copy),
                          reads=[R_ps[pj]], writes=[Rxb])
                    if g == 0:
                        sc.op("pool", lambda h, xb=xb: h.memset(xb[:, 0:3], 0.0), writes=[Rxb])
                    else:
                        sc.op("pool", lambda h, xb=xb, xprev=xprev: h.tensor_copy(out=xb[:, 0:3], in_=xprev[:, 512:515]),
                              reads=[Rxprev], writes=[Rxb])
                    for j in range(4):
                        sc.op("pe", lambda h, j=j, cc=cc, pc=pc, xb=xb: h.matmul(
                            psum[pc][:], lhsT=dg[:, cc * 4 + j, :], rhs=xb[:, j:j + 512], start=(j == 0), stop=(j == 3)),
                            reads=[R_dg, Rxb], writes=[R_ps[pc]])
                    sc.op("act", lambda h, pc=pc, cc=cc, g=g: h.activation(
                        out=csT[cc][:, g * 512:(g + 1) * 512], in_=psum[pc][:], func=AF.Silu),
                        reads=[R_ps[pc]], writes=[R_csT[cc]])
                    yield
            for t4 in range(4):
                pz = 2 + (t4 % 2)
                for j in range(4):
                    t = t4 * 4 + j
                    for k in range(KC):
                        sc.op("pe", lambda h, t=t, j=j, k=k, pz=pz: h.matmul(
                            psum[pz][:, j * 128:(j + 1) * 128], lhsT=hT[:, k, t * 128:(t + 1) * 128],
                            rhs=Wd[:, k, 384:512], start=(k == 0), stop=(k == KC - 1)),
                            reads=[R_Wd, R_hT[t]], writes=[R_ps[pz]])
                sc.op("act", lambda h, t4=t4, pz=pz: h.activation(
                    out=zw[:, t4 * 4:(t4 + 1) * 4, :], in_=psum[pz][:].rearrange("p (j c) -> p j c", j=4), func=AF.Silu),
                    reads=[R_ps[pz]], writes=[R_zw])
                yield
            if hd < 3:
                load_wdn(hd + 1)
            pss = psum[4]
            for qi in range(2):
                if qi == 0:
                    sc.op("act", lambda h, qi=qi: h.activation(out=sq, in_=csT[qi][:], func=AF.Square),
                          reads=[R_csT[qi]], writes=R_sqall)
                else:
                    sc.op("dve", lambda h, qi=qi: h.tensor_tensor(out=sq, in0=csT[qi][:], in1=csT[qi][:], op=ALU.mult),
                          reads=[R_csT[qi]], writes=R_sqall)
                for t in range(NT):
                    sc.op("pe", lambda h, t=t, qi=qi: h.matmul(
                        pss[:, qi * 16 + t:qi * 16 + t + 1], lhsT=sq[:, t * 128:(t + 1) * 128], rhs=onesb[:],
                        start=True, stop=True), reads=R_sqall + [R_c2], writes=[R_ps[4]])
                yield
            rqk = HS[:, 0:2, :]
            sc.op("dve", lambda h: h.tensor_scalar(out=rqk, in0=pss[:, 0:32].rearrange("p (a t) -> p a t", a=2),
                                                   scalar1=EPS, scalar2=None, op0=ALU.add),
                  reads=[R_ps[4]], writes=[R_hs])
            sc.op("act", lambda h: h.activation(out=rqk, in_=rqk, func=AF.Sqrt), reads=[R_hs], writes=[R_hs])
            sc.op("dve", lambda h: h.reciprocal(out=rqk, in_=rqk), reads=[R_hs], writes=[R_hs])
            sc.op("dve", lambda h: h.tensor_scalar(out=HS[:, 0, :], in0=HS[:, 0, :], scalar1=float(128 ** -0.5),
                                                   scalar2=None, op0=ALU.mult), reads=[R_hs], writes=[R_hs])
            sc.op("dve", lambda h, hd=hd: h.tensor_tensor(out=HS[:, 4, :], in0=HS[:, 1, :], in1=beta[:, :, hd], op=ALU.mult),
                  reads=[R_hs, R_sc], writes=[R_hs])
            sc.op("dve", lambda h, hd=hd: h.tensor_tensor(out=HS[:, 2, :], in0=HS[:, 4, :], in1=egc[:, :, hd], op=ALU.mult),
                  reads=[R_hs, R_sc], writes=[R_hs])
            sc.op("dve", lambda h, hd=hd: h.tensor_tensor(out=HS[:, 3, :], in0=HS[:, 1, :], in1=ekd[:, :, hd], op=ALU.mult),
                  reads=[R_hs, R_sc], writes=[R_hs])
            sc.op("dve", lambda h, hd=hd: h.tensor_scalar(out=HS[:, 5, :], in0=beta[:, :, hd], scalar1=-1.0, scalar2=None,
                                                          op0=ALU.mult), reads=[R_sc], writes=[R_hs])
            yield

        a_rr = [0, 0]

        def stage_A(G, hd, half):
            par = G % 2
            M2g, Dm, DTm, Pa, Pb, Ra, Rb, Ya, Yb = GMh[half]
            R_M2g, R_Dm, R_DTm, R_Pa, R_Pb, R_Ra, R_Rb, R_Ya, R_Yb = R_GMh[half]
            KT, kbg, knt, qnt = ABh[half]
            R_KT, R_kbg, R_knt, R_qnt = R_ABh[half]
            TT, WnT, qkT, QT, kd0, kd1, vbt = HB[par][half]
            R_TT, R_WnT, R_qkT, R_QT, R_kd0, R_kd1, R_vbt = R_HB[par][half]
            T0 = G * 4 + half * NH
            NC2 = NH * 128

            def abank():
                b = half * 3 + a_rr[half] % 3
                a_rr[half] += 1
                return b
            for j in range(NH):
                t = T0 + j
                bk = abank()
                ptv = psum[bk][:].bitcast(BF16)
                Rp = R_ps[bk]
                for ci in range(3):
                    sc.op("pe", lambda h, ci=ci, t=t, ptv=ptv: h.transpose(
                        ptv[:, ci * 128:(ci + 1) * 128], csT[ci][:, t * 128:(t + 1) * 128], ident[:]),
                        reads=[R_csT[ci], R_const], writes=[Rp])
                qv, kv, vv = ptv[:, 0:128], ptv[:, 128:256], ptv[:, 256:384]
                sc.op("act", lambda h, j=j, t=t, qv=qv: h.activation(out=qnt[:, j, :], in_=qv, func=AF.Copy,
                                                                     scale=HS[:, 0, t:t + 1]),
                      reads=[Rp, R_hs], writes=[R_qnt])
                sc.op("dve", lambda h, j=j, t=t, kv=kv: h.tensor_scalar(out=knt[:, j, :], in0=kv, scalar1=HS[:, 1, t:t + 1],
                                                                        scalar2=None, op0=ALU.mult),
                      reads=[Rp, R_hs], writes=[R_knt])
                sc.op("act", lambda h, j=j, t=t, kv=kv: h.activation(out=kbg[:, j, :], in_=kv, func=AF.Copy,
                                                                     scale=HS[:, 2, t:t + 1]),
                      reads=[Rp, R_hs], writes=[R_kbg])
                sc.op("dve", lambda h, j=j, t=t, kv=kv: h.tensor_scalar(out=kd0[0:64, j, :], in0=kv[0:64, :],
                                                                        scalar1=HS[0:64, 3, t:t + 1], scalar2=None,
                                                                        op0=ALU.mult),
                      reads=[Rp, R_hs], writes=[R_kd0])
                sc.op("dve", lambda h, j=j, t=t, kv=kv: h.tensor_scalar(out=kd1[64:128, j, :], in0=kv[64:128, :],
                                                                        scalar1=HS[64:128, 3, t:t + 1], scalar2=None,
                                                                        op0=ALU.mult),
                      reads=[Rp, R_hs], writes=[R_kd1])
                sc.op("act", lambda h, j=j, t=t, vv=vv: h.activation(out=vbt[:, j, :], in_=vv, func=AF.Copy,
                                                                     scale=beta[:, t, hd:hd + 1]),
                      reads=[Rp, R_sc], writes=[R_vbt])
                yield
            bk = abank()
            pkq = psum[bk][:].bitcast(BF16)
            for j in range(NH):
                sc.op("pe", lambda h, j=j: h.transpose(pkq[:, j * 128:(j + 1) * 128], knt[:, j, :], ident[:]),
                      reads=[R_knt, R_const], writes=[R_ps[bk]])
                sc.op("pe", lambda h, j=j: h.transpose(pkq[:, NC2 + j * 128:NC2 + (j + 1) * 128], qnt[:, j, :], ident[:]),
                      reads=[R_qnt, R_const], writes=[R_ps[bk]])
            sc.op("act", lambda h: h.activation(out=KT[:].rearrange("p j c -> p (j c)"), in_=pkq[:, 0:NC2], func=AF.Copy),
                  reads=[R_ps[bk]], writes=[R_KT])
            sc.op("dve", lambda h: h.tensor_copy(out=QT[:].rearrange("p j c -> p (j c)"), in_=pkq[:, NC2:2 * NC2]),
                  reads=[R_ps[bk]], writes=[R_QT])
            yield
            sc.op("dve", lambda h: h.tensor_tensor(
                out=M2g[:], in0=M2.unsqueeze(1).to_broadcast([128, NH, 128]),
                in1=graw[:, T0:T0 + NH, hd:hd + 1].to_broadcast([128, NH, 128]), op=ALU.mult),
                reads=[R_c, R_sc], writes=[R_M2g])
            bg, bt = abank(), abank()
            sc.op("pe", lambda h: h.matmul(psum[bg][:, 0:NC2], lhsT=M1, rhs=M2g[:].rearrange("p j c -> p (j c)"),
                                           start=True, stop=False), reads=[R_c, R_M2g], writes=[R_ps[bg]])
            for j in range(NH):
                sc.op("pe", lambda h, j=j: h.matmul(psum[bg][:, j * 128:(j + 1) * 128], lhsT=identf, rhs=maskS,
                                                    start=False, stop=(j == NH - 1)), reads=[R_c], writes=[R_ps[bg]])
            for j in range(NH):
                sc.op("pe", lambda h, j=j: h.matmul(psum[bt][:, j * 128:(j + 1) * 128], lhsT=M2g[:, j, :], rhs=M1,
                                                    start=True, stop=False), reads=[R_c, R_M2g], writes=[R_ps[bt]])
                sc.op("pe", lambda h, j=j: h.matmul(psum[bt][:, j * 128:(j + 1) * 128], lhsT=identf, rhs=maskTu,
                                                    start=False, stop=True), reads=[R_c], writes=[R_ps[bt]])
            sc.op("act", lambda h: h.activation(out=Dm[:].rearrange("p j c -> p (j c)"), in_=psum[bg][:, 0:NC2], func=AF.Exp),
                  reads=[R_ps[bg]], writes=[R_Dm])
            sc.op("act", lambda h: h.activation(out=DTm[:].rearrange("p j c -> p (j c)"), in_=psum[bt][:, 0:NC2], func=AF.Exp),
                  reads=[R_ps[bt]], writes=[R_DTm])
            yield
            bs, bq = abank(), abank()
            for j in range(NH):
                sc.op("pe", lambda h, j=j: h.matmul(psum[bs][:, j * 128:(j + 1) * 128], lhsT=KT[:, j, :], rhs=KT[:, j, :],
                                                    start=True, stop=True), reads=[R_KT], writes=[R_ps[bs]])
            for j in range(NH):
                sc.op("pe", lambda h, j=j: h.matmul(psum[bq][:, j * 128:(j + 1) * 128], lhsT=KT[:, j, :], rhs=QT[:, j, :],
                                                    start=True, stop=True), reads=[R_KT, R_QT], writes=[R_ps[bq]])
            for j in range(NH):
                sc.op("dve", lambda h, j=j: h.scalar_tensor_tensor(
                    out=Pa[:, j, :], in0=psum[bs][:, j * 128:(j + 1) * 128], scalar=HS[:, 5, T0 + j:T0 + j + 1],
                    in1=Dm[:, j, :], op0=ALU.mult, op1=ALU.mult), reads=[R_ps[bs], R_Dm, R_hs], writes=[R_Pa])
            sc.op("dve", lambda h: h.tensor_tensor(out=qkT[:].rearrange("p j c -> p (j c)"), in0=psum[bq][:, 0:NC2],
                                                   in1=DTm[:].rearrange("p j c -> p (j c)"), op=ALU.mult),
                  reads=[R_ps[bq], R_DTm], writes=[R_qkT])
            yield
            br = abank()
            for j in range(NH):
                sc.op("pe", lambda h, j=j: h.transpose(psum[br][:, j * 128:(j + 1) * 128], Pa[:, j, :], identf),
                      reads=[R_Pa, R_c], writes=[R_ps[br]])
            sc.op("act", lambda h: h.activation(out=Ra[:].rearrange("p j c -> p (j c)"), in_=psum[br][:, 0:NC2], func=AF.Copy),
                  reads=[R_ps[br]], writes=[R_Ra])
            sc.op("dve", lambda h: h.tensor_tensor(out=Ya[:], in0=psum[br][:, 0:NC2].rearrange("p (j c) -> p j c", j=NH),
                                                   in1=identf.unsqueeze(1).to_broadcast([128, NH, 128]),
                                                   op=ALU.add), reads=[R_ps[br], R_c], writes=[R_Ya])
            yield
            Pc, Pn, Rc, Rn, Yc, Yn = Pa, Pb, Ra, Rb, Ya, Yb
            RPc, RPn, RRc, RRn, RYc, RYn = R_Pa, R_Pb, R_Ra, R_Rb, R_Ya, R_Yb
            NL = 5
            for lev in range(1, NL + 1):
                bp = abank()
                for j in range(NH):
                    sc.op("pe", lambda h, j=j, Rc=Rc, Pc=Pc, bp=bp: h.matmul(psum[bp][:, j * 128:(j + 1) * 128], lhsT=Rc[:, j, :],
                                                                             rhs=Pc[:, j, :], start=True, stop=True),
                          reads=[RRc, RPc], writes=[R_ps[bp]])
                if lev < NL:
                    brr = abank()
                    for j in range(NH):
                        sc.op("pe", lambda h, j=j, Rc=Rc, Pc=Pc, brr=brr: h.matmul(psum[brr][:, j * 128:(j + 1) * 128], lhsT=Pc[:, j, :],
                                                                                   rhs=Rc[:, j, :], start=True, stop=True),
                              reads=[RRc, RPc], writes=[R_ps[brr]])
                sc.op("act", lambda h, Pn=Pn, bp=bp: h.activation(out=Pn[:].rearrange("p j c -> p (j c)"), in_=psum[bp][:, 0:NC2], func=AF.Copy),
                      reads=[R_ps[bp]], writes=[RPn])
                if lev < NL:
                    sc.op("act", lambda h, Rn=Rn, brr=brr: h.activation(out=Rn[:].rearrange("p j c -> p (j c)"), in_=psum[brr][:, 0:NC2],
                                                                        func=AF.Copy), reads=[R_ps[brr]], writes=[RRn])
                yield
                by = abank()
                for j in range(NH):
                    sc.op("pe", lambda h, j=j, Pn=Pn, Yc=Yc, by=by: h.matmul(psum[by][:, j * 128:(j + 1) * 128], lhsT=Pn[:, j, :],
                                                                             rhs=Yc[:, j, :], start=True, stop=True),
                          reads=[RPn, RYc], writes=[R_ps[by]])
                if lev < NL:
                    sc.op("dve", lambda h, Yn=Yn, Yc=Yc, by=by: h.tensor_tensor(out=Yn[:].rearrange("p j c -> p (j c)"), in0=psum[by][:, 0:NC2],
                                                                                in1=Yc[:].rearrange("p j c -> p (j c)"), op=ALU.add),
                          reads=[R_ps[by], RYc], writes=[RYn])
                else:
                    sc.op("dve", lambda h, Yc=Yc, by=by: h.tensor_tensor(out=TT[:].rearrange("p j c -> p (j c)"), in0=psum[by][:, 0:NC2],
                                                                         in1=Yc[:].rearrange("p j c -> p (j c)"), op=ALU.add),
                          reads=[R_ps[by], RYc], writes=[R_TT])
                Pc, Pn, Rc, Rn, Yc, Yn = Pn, Pc, Rn, Rc, Yn, Yc
                RPc, RPn, RRc, RRn, RYc, RYn = RPn, RPc, RRn, RRc, RYn, RYc
                yield
            bw = abank()
            for j in range(NH):
                sc.op("pe", lambda h, j=j: h.matmul(psum[bw][:, j * 128:(j + 1) * 128], lhsT=kbg[:, j, :], rhs=TT[:, j, :],
                                                    start=True, stop=True), reads=[R_kbg, R_TT], writes=[R_ps[bw]])
            sc.op("act", lambda h: h.activation(out=WnT[:].rearrange("p j c -> p (j c)"), in_=psum[bw][:, 0:NC2], func=AF.Copy,
                                                scale=-1.0), reads=[R_ps[bw]], writes=[R_WnT])
            yield

        def stage_B(q, hd):
            G, bh = q // 2, q % 2
            zw = zw2[hd % 2]
            R_zw = R_zw2[hd % 2]
            if q == 0:
                sc.op("pool", lambda h: h.memset(Sf[:], 0.0), writes=[R_S])
                sc.op("pool", lambda h: h.memset(Sb[:], 0.0), writes=[R_S])
            par = G % 2
            T0 = G * 4
            pA, pB = psum[7], psum[6]
            pC = psum[7][:, 256:384]
            RA, RB, RC = R_ps[7], R_ps[6], R_ps[7]
            def tile_steps(j4):
                t = T0 + j4
                half, j = j4 // NH, j4 % NH
                TT, WnT, qkT, QT, kd0, kd1, vbt = HB[par][half]
                R_TT, R_WnT, R_qkT, R_QT, R_kd0, R_kd1, R_vbt = R_HB[par][half]
                for c in range(2):
                    kd = kd0 if c == 0 else kd1
                    Rkd = R_kd0 if c == 0 else R_kd1
                    sc.op("pe", lambda h, j=j, c=c: h.matmul(pA[:, c * 128:(c + 1) * 128], lhsT=TT[:, j, :], rhs=vbt[:, j, :],
                                                             start=True, stop=False),
                          reads=[R_TT, R_vbt], writes=[RA])
                    sc.op("pe", lambda h, j=j, c=c: h.matmul(pA[:, c * 128:(c + 1) * 128], lhsT=WnT[:, j, :], rhs=Sb[:],
                                                             start=False, stop=True),
                          reads=[R_WnT, R_S], writes=[RA])
                    if c == 0:
                        sc.op("dve", lambda h: h.tensor_copy(out=vnew[:], in_=pA[:, 0:128]),
                              reads=[RA], writes=[R_vnew])
                    else:
                        sc.op("dve", lambda h: h.tensor_copy(out=vnew[64:128, :], in_=pA[64:128, 128:256]),
                              reads=[RA], writes=[R_vnew])
                    yield
                    sc.op("pe", lambda h, j=j, c=c: h.matmul(pB[:, c * 128:(c + 1) * 128], lhsT=QT[:, j, :], rhs=Sb[:],
                                                             start=True, stop=True),
                          reads=[R_QT, R_S], writes=[RB])
                    sc.op("pe", lambda h, j=j, kd=kd: h.matmul(pC, lhsT=kd[:, j, :], rhs=vnew[:],
                                                               start=True, stop=True),
                          reads=[Rkd, R_vnew], writes=[RC])
                    if c == 1:
                        sc.op("pe", lambda h, j=j: h.matmul(pB[:, 256:384], lhsT=qkT[:, j, :], rhs=vnew[:],
                                                            start=True, stop=True),
                              reads=[R_qkT, R_vnew], writes=[RB])
                    sc.op("dve", lambda h, c=c, t=t: h.scalar_tensor_tensor(
                        out=Sb[:], in0=Sf[:], scalar=egl[:, c, t, hd:hd + 1], in1=pC, op0=ALU.mult, op1=ALU.add),
                        reads=[R_S, RC, R_sc], writes=[R_S])
                    sc.op("dve", lambda h, c=c, t=t: h.scalar_tensor_tensor(
                        out=Sf[:], in0=Sf[:], scalar=egl[:, c, t, hd:hd + 1], in1=pC, op0=ALU.mult, op1=ALU.add),
                        reads=[R_S, RC, R_sc], writes=[R_S])
                    yield
                p2 = p2s[j4 % 2]
                Rp2 = R_p2s[j4 % 2]
                sc.op("act", lambda h, p2=p2: h.activation(out=p2[:], in_=pB[:, 256:384], func=AF.Copy),
                      reads=[RB], writes=[Rp2])
                for c in range(2):
                    rs = slice(64 * c, 64 * c + 64)
                    sc.op("dve", lambda h, c=c, rs=rs, j4=j4, t=t, p2=p2: h.scalar_tensor_tensor(
                        out=osb[rs, j4, :], in0=pB[rs, c * 128:(c + 1) * 128], scalar=egc[rs, t, hd:hd + 1],
                        in1=p2[rs, :], op0=ALU.mult, op1=ALU.add), reads=[RB, Rp2, R_sc], writes=[R_osb])
                sc.op("act", lambda h, j4=j4: h.activation(out=junk[:], in_=osb[:, j4, :], func=AF.Square,
                                                           accum_out=oss[:, j4:j4 + 1]), reads=[R_osb], writes=[R_oss])
                yield
            for j4 in range(bh * NH, bh * NH + NH):
                yield from tile_steps(j4)
            c0 = bh * NH
            sc.op("dve", lambda h: h.tensor_scalar(out=oss[:, 4 + c0:4 + c0 + NH], in0=oss[:, c0:c0 + NH], scalar1=1.0 / 128,
                                                   scalar2=EPS, op0=ALU.mult, op1=ALU.add), reads=[R_oss], writes=[R_oss])
            sc.op("act", lambda h: h.activation(out=oss[:, 4 + c0:4 + c0 + NH], in_=oss[:, 4 + c0:4 + c0 + NH], func=AF.Ln),
                  reads=[R_oss], writes=[R_oss])
            sc.op("act", lambda h: h.activation(out=oss[:, 4 + c0:4 + c0 + NH], in_=oss[:, 4 + c0:4 + c0 + NH], func=AF.Exp,
                                                scale=-0.5), reads=[R_oss], writes=[R_oss])
            pO = pB[:].bitcast(BF16)
            for j in range(NH):
                j4 = c0 + j
                t = T0 + j4
                sc.op("dve", lambda h, j4=j4, t=t: h.scalar_tensor_tensor(
                    out=otk[:, j4, :], in0=osb[:, j4, :], scalar=oss[:, 4 + j4:5 + j4], in1=zw[:, t, :],
                    op0=ALU.mult, op1=ALU.mult), reads=[R_osb, R_oss, R_zw], writes=[R_otk])
                sc.op("pe", lambda h, j=j, j4=j4: h.transpose(pO[:, j * 128:(j + 1) * 128], otk[:, j4, :], ident[:]),
                      reads=[R_otk, R_const], writes=[RB])
            sc.op("act", lambda h: h.activation(out=catD[:, hd, (T0 + c0) * 128:(T0 + c0 + NH) * 128], in_=pO[:, 0:NH * 128],
                                                func=AF.Copy), reads=[RB], writes=[R_catD[hd]])
            sc.op("dve", lambda h: h.memset(oss[:, c0:c0 + NH], 0.0), reads=[R_oss], writes=[R_oss])
            yield

        run_interleaved([stage_C(0)])
        A_OFF, A_LEN, B_LEN = 9, 17, 6
        for hd in range(4):
            items = []
            for q in range(8):
                items.append((A_OFF * q, 1, stage_A(q // 2, hd, q % 2)))
                items.append((A_OFF * q + A_LEN + 1, 2, stage_B(q, hd)))
            if hd < 3:
                items.append((A_OFF * 7 + A_LEN + 1, 1, stage_C(hd + 1)))
            run_pipelined(items)
        end_phase()
        e.close()


    def postnorm_residual(t, pa, pb, wpost, R_wpost, tmpb, R_tmpb, ssb, R_ssb, junkp):
        sc.op("act", lambda h: h.activation(out=junkp[:], in_=psum[pa][:], func=AF.Square, accum_out=ssb[:, 0:1]),
              reads=[R_ps[pa]], writes=[R_ssb])
        sc.op("act", lambda h: h.activation(out=junkp[:], in_=psum[pb][:], func=AF.Square, accum_out=ssb[:, 1:2]),
              reads=[R_ps[pb]], writes=[R_ssb])
        sc.op("dve", lambda h: h.tensor_tensor(out=ssb[:, 2:3], in0=ssb[:, 0:1], in1=ssb[:, 1:2], op=ALU.add),
              reads=[R_ssb], writes=[R_ssb])
        sc.op("dve", lambda h: h.tensor_scalar(out=ssb[:, 2:3], in0=ssb[:, 2:3], scalar1=1.0 / D, scalar2=EPS,
                                               op0=ALU.mult, op1=ALU.add), reads=[R_ssb], writes=[R_ssb])
        sc.op("act", lambda h: h.activation(out=ssb[:, 2:3], in_=ssb[:, 2:3], func=AF.Sqrt), reads=[R_ssb], writes=[R_ssb])
        sc.op("dve", lambda h: h.reciprocal(out=ssb[:, 3:4], in_=ssb[:, 2:3]), reads=[R_ssb], writes=[R_ssb])
        sc.op("dve", lambda h: h.tensor_tensor(out=tmpb[:, 0:512], in0=psum[pa][:], in1=wpost[:, 0:512], op=ALU.mult),
              reads=[R_ps[pa], R_wpost], writes=[R_tmpb])
        sc.op("dve", lambda h: h.tensor_tensor(out=tmpb[:, 512:1024], in0=psum[pb][:], in1=wpost[:, 512:1024], op=ALU.mult),
              reads=[R_ps[pb], R_wpost], writes=[R_tmpb])
        sc.op("dve", lambda h: h.scalar_tensor_tensor(out=x_sb[:, t, :], in0=tmpb[:], scalar=ssb[:, 3:4], in1=x_sb[:, t, :],
                                                      op0=ALU.mult, op1=ALU.add),
              reads=[R_tmpb, R_ssb, R_x[t]], writes=[R_x[t]])
        sc.op("dve", lambda h: h.memset(ssb[:, 0:2], 0.0), reads=[R_ssb], writes=[R_ssb])

    def outproj_phase(l, catA, R_catA, catD, R_catD):
        e = ExitStack()
        Wo = sb(e, "w_o", [128, KC, D], BF16)
        R_Wo = Res("wo")
        phase_load(Wo[:], w_o_d[l], R_Wo, eng="pool")
        wpost = sb(e, "wpost", [128, D], F32)
        R_wpost = Res("wpost")
        phase_load(wpost[:], npost_d[:, l, 0, :], R_wpost)
        nwc = sb(e, "nwc", [128, 1], F32)
        R_nwc = Res("nwc")
        phase_load(nwc[:], dn_nwT_d[l], R_nwc)
        sc.op("dve", lambda h: h.tensor_scalar(out=Wo[:, 4:8, :], in0=Wo[:, 4:8, :], scalar1=nwc[:, 0:1], scalar2=None,
                                               op0=ALU.mult), reads=[R_Wo, R_nwc], writes=[R_Wo])
        tmpb = [sb(e, "tmpb%d" % i, [128, D], F32) for i in range(2)]
        R_tmpb = [Res("tmpb%d" % i) for i in range(2)]
        ssb = [sb(e, "ssb%d" % i, [128, 4], F32) for i in range(2)]
        R_ssb = [Res("ssb%d" % i) for i in range(2)]
        junkp = sb(e, "junkp", [128, 512], BF16)
        for i in range(2):
            sc.op("pool", lambda h, i=i: h.memset(ssb[i][:], 0.0), writes=[R_ssb[i]])
        for t in range(NT):
            pa, pb = (0, 1) if t % 2 == 0 else (2, 3)
            for n, pk in ((0, pa), (1, pb)):
                for k in range(KC):
                    src = catA if k < 4 else catD
                    Rsrc = R_catA[k] if k < 4 else R_catD[k - 4]
                    sc.op("pe", lambda h, k=k, n=n, pk=pk, t=t, src=src: h.matmul(
                        psum[pk][:], lhsT=src[:, k % 4, t * 128:(t + 1) * 128], rhs=Wo[:, k, n * 512:(n + 1) * 512],
                        start=(k == 0), stop=(k == KC - 1)), reads=[Rsrc, R_Wo], writes=[R_ps[pk]])
            postnorm_residual(t, pa, pb, wpost, R_wpost, tmpb[t % 2], R_tmpb[t % 2], ssb[t % 2], R_ssb[t % 2], junkp)
        end_phase()
        e.close()

    def ffn_phase(l):
        e = ExitStack()
        fcw = sb(e, "fcw", [128, 2 * NFC, 4], F32)
        R_fcw = Res("fcw")
        phase_load(fcw[:], fcw_d[:, l], R_fcw)
        halo = sb(e, "halo", [128, 2 * NFC, 2], F32)
        R_halo = Res("halo")
        hTh = sb(e, "hTh", [128, KC, 1024], BF16)
        R_hTh = [Res("hTh%d" % i) for i in range(8)]
        actT = sb(e, "actT", [128, NFC, 1024], BF16)
        R_act = [Res("act%d" % i) for i in range(NFC)]
        W2 = sb(e, "w_f2", [128, NFC, D], BF16)
        R_W2 = Res("w2")
        q_w2 = sc.dma_sem("wf2_%d" % l)
        q_w1 = [sc.dma_sem("wf1_%d_%d" % (l, i)) for i in range(2)]
        for hf in range(2):
            prenorm(l, 1, hTh, list(range(hf * 8, hf * 8 + 8)), R_hTh)
            if hf == 0:
                sc.dma("pool", q_w2, lambda h: h.dma_start(out=W2[:, 0:11, :], in_=w_f2_d[l, :, 0:11, :]), writes=[R_W2])
                sc.dma("pool", q_w2, lambda h: h.dma_start(out=W2[:, 11:22, :], in_=w_f2_d[l, :, 11:22, :]), writes=[R_W2])
                R_W2.w = (q_w2, sc.cnt[q_w2])
            eu = ExitStack()
            W1 = [sb(eu, "w_f1_%d" % i, [128, KC, 256], BF16) for i in range(2)]
            R_W1 = [Res("w1_%d" % i) for i in range(2)]
            xpd = [[sb(eu, "fxp%d_%d" % (p, i), [128, 2 + 512], F32) for i in range(2)] for p in range(2)]
            R_xpd = [[Res("fxp%d_%d" % (p, i)) for i in range(2)] for p in range(2)]
            ycv = [[sb(eu, "fy%d_%d" % (p, i), [128, 512], F32) for i in range(2)] for p in range(2)]
            R_ycv = [[Res("fy%d_%d" % (p, i)) for i in range(2)] for p in range(2)]
            ggl = [sb(eu, "ggl%d" % i, [128, 512], F32) for i in range(2)]
            R_ggl = [Res("ggl%d" % i) for i in range(2)]

            def load_w1(jc):
                i = jc % 2
                sc.dma("pool", q_w1[i], lambda h, i=i, jc=jc: h.dma_start(out=W1[i][:], in_=w_f1_d[l, jc]), writes=[R_W1[i]])

            load_w1(0)
            for jc in range(NFC):
                if jc + 1 < NFC:
                    load_w1(jc + 1)
                Wc = W1[jc % 2]
                RWc = R_W1[jc % 2]
                for gi in range(2):
                    g = hf * 2 + gi
                    bi = (jc * 2 + gi) % 2
                    for part in range(2):
                        ch = part * NFC + jc
                        pk = part * 2 + bi
                        xb, Rxb = xpd[part][bi], R_xpd[part][bi]
                        xprev, Rxprev = xpd[part][1 - bi], R_xpd[part][1 - bi]
                        yb, Ryb = ycv[part][bi], R_ycv[part][bi]
                        for k in range(KC):
                            sc.op("pe", lambda h, k=k, part=part, pk=pk, gi=gi, Wc=Wc: h.matmul(
                                psum[pk][:], lhsT=Wc[:, k, part * 128:(part + 1) * 128], rhs=hTh[:, k, gi * 512:(gi + 1) * 512],
                                start=(k == 0), stop=(k == KC - 1)), reads=[RWc] + R_hTh[gi * 4:(gi + 1) * 4], writes=[R_ps[pk]])
                        sc.op("act", lambda h, pk=pk, xb=xb: h.activation(out=xb[:, 2:514], in_=psum[pk][:], func=AF.Copy),
                              reads=[R_ps[pk]], writes=[Rxb])
                        if g == 0:
                            sc.op("pool", lambda h, xb=xb: h.memset(xb[:, 0:2], 0.0), writes=[Rxb])
                        elif gi == 0:
                            sc.op("pool", lambda h, xb=xb, ch=ch: h.tensor_copy(out=xb[:, 0:2], in_=halo[:, ch, :]),
                                  reads=[R_halo], writes=[Rxb])
                        else:
                            sc.op("pool", lambda h, xb=xb, xprev=xprev: h.tensor_copy(out=xb[:, 0:2], in_=xprev[:, 512:514]),
                                  reads=[Rxprev], writes=[Rxb])
                        if hf == 0 and gi == 1:
                            sc.op("pool", lambda h, xb=xb, ch=ch: h.tensor_copy(out=halo[:, ch, :], in_=xb[:, 512:514]),
                                  reads=[Rxb], writes=[R_halo])
                        sc.op("act", lambda h, pk=pk, yb=yb, ch=ch: h.activation(
                            out=yb[:], in_=psum[pk][:], func=AF.Identity, scale=fcw[:, ch, 2:3], bias=fcw[:, ch, 3:4]),
                            reads=[R_ps[pk], R_fcw], writes=[Ryb])
                        for j in range(2):
                            sc.op("dve", lambda h, j=j, yb=yb, xb=xb, ch=ch: h.scalar_tensor_tensor(
                                out=yb[:], in0=xb[:, j:j + 512], scalar=fcw[:, ch, j:j + 1], in1=yb[:],
                                op0=ALU.mult, op1=ALU.add), reads=[Rxb, Ryb, R_fcw], writes=[Ryb])
                    gb, Rgb = ggl[bi], R_ggl[bi]
                    sc.op("act", lambda h, gb=gb, bi=bi: h.activation(out=gb[:], in_=ycv[0][bi][:], func=AF.Gelu_apprx_tanh),
                          reads=[R_ycv[0][bi]], writes=[Rgb])
                    sc.op("pool", lambda h, gb=gb, bi=bi, jc=jc, gi=gi: h.tensor_tensor(
                        out=actT[:, jc, gi * 512:(gi + 1) * 512], in0=gb[:], in1=ycv[1][bi][:], op=ALU.mult),
                        reads=[Rgb, R_ycv[1][bi]], writes=[R_act[jc]])
            sc.barrier()
            sc.emit_all()
            eu.close()
            ed = ExitStack()
            wpost = sb(ed, "wpost", [128, D], F32)
            R_wpost = Res("wpost")
            phase_load(wpost[:], npost_d[:, l, 1, :], R_wpost)
            tmpb = [sb(ed, "tmpb%d" % i, [128, D], F32) for i in range(2)]
            R_tmpb = [Res("tmpb%d" % i) for i in range(2)]
            ssb = [sb(ed, "ssb%d" % i, [128, 4], F32) for i in range(2)]
            R_ssb = [Res("ssb%d" % i) for i in range(2)]
            junkp = sb(ed, "junkp", [128, 512], BF16)
            for i in range(2):
                sc.op("pool", lambda h, i=i: h.memset(ssb[i][:], 0.0), writes=[R_ssb[i]])
            for tl in range(8):
                t = hf * 8 + tl
                pa, pb = (0, 1) if tl % 2 == 0 else (2, 3)
                for n, pk in ((0, pa), (1, pb)):
                    for jc in range(NFC):
                        sc.op("pe", lambda h, jc=jc, n=n, pk=pk, tl=tl: h.matmul(
                            psum[pk][:], lhsT=actT[:, jc, tl * 128:(tl + 1) * 128], rhs=W2[:, jc, n * 512:(n + 1) * 512],
                            start=(jc == 0), stop=(jc == NFC - 1)), reads=[R_act[jc], R_W2], writes=[R_ps[pk]])
                postnorm_residual(t, pa, pb, wpost, R_wpost, tmpb[tl % 2], R_tmpb[tl % 2], ssb[tl % 2], R_ssb[tl % 2], junkp)
            sc.barrier()
            sc.emit_all()
            ed.close()
        e.close()

    for l in range(depth):
        e_mix = ExitStack()
        hT = sb(e_mix, "hT", [128, KC, S], BF16)
        hT_box[0] = hT
        R_hT[:] = [Res("hT%d" % t) for t in range(NT)]
        prenorm(l, 0, hT, list(range(NT)), R_hT)
        if dbg and l == 0:
            sc.dma("sp", q_out, lambda h: h.dma_start(out=dbg_d["hT"], in_=hT[:]), reads=R_hT)
        catA = sb(e_mix, "catA", [128, 4, S], BF16)
        R_catA = [Res("catA%d" % c) for c in range(4)]
        if stage >= 1:
            attention_phase(l, catA, R_catA)
        if dbg and l == 0:
            sc.dma("sp", q_out, lambda h: h.dma_start(out=dbg_d["catA"], in_=catA[:]), reads=R_catA)
            end_phase()
        catD = sb(e_mix, "catD", [128, 4, S], BF16)
        R_catD = [Res("catD%d" % c) for c in range(4)]
        if stage >= 2:
            deltanet_phase(l, catD, R_catD)
        if dbg and l == 0:
            sc.dma("sp", q_out, lambda h: h.dma_start(out=dbg_d["catD"], in_=catD[:]), reads=R_catD)
            end_phase()
        if stage >= 3:
            outproj_phase(l, catA, R_catA, catD, R_catD)
        e_mix.close()
        if dbg and l == 0:
            sc.dma("sp", q_out, lambda h: h.dma_start(out=dbg_d["xmid"].rearrange("(t p) d -> p t d", p=128), in_=x_sb[:]),
                   reads=R_x)
            end_phase()
        if stage >= 4:
            ffn_phase(l)

    yv = y_d.rearrange("(t p) d -> p t d", p=128)
    sc.dma("sp", q_out, lambda h: h.dma_start(out=yv, in_=x_sb[:]), reads=R_x)
    sc.wait_all("sp", [(q_out, sc.cnt[q_out])])
    sc.emit_all()
    es.close()
    return nc, sc


def host_prep(inputs, depth=DEPTH):
    f32 = np.float32
    w_in = np.asarray(inputs["w_in"], f32)
    out = {}
    w_att = np.empty((depth, 4, 128, KC, 640), f32)
    for l in range(depth):
        Wl = w_in[l].reshape(KC, 128, -1)
        for hp in range(4):
            cols = []
            for base in (0, 512):
                c = np.arange(base + hp * 128, base + hp * 128 + 128)
                cols.append(c)
                cs = c.reshape(2, 2, 32)[:, ::-1, :].reshape(-1)
                cols.append(cs)
            cols.append(np.arange(1024 + hp * 128, 1024 + hp * 128 + 128))
            cols = np.concatenate(cols)
            w_att[l, hp] = Wl[:, :, cols].transpose(1, 0, 2)
    out["w_att"] = w_att
    out["ident"] = np.eye(128, dtype=f32)
    inv = 1.0 / (10000.0 ** (np.arange(0, 64, 2, dtype=np.float32) / 64.0))
    ang = np.arange(S, dtype=np.float32)[None, :] * inv[:, None].astype(np.float32)
    cos = np.cos(ang).astype(f32)
    sin = np.sin(ang).astype(f32)
    rc = np.empty((128, S), f32)
    rs = np.empty((128, S), f32)
    for p in range(128):
        rc[p] = cos[p % 32]
        rs[p] = -sin[p % 32] if (p % 64) < 32 else sin[p % 32]
    out["rope_c"] = rc
    out["rope_s"] = rs
    c = np.arange(128)[:, None]
    a = np.arange(128)[None, :]
    Lm = (c >= a).astype(f32)
    Um = (c <= a).astype(f32)
    out["amask"] = np.concatenate([Lm, Um, Lm, Um, Lm, Um, Lm, Um], axis=1)
    npre = np.empty((128, depth, 2, KC), f32)
    for l in range(depth):
        npre[:, l, 0, :] = np.asarray(inputs["norm_pre_mix"], f32)[l].reshape(KC, 128).T
        npre[:, l, 1, :] = np.asarray(inputs["norm_pre_ffn"], f32)[l].reshape(KC, 128).T
    out["npre"] = npre
    w_dn = np.empty((depth, 4, 128, KC, 512), f32)
    w_ba = np.empty((depth, 128, KC, 8), f32)
    dn_cw = np.empty((128, depth, 4, 3, 4), f32)
    dn_hp = np.empty((128, depth, 2, 4), f32)
    dn_nw = np.empty((depth, 128, 1), f32)
    cwl = np.asarray(inputs["dn_conv_w"], f32)
    for l in range(depth):
        Wl = w_in[l].reshape(KC, 128, -1)
        for hd in range(4):
            cols = np.concatenate([np.arange(b + hd * 128, b + hd * 128 + 128) for b in (1536, 2048, 2560, 3072)])
            w_dn[l, hd] = Wl[:, :, cols].transpose(1, 0, 2)
            for cc in range(3):
                ch = cc * 512 + hd * 128 + np.arange(128)
                dn_cw[:, l, hd, cc, :] = cwl[l][:, ch].T
        w_ba[l] = Wl[:, :, 3584:3592].transpose(1, 0, 2)
        dn_hp[:, l, 0, :] = np.asarray(inputs["dn_a_log"], f32)[l][None, :]
        dn_hp[:, l, 1, :] = np.asarray(inputs["dn_dt_bias"], f32)[l][None, :]
        dn_nw[l, :, 0] = np.asarray(inputs["dn_norm_w"], f32)[l]
    out["w_dn"] = w_dn
    out["w_ba"] = w_ba
    out["dn_cw"] = dn_cw
    out["dn_hp"] = dn_hp
    out["dn_nwT"] = dn_nw
    ti = np.arange(128)[:, None]
    tj = np.arange(128)[None, :]
    same = (ti // 64) == (tj // 64)
    dn_c = np.zeros((128, 8, 128), f32)
    dn_c[:, 0] = (same & (ti <= tj))
    dn_c[:, 1] = (same & (ti > tj))
    dn_c[:, 2] = np.where(same & (ti > tj), 0.0, -30000.0)
    dn_c[:, 3] = np.where(same & (tj >= ti), 0.0, -30000.0)
    dn_c[:, 4] = same
    dn_c[:, 5] = (ti < 64) & (tj >= 0)
    dn_c[:, 6] = (ti >= 64) & (tj >= 0)
    dn_c[:, 7] = np.eye(128)
    out["dn_c"] = dn_c
    w_out = np.asarray(inputs["w_out"], f32)
    out["w_o"] = np.ascontiguousarray(w_out.reshape(depth, KC, 128, D).transpose(0, 2, 1, 3))
    npost = np.empty((128, depth, 2, D), f32)
    npost[:, :, 0, :] = np.asarray(inputs["norm_post_mix"], f32)[None]
    npost[:, :, 1, :] = np.asarray(inputs["norm_post_ffn"], f32)[None]
    out["npost"] = npost
    fw1 = np.asarray(inputs["ffn_w_in"], f32).reshape(depth, KC, 128, 2, NFC, 128)
    out["w_f1"] = np.ascontiguousarray(fw1.transpose(0, 4, 2, 1, 3, 5).reshape(depth, NFC, 128, KC, 256))
    fw2 = np.asarray(inputs["ffn_w_out"], f32).reshape(depth, NFC, 128, D)
    out["w_f2"] = np.ascontiguousarray(fw2.transpose(0, 2, 1, 3))
    fcw = np.empty((128, depth, 2 * NFC, 4), f32)
    cwf = np.asarray(inputs["ffn_conv_w"], f32).reshape(depth, 3, 2 * NFC, 128)
    cbf = np.asarray(inputs["ffn_conv_b"], f32).reshape(depth, 2 * NFC, 128)
    fcw[:, :, :, 0:3] = cwf.transpose(3, 0, 2, 1)
    fcw[:, :, :, 3] = cbf.transpose(2, 0, 1)
    out["fcw"] = fcw
    return out


_CACHE = {}


def kernel(**inputs):
    x = np.ascontiguousarray(np.asarray(inputs["x"], np.float32))
    shared = host_prep(inputs)
    if "nc" not in _CACHE:
        _CACHE["nc"] = build_program()[0]
    nc = _CACHE["nc"]
    in_maps = []
    for c in range(N_CORES):
        m = dict(shared)
        m["x"] = x[c]
        in_maps.append(m)
    res = run_bass_kernel_spmd(nc, in_maps, core_ids=list(range(N_CORES)))
    return np.stack([np.asarray(r["y"], np.float32) for r in res.results], axis=0)
```

```python
import numpy as np
import ml_dtypes
from contextlib import ExitStack
import concourse.bass as bass
import concourse.mybir as mybir
from concourse.bass_utils import run_bass_kernel_spmd

F32 = mybir.dt.float32
BF16 = mybir.dt.bfloat16
AF = mybir.ActivationFunctionType
ALU = mybir.AluOpType

S = 2048
D = 1024
NT = 16
KC = 8
DEPTH = 2
DFF = 2816
NFC = 22
EPS = 1e-6
N_CORES = 8


class Res:
    __slots__ = ("w", "r", "excl", "name")

    def __init__(self, name="", excl=False):
        self.w = None
        self.r = []
        self.excl = excl
        self.name = name


def run_interleaved(gens):
    active = list(gens)
    while active:
        for g in list(active):
            try:
                next(g)
            except StopIteration:
                active.remove(g)


class Sched:
    ENG = ("pe", "act", "dve", "pool", "sp")

    def __init__(self, nc, es):
        self.nc = nc
        self.es = es
        self.sems = {}
        self.cnt = {}
        for e in self.ENG:
            self.sems[e] = es.enter_context(nc.semaphore("sem_" + e))
            self.cnt[e] = 0
        self.waited = {e: {} for e in self.ENG}
        self.stream = {e: [] for e in self.ENG}
        self.nwaits = 0

    def dma_sem(self, name):
        self.sems[name] = self.es.enter_context(self.nc.semaphore("dq_" + name))
        self.cnt[name] = 0
        return name

    def _collect(self, eng, reads, writes):
        need = {}

        def add(tok, kind):
            if tok is None:
                return
            key, val = tok
            if key == eng:
                if eng in ("pe", "sp"):
                    return
            if self.waited[eng].get(key, 0) >= val:
                return
            if need.get(key, 0) < val:
                need[key] = val

        for res in reads:
            add(res.w, "raw")
            if res.excl:
                for t in res.r:
                    if t[0] != eng:
                        add(t, "war")
        for res in writes:
            add(res.w, "waw")
            for t in res.r:
                add(t, "war")
        for k, v in need.items():
            self.waited[eng][k] = v
        return list(need.items())

    def _commit(self, tok, reads, writes):
        for res in reads:
            res.r = [t for t in res.r if t[0] != tok[0]] + [tok]
        for res in writes:
            res.w = tok
            res.r = []

    def op(self, eng, fn, reads=(), writes=()):
        waits = self._collect(eng, reads, writes)
        self.cnt[eng] += 1
        tok = (eng, self.cnt[eng])
        sems = self.sems
        semh = sems[eng]
        self.nwaits += len(waits)

        def emit(h, waits=waits, fn=fn, semh=semh):
            for k, v in waits:
                h.wait_ge(sems[k], v)
            fn(h).then_inc(semh, 1)

        self.stream[eng].append(emit)
        self._commit(tok, reads, writes)
        return tok

    def dma(self, eng, semkey, fn, reads=(), writes=()):
        waits = self._collect(eng, reads, writes)
        self.cnt[semkey] += 16
        tok = (semkey, self.cnt[semkey])
        sems = self.sems
        semh = sems[semkey]

        def emit(h, waits=waits, fn=fn, semh=semh):
            for k, v in waits:
                h.wait_ge(sems[k], v)
            fn(h).then_inc(semh, 16)

        self.stream[eng].append(emit)
        self._commit(tok, reads, writes)
        return tok

    def wait_all(self, eng, toks):
        need = {}
        for key, val in toks:
            if need.get(key, 0) < val:
                need[key] = val
        waits = list(need.items())
        sems = self.sems

        def emit(h, waits=waits):
            for k, v in waits:
                h.wait_ge(sems[k], v)

        self.stream[eng].append(emit)

    def barrier(self, wait_dma=True):
        snap = dict(self.cnt)
        if not wait_dma:
            snap = {k: v for k, v in snap.items() if k in self.ENG}
        for e in self.ENG:
            waits = [(k, v) for k, v in snap.items() if v > 0 and k != e and self.waited[e].get(k, 0) < v]
            for k, v in waits:
                self.waited[e][k] = v
            if e in ("act", "dve", "pool") and snap[e] > 0:
                waits.append((e, snap[e]))
                self.waited[e][e] = snap[e]
            sems = self.sems

            def emit(h, waits=waits):
                for k, v in waits:
                    h.wait_ge(sems[k], v)

            self.stream[e].append(emit)

    def emit_all(self):
        nc = self.nc
        streams = self.stream
        self.stream = {e: [] for e in self.ENG}
        self._emit(streams)

    def _emit(self, streams):
        nc = self.nc
        self_stream = streams
        with nc.Block() as block:
            @block.tensor
            def _(h):
                for f in self_stream["pe"]:
                    f(h)

            @block.scalar
            def _(h):
                for f in self_stream["act"]:
                    f(h)

            @block.vector
            def _(h):
                for f in self_stream["dve"]:
                    f(h)

            @block.gpsimd
            def _(h):
                for f in self_stream["pool"]:
                    f(h)

            @block.sync
            def _(h):
                for f in self_stream["sp"]:
                    f(h)


def build_program(depth=DEPTH, stage=99, dbg=False):
    nc = bass.Bass("TRN2", target_bir_lowering=False)
    es = ExitStack()
    sc = Sched(nc, es)

    def dram_in(name, shape, dt=F32):
        return nc.dram_tensor(name, list(shape), dt, kind="ExternalInput").ap()

    def dram_out(name, shape, dt=F32):
        return nc.dram_tensor(name, list(shape), dt, kind="ExternalOutput").ap()

    uid = [0]

    def sb(e, name, shape, dt):
        uid[0] += 1
        return e.enter_context(nc.sbuf_tensor("sb%d_%s" % (uid[0], name), list(shape), dt))

    x_d = dram_in("x", [S, D])
    y_d = dram_out("y", [S, D])
    w_att_d = dram_in("w_att", [depth, 4, 128, KC, 640])
    ident_d = dram_in("ident", [128, 128])
    ropec_d = dram_in("rope_c", [128, S])
    ropes_d = dram_in("rope_s", [128, S])
    amask_d = dram_in("amask", [128, 1024])
    npre_d = dram_in("npre", [128, depth, 2, KC])
    w_dn_d = dram_in("w_dn", [depth, 4, 128, KC, 512])
    w_ba_d = dram_in("w_ba", [depth, 128, KC, 8])
    dn_cw_d = dram_in("dn_cw", [128, depth, 4, 3, 4])
    dn_hp_d = dram_in("dn_hp", [128, depth, 2, 4])
    dn_nwT_d = dram_in("dn_nwT", [depth, 128, 1])
    dn_c_d = dram_in("dn_c", [128, 8, 128])
    w_o_d = dram_in("w_o", [depth, 128, KC, D])
    npost_d = dram_in("npost", [128, depth, 2, D])
    w_f1_d = dram_in("w_f1", [depth, NFC, 128, KC, 256])
    w_f2_d = dram_in("w_f2", [depth, 128, NFC, D])
    fcw_d = dram_in("fcw", [128, depth, 2 * NFC, 4])
    dbg_d = {}
    if dbg:
        dbg_d["hT"] = dram_out("dbg_hT", [128, KC, S], BF16)
        dbg_d["catA"] = dram_out("dbg_catA", [128, 4, S], BF16)
        dbg_d["catD"] = dram_out("dbg_catD", [128, 4, S], BF16)
        dbg_d["xmid"] = dram_out("dbg_xmid", [S, D], F32)

    x_sb = sb(es, "x_sb", [128, NT, D], F32)
    hT_box = [None]
    ident = sb(es, "ident", [128, 128], BF16)
    npre = sb(es, "npre", [128, depth, 2, KC], F32)
    R_x = [Res("x%d" % t) for t in range(NT)]
    R_hT = [Res("hT%d" % t) for t in range(NT)]
    R_const = Res("const")

    psum = [es.enter_context(nc.psum_tensor("ps%d" % i, [128, 512], F32)) for i in range(8)]
    R_ps = [Res("ps%d" % i, excl=True) for i in range(8)]

    q_out = sc.dma_sem("out")
    q_ld = {"sp": [sc.dma_sem("ld%d" % i) for i in range(4)],
            "pool": [sc.dma_sem("ldp%d" % i) for i in range(3)]}
    ld_rr = {"sp": 0, "pool": 0}

    def phase_load(out_ap, in_ap, res, eng="sp"):
        qs = q_ld[eng]
        q = qs[ld_rr[eng] % len(qs)]
        ld_rr[eng] += 1
        return sc.dma(eng, q, lambda h: h.dma_start(out=out_ap, in_=in_ap), writes=[res])

    def end_phase(wait_dma=True):
        sc.barrier(wait_dma)
        sc.emit_all()

    xv = x_d.rearrange("(t p) d -> p t d", p=128)
    for t4 in range(4):
        qx = sc.dma_sem("xin%d" % t4)
        sc.dma("sp", qx, lambda h, t4=t4: h.dma_start(out=x_sb[:, t4 * 4:(t4 + 1) * 4, :], in_=xv[:, t4 * 4:(t4 + 1) * 4, :]),
               writes=R_x[t4 * 4:(t4 + 1) * 4])
    phase_load(ident[:], ident_d, R_const, eng="pool")
    phase_load(npre[:], npre_d, R_const)
    end_phase()

    def prenorm(l, which, hT, tiles, R_hTl):
        e = ExitStack()
        ss_all = sb(e, "ss_all", [128, NT], F32)
        rstd_all = sb(e, "rstd_all", [128, NT], F32)
        junk = sb(e, "junk", [128, D], BF16)
        hb = [sb(e, "hb%d" % i, [128, D], BF16) for i in range(2)]
        R_hb = [Res("hb%d" % i) for i in range(2)]
        R_ss = Res("ss")
        sc.op("dve", lambda h: h.memset(ss_all[:], 0.0), writes=[R_ss])
        for t in tiles:
            sc.op("act", lambda h, t=t: h.activation(out=junk[:], in_=x_sb[:, t, :], func=AF.Square,
                                                     accum_out=ss_all[:, t:t + 1]),
                  reads=[R_x[t]], writes=[R_ss])
        sc.op("dve", lambda h: h.tensor_scalar(out=rstd_all[:], in0=ss_all[:], scalar1=1.0 / D, scalar2=EPS,
                                               op0=ALU.mult, op1=ALU.add), reads=[R_ss], writes=[R_ss])
        sc.op("act", lambda h: h.activation(out=rstd_all[:], in_=rstd_all[:], func=AF.Sqrt),
              reads=[R_ss], writes=[R_ss])
        sc.op("dve", lambda h: h.reciprocal(out=rstd_all[:], in_=rstd_all[:]), reads=[R_ss], writes=[R_ss])
        for ti, t in enumerate(tiles):
            b = t % 2
            pb = 6 + (t % 2)
            sc.op("act", lambda h, t=t, b=b: h.activation(out=hb[b][:], in_=x_sb[:, t, :], func=AF.Copy,
                                                          scale=rstd_all[:, t:t + 1]),
                  reads=[R_x[t], R_ss], writes=[R_hb[b]])
            pst = psum[pb][:].bitcast(BF16)
            for k in range(KC):
                sc.op("pe", lambda h, k=k, b=b, pst=pst: h.transpose(pst[:, k * 128:(k + 1) * 128],
                                                                     hb[b][:, k * 128:(k + 1) * 128], ident[:]),
                      reads=[R_hb[b], R_const], writes=[R_ps[pb]])
            wv = npre[:, l, which, :].unsqueeze(2).to_broadcast([128, KC, 128])
            sc.op("dve", lambda h, ti=ti, pst=pst, wv=wv: h.tensor_tensor(
                out=hT[:, :, ti * 128:(ti + 1) * 128], in0=pst.rearrange("p (k c) -> p k c", k=KC), in1=wv, op=ALU.mult),
                reads=[R_ps[pb], R_const], writes=[R_hTl[ti]])
        end_phase(wait_dma=False)
        e.close()

    BR = ((1, 16), (4, 4), (16, 1))

    def attention_phase(l, catA, R_catA, pre=None):
        hT = hT_box[0]
        e = ExitStack()
        ropec = sb(e, "ropec", [128, S], F32)
        ropes = sb(e, "ropes", [128, S], F32)
        amask = sb(e, "amask", [128, 1024], BF16)
        R_tab = Res("tab")
        phase_load(ropec[:], ropec_d, R_tab)
        phase_load(ropes[:], ropes_d, R_tab)
        phase_load(amask[:], amask_d, R_tab, eng="pool")
        w_att = [sb(e, "w_att%d" % i, [128, KC, 640], BF16) for i in range(2)]
        R_watt = [Res("watt%d" % i) for i in range(2)]
        q_watt = [sc.dma_sem("watt%d_%d" % (l, i)) for i in range(2)]

        def load_watt(hp):
            i = hp % 2
            sc.dma("pool", q_watt[i], lambda h, i=i: h.dma_start(out=w_att[i][:], in_=w_att_d[l, hp]),
                   writes=[R_watt[i]])

        load_watt(0)
        if pre is not None:
            pre()
        qT = sb(e, "qT", [128, S], BF16)
        kTe = sb(e, "kTe", [128, S], BF16)
        kTo = sb(e, "kTo", [128, S], BF16)
        R_qT, R_kT = Res("qT"), Res("kT")
        vT = sb(e, "vT", [128, S], BF16)
        R_vT = Res("vT")
        rtmp = [sb(e, "rtmp%d" % i, [128, 2, 512], F32) for i in range(2)]
        R_rtmp = [Res("rtmp%d" % i) for i in range(2)]
        VX = sb(e, "vext", [128, 16, 2, 128], BF16)
        R_vext = Res("vext")
        pT = [sb(e, "pT%d" % i, [128, 512], BF16) for i in range(6)]
        R_pT = [Res("pT%d" % i) for i in range(6)]
        acc = sb(e, "acc", [128, 2, S], F32)
        R_acc = [Res("acc0"), Res("acc1")]

        sc.op("pool", lambda h: h.memset(kTe[64:128, :], 0.0), writes=[R_kT])
        sc.op("pool", lambda h: h.memset(kTo[0:64, :], 0.0), writes=[R_kT])
        sc.op("pool", lambda h: h.memset(VX[:, :, :, 64:128], 1.0), writes=[R_vext])

        def attention_unit(hp):
            W = w_att[hp % 2]
            RW = R_watt[hp % 2]
            for qk in range(2):
                for g in range(4):
                    pa, pb = (0, 1) if g % 2 == 0 else (2, 3)
                    for ci, pbk in ((0, pa), (1, pb)):
                        c0 = qk * 256 + ci * 128
                        for k in range(KC):
                            sc.op("pe", lambda h, k=k, c0=c0, pbk=pbk, g=g: h.matmul(
                                psum[pbk][:], lhsT=W[:, k, c0:c0 + 128], rhs=hT[:, k, g * 512:(g + 1) * 512],
                                start=(k == 0), stop=(k == KC - 1)),
                                reads=[RW] + R_hT[g * 4:(g + 1) * 4], writes=[R_ps[pbk]])
                    rb = g % 2
                    sc.op("dve", lambda h, pa=pa, g=g, rb=rb: h.tensor_tensor(
                        out=rtmp[rb][:, 0, :], in0=psum[pa][:], in1=ropec[:, g * 512:(g + 1) * 512], op=ALU.mult),
                        reads=[R_ps[pa], R_tab], writes=[R_rtmp[rb]])
                    sc.op("dve", lambda h, pb=pb, g=g, rb=rb: h.tensor_tensor(
                        out=rtmp[rb][:, 1, :], in0=psum[pb][:], in1=ropes[:, g * 512:(g + 1) * 512], op=ALU.mult),
                        reads=[R_ps[pb], R_tab], writes=[R_rtmp[rb]])
                    if qk == 0:
                        sc.op("pool", lambda h, g=g, rb=rb: h.tensor_tensor(
                            out=qT[:, g * 512:(g + 1) * 512], in0=rtmp[rb][:, 0, :], in1=rtmp[rb][:, 1, :],
                            op=ALU.add), reads=[R_rtmp[rb]], writes=[R_qT])
                    else:
                        sc.op("pool", lambda h, g=g, rb=rb: h.tensor_tensor(
                            out=kTe[0:64, g * 512:(g + 1) * 512], in0=rtmp[rb][0:64, 0, :],
                            in1=rtmp[rb][0:64, 1, :], op=ALU.add), reads=[R_rtmp[rb]], writes=[R_kT])
                        sc.op("pool", lambda h, g=g, rb=rb: h.tensor_tensor(
                            out=kTo[64:128, g * 512:(g + 1) * 512], in0=rtmp[rb][64:128, 0, :],
                            in1=rtmp[rb][64:128, 1, :], op=ALU.add), reads=[R_rtmp[rb]], writes=[R_kT])

            for g in range(4):
                pbk = g % 2
                for k in range(KC):
                    sc.op("pe", lambda h, k=k, pbk=pbk, g=g: h.matmul(
                        psum[pbk][:], lhsT=W[:, k, 512:640], rhs=hT[:, k, g * 512:(g + 1) * 512],
                        start=(k == 0), stop=(k == KC - 1)),
                        reads=[RW] + R_hT[g * 4:(g + 1) * 4], writes=[R_ps[pbk]])
                sc.op("act", lambda h, pbk=pbk, g=g: h.activation(out=vT[:, g * 512:(g + 1) * 512], in_=psum[pbk][:], func=AF.Copy),
                      reads=[R_ps[pbk]], writes=[R_vT])
            kTs = (kTe, kTo)

            def head_task(hh, bi, d, nb):
                kT_h = kTs[hh]
                sbank = (0, 1) if hh == 0 else (2, 3)
                obank = (4, 5) if hh == 0 else (6, 7)
                pTs = pT[hh * 3:hh * 3 + 3]
                RpTs = R_pT[hh * 3:hh * 3 + 3]
                steps = []
                oi = 0
                for r in range(d):
                    for ob in range(0, nb, 4):
                        nq = min(4, nb - ob)
                        for half in range(0, nq, 2):
                            steps.append((r, ob, nq, half, min(2, nq - half), oi, half + 2 >= nq))
                        oi += 1
                pending = None

                def emit_qk(i, st):
                    r, ob, nq, half, n2, oi, last = st
                    pss = sbank[i % 2]
                    pti = i % 3
                    for jj in range(n2):
                        bq = ob + half + jj
                        q0 = r + d * 128 * bq
                        qsl = qT[:, q0:q0 + d * 127 + 1:d]
                        if bq > 0:
                            k0 = r + d * 128 * (bq - 1)
                            sc.op("pe", lambda h, jj=jj, k0=k0, qsl=qsl, pss=pss: h.matmul(
                                psum[pss][:, jj * 256:jj * 256 + 128],
                                lhsT=kT_h[:, k0:k0 + d * 127 + 1:d], rhs=qsl, start=True, stop=True),
                                reads=[R_kT, R_qT], writes=[R_ps[pss]])
                        k0 = r + d * 128 * bq
                        sc.op("pe", lambda h, jj=jj, k0=k0, qsl=qsl, pss=pss: h.matmul(
                            psum[pss][:, jj * 256 + 128:jj * 256 + 256],
                            lhsT=kT_h[:, k0:k0 + d * 127 + 1:d], rhs=qsl, start=True, stop=True),
                            reads=[R_kT, R_qT], writes=[R_ps[pss]])
                    c_lo = 128 if (ob + half == 0) else 0
                    c_hi = n2 * 256
                    sc.op("act", lambda h, pss=pss, pti=pti, c_lo=c_lo, c_hi=c_hi: h.activation(
                        out=pTs[pti][:, c_lo:c_hi], in_=psum[pss][:, c_lo:c_hi], func=AF.Exp, scale=0.125),
                        reads=[R_ps[pss]], writes=[RpTs[pti]])
                    sc.op("dve", lambda h, pti=pti, c_lo=c_lo, c_hi=c_hi: h.tensor_tensor(
                        out=pTs[pti][:, c_lo:c_hi], in0=pTs[pti][:, c_lo:c_hi], in1=amask[:, c_lo:c_hi],
                        op=ALU.mult), reads=[RpTs[pti], R_tab], writes=[RpTs[pti]])

                def emit_pv(i, st):
                    r, ob, nq, half, n2, oi, last = st
                    pti = i % 3
                    pso = obank[oi % 2]
                    for jj in range(n2):
                        bq = ob + half + jj
                        oc = (half + jj) * 128
                        if bq > 0:
                            sc.op("pe", lambda h, jj=jj, bq=bq, oc=oc, pso=pso, pti=pti, r=r: h.matmul(
                                psum[pso][:, oc:oc + 128], lhsT=VX[:, r * nb + bq - 1, hh, :],
                                rhs=pTs[pti][:, jj * 256:jj * 256 + 128], start=True, stop=False),
                                reads=[R_vext, RpTs[pti]], writes=[R_ps[pso]])
                        sc.op("pe", lambda h, jj=jj, bq=bq, oc=oc, pso=pso, pti=pti, r=r: h.matmul(
                            psum[pso][:, oc:oc + 128], lhsT=VX[:, r * nb + bq, hh, :],
                            rhs=pTs[pti][:, jj * 256 + 128:jj * 256 + 256], start=(bq == 0), stop=True),
                            reads=[R_vext, RpTs[pti]], writes=[R_ps[pso]])
                    if last:
                        t0 = r + d * 128 * ob
                        n = nq * 128
                        dst = acc[:, hh, t0:t0 + d * (n - 1) + 1:d]
                        if bi == 0:
                            sc.op("act", lambda h, dst=dst, pso=pso, n=n: h.activation(
                                out=dst, in_=psum[pso][:, 0:n], func=AF.Copy),
                                reads=[R_ps[pso]], writes=[R_acc[hh]])
                        else:
                            sc.op("dve", lambda h, dst=dst, pso=pso, n=n: h.tensor_tensor(
                                out=dst, in0=psum[pso][:, 0:n], in1=dst, op=ALU.add),
                                reads=[R_ps[pso], R_acc[hh]], writes=[R_acc[hh]])

                for i, st in enumerate(steps):
                    emit_qk(i, st)
                    if pending is not None:
                        emit_pv(*pending)
                    pending = (i, st)
                    yield
                emit_pv(*pending)
                yield

            for bi, (d, nb) in enumerate(BR):
                for q4 in range(4):
                    pbk = q4 % 2
                    pvb = psum[pbk][:].bitcast(BF16)
                    for j in range(4):
                        blk = q4 * 4 + j
                        r, kb = blk // nb, blk % nb
                        t0 = r + d * 128 * kb
                        sc.op("pe", lambda h, j=j, t0=t0, pvb=pvb, d=d: h.transpose(
                            pvb[:, j * 128:(j + 1) * 128], vT[:, t0:t0 + d * 127 + 1:d], ident[:]),
                            reads=[R_vT, R_const], writes=[R_ps[pbk]])
                    sc.op("act", lambda h, q4=q4, pvb=pvb: h.activation(
                        out=VX[:, q4 * 4:(q4 + 1) * 4, :, 0:64],
                        in_=pvb[:, 0:512].rearrange("p (j h c) -> p j h c", j=4, h=2), func=AF.Copy),
                        reads=[R_ps[pbk]], writes=[R_vext])
                run_interleaved([head_task(0, bi, d, nb), head_task(1, bi, d, nb)])
            for hh in range(2):
                for g in range(4):
                    rb = g % 2
                    rlv = rtmp[rb][0:64, 0, :]
                    sc.op("act", lambda h, hh=hh, g=g, rlv=rlv: h.activation(
                        out=rlv, in_=acc[64:128, hh, g * 512:(g + 1) * 512], func=AF.Ln),
                        reads=[R_acc[hh]], writes=[R_rtmp[rb]])
                    sc.op("act", lambda h, rlv=rlv: h.activation(out=rlv, in_=rlv, func=AF.Exp, scale=-1.0),
                          reads=[R_rtmp[rb]], writes=[R_rtmp[rb]])
                    sc.op("pool", lambda h, hh=hh, g=g, rlv=rlv: h.tensor_tensor(
                        out=catA[64 * hh:64 * hh + 64, hp, g * 512:(g + 1) * 512],
                        in0=acc[0:64, hh, g * 512:(g + 1) * 512], in1=rlv, op=ALU.mult),
                        reads=[R_acc[hh], R_rtmp[rb]], writes=[R_catA[hp]])

        for hp in range(4):
            if hp < 3:
                load_watt(hp + 1)
            attention_unit(hp)
        end_phase()
        e.close()


    def deltanet_phase(l, catD, R_catD):
        hT = hT_box[0]
        e = ExitStack()
        C = sb(e, "dn_c", [128, 8, 128], F32)
        M1, M2, maskS, maskTu, Mblk, Mc0, Mc1, identf = [C[:, i, :] for i in range(8)]
        R_c = Res("dnc")
        phase_load(C[:], dn_c_d, R_c)
        cw = sb(e, "dn_cw", [128, 4, 3, 4], F32)
        hp = sb(e, "dn_hp", [128, 2, 4], F32)
        wba = sb(e, "w_ba", [128, KC, 8], BF16)
        R_c2 = Res("dnc2")
        phase_load(cw[:], dn_cw_d[:, l], R_c2)
        phase_load(hp[:], dn_hp_d[:, l], R_c2)
        R_wba = Res("wba")
        phase_load(wba[:], w_ba_d[l], R_wba, eng="pool")
        onesb = sb(e, "onesb", [128, 1], BF16)
        sc.op("pool", lambda h: h.memset(onesb[:], 1.0), writes=[R_c2])

        Wd = sb(e, "w_dn", [128, KC, 512], BF16)
        R_Wd = Res("wdn")
        q_wdn = sc.dma_sem("wdn_%d" % l)

        def load_wdn(hd):
            sc.dma("pool", q_wdn, lambda h: h.dma_start(out=Wd[:], in_=w_dn_d[l, hd]), writes=[R_Wd])

        SC = sb(e, "dn_sc", [128, 8, NT, 4], F32)
        beta, xa, graw, egc, ekd, tmpa = [SC[:, i] for i in range(6)]
        egl = SC[:, 6:8]
        R_sc = Res("dnsc")
        expA = sb(e, "expA", [128, 4], F32)
        pb = psum[0]
        for t in range(NT):
            for k in range(KC):
                sc.op("pe", lambda h, t=t, k=k: h.matmul(pb[:, t * 8:(t + 1) * 8], lhsT=hT[:, k, t * 128:(t + 1) * 128],
                                                         rhs=wba[:, k, :], start=(k == 0), stop=(k == KC - 1)),
                      reads=[R_hT[t], R_wba], writes=[R_ps[0]])
        bav = pb[:, 0:128].rearrange("p (t c) -> p t c", c=8)
        sc.op("act", lambda h: h.activation(out=beta, in_=bav[:, :, 0:4], func=AF.Sigmoid),
              reads=[R_ps[0]], writes=[R_sc])
        sc.op("dve", lambda h: h.tensor_tensor(out=xa, in0=bav[:, :, 4:8],
                                               in1=hp[:, 1, :].unsqueeze(1).to_broadcast([128, NT, 4]), op=ALU.add),
              reads=[R_ps[0], R_c2], writes=[R_sc])
        sc.op("act", lambda h: h.activation(out=xa, in_=xa, func=AF.Exp), reads=[R_sc], writes=[R_sc])
        sc.op("act", lambda h: h.activation(out=xa, in_=xa, func=AF.Ln, bias=1.0), reads=[R_sc], writes=[R_sc])
        sc.op("act", lambda h: h.activation(out=expA[:], in_=hp[:, 0, :], func=AF.Exp), reads=[R_c2], writes=[R_sc])
        sc.op("dve", lambda h: h.scalar_tensor_tensor(out=graw, in0=xa, scalar=-1.0,
                                                      in1=expA[:].unsqueeze(1).to_broadcast([128, NT, 4]),
                                                      op0=ALU.mult, op1=ALU.mult), reads=[R_sc], writes=[R_sc])
        grf = graw.rearrange("p t c -> p (t c)")
        pg = psum[1]
        for i, Mx in enumerate((M1, Mblk, Mc0, Mc1)):
            sc.op("pe", lambda h, i=i, Mx=Mx: h.matmul(pg[:, i * 64:(i + 1) * 64], lhsT=Mx, rhs=grf, start=True, stop=True),
                  reads=[R_c, R_sc], writes=[R_ps[1]])
        pgv = pg[:, 0:256].rearrange("p (i t c) -> p i t c", i=4, c=4)
        sc.op("act", lambda h: h.activation(out=egc, in_=pgv[:, 0], func=AF.Exp), reads=[R_ps[1]], writes=[R_sc])
        sc.op("dve", lambda h: h.tensor_copy(out=tmpa, in_=pgv[:, 0]), reads=[R_ps[1]], writes=[R_sc])
        sc.op("dve", lambda h: h.tensor_tensor(out=ekd, in0=pgv[:, 1], in1=tmpa, op=ALU.subtract),
              reads=[R_ps[1], R_sc], writes=[R_sc])
        sc.op("act", lambda h: h.activation(out=ekd, in_=ekd, func=AF.Exp), reads=[R_sc], writes=[R_sc])
        sc.op("act", lambda h: h.activation(out=SC[:, 6:8].rearrange("p a t c -> p (a t c)"), in_=pg[:, 128:256], func=AF.Exp),
              reads=[R_ps[1]], writes=[R_sc])

        xpad = [sb(e, "xpad%d" % i, [128, 3 + 512], BF16) for i in range(2)]
        R_xpad = [Res("xpad%d" % i) for i in range(2)]
        dg = sb(e, "dgw", [128, 12, 128], BF16)
        R_dg = Res("dg")
        csT = [sb(e, "csT%d" % i, [128, S], BF16) for i in range(3)]
        R_csT = [Res("csT%d" % i) for i in range(3)]
        sqt = sb(e, "sq", [128, S], BF16)
        sq = sqt[:]
        R_sqall = [Res("sq")]
        zw2 = [sb(e, "zw%d" % i, [128, NT, 128], BF16) for i in range(2)]
        R_zw2 = [Res("zw%d" % i) for i in range(2)]
        HS = sb(e, "dn_hs", [128, 6, NT], F32)
        R_hs = Res("hs")
        NH = 2
        GMh, R_GMh, ABh, R_ABh = [], [], [], []
        for hf_ in range(2):
            gmb = [sb(e, "gm%d_%d" % (hf_, i), [128, NH, 128], F32) for i in range(6)]
            rgm = [Res("gm%d_%d" % (hf_, i)) for i in range(6)]
            GMh.append([gmb[0], gmb[1], gmb[2], gmb[3], gmb[0], gmb[4], gmb[1], gmb[5], gmb[2]])
            R_GMh.append([rgm[0], rgm[1], rgm[2], rgm[3], rgm[0], rgm[4], rgm[1], rgm[5], rgm[2]])
            ABh.append([sb(e, "ab%d_%d" % (hf_, i), [128, NH, 128], BF16) for i in range(4)])
            R_ABh.append([Res("ab%d_%d" % (hf_, i)) for i in range(4)])
        HB = [[[sb(e, "hb%d_%d_%d" % (p, hf_, i), [128, NH, 128], BF16) for i in range(7)] for hf_ in range(2)]
              for p in range(2)]
        R_HB = [[[Res("hb%d_%d_%d" % (p, hf_, i)) for i in range(7)] for hf_ in range(2)] for p in range(2)]
        Sf = sb(e, "Sf", [128, 128], F32)
        Sb = sb(e, "Sb", [128, 128], BF16)
        R_S = Res("S")
        vnew = sb(e, "vnew", [128, 128], BF16)
        R_vnew = Res("vnew")
        p2s = [sb(e, "p2s%d" % i, [128, 128], F32) for i in range(2)]
        R_p2s = [Res("p2s%d" % i) for i in range(2)]
        osb = sb(e, "osb", [128, 4, 128], F32)
        R_osb = Res("osb")
        otk = sb(e, "otk", [128, 4, 128], BF16)
        R_otk = Res("otk")
        oss = sb(e, "oss", [128, 8], F32)
        R_oss = Res("oss")
        junk = sb(e, "junkd", [128, 128], BF16)
        sc.op("pool", lambda h: h.memset(oss[:], 0.0), writes=[R_oss])
        for p in range(2):
            for hf_ in range(2):
                sc.op("pool", lambda h, p=p, hf_=hf_: h.memset(HB[p][hf_][4][:], 0.0), writes=[R_HB[p][hf_][4]])
                sc.op("pool", lambda h, p=p, hf_=hf_: h.memset(HB[p][hf_][5][:], 0.0), writes=[R_HB[p][hf_][5]])

        load_wdn(0)

        def stage_C(hd):
            zw = zw2[hd % 2]
            R_zw = R_zw2[hd % 2]
            for cc in range(3):
                for j in range(4):
                    sc.op("dve", lambda h, cc=cc, j=j: h.tensor_scalar(
                        out=dg[:, cc * 4 + j, :], in0=ident[:], scalar1=cw[:, hd, cc, j:j + 1], scalar2=None, op0=ALU.mult),
                        reads=[R_const, R_c2], writes=[R_dg])
            yield
            for cc in range(3):
                for g in range(4):
                    pj = g % 2
                    pc = 2 + (g % 2)
                    xb = xpad[(cc * 4 + g) % 2]
                    Rxb = R_xpad[(cc * 4 + g) % 2]
                    xprev = xpad[(cc * 4 + g + 1) % 2]
                    Rxprev = R_xpad[(cc * 4 + g + 1) % 2]
                    for k in range(KC):
                        sc.op("pe", lambda h, k=k, cc=cc, g=g, pj=pj: h.matmul(
                            psum[pj][:], lhsT=Wd[:, k, cc * 128:(cc + 1) * 128], rhs=hT[:, k, g * 512:(g + 1) * 512],
                            start=(k == 0), stop=(k == KC - 1)), reads=[R_Wd] + R_hT[g * 4:(g + 1) * 4], writes=[R_ps[pj]])
                    sc.op("act", lambda h, pj=pj, xb=xb: h.activation(out=xb[:, 3:515], in_=psum[pj][:], func=AF.Copy),
                          reads=[R_ps[pj]], writes=[Rxb])
                    if g == 0:
                        sc.op("pool", lambda h, xb=xb: h.memset(xb[:, 0:3], 0.0), writes=[Rxb])
                    else:
                        sc.op("pool", lambda h, xb=xb, xprev=xprev: h.tensor_copy(out=xb[:, 0:3], in_=xprev[:, 512:515]),
                              reads=[Rxprev], writes=[Rxb])
                    for j in range(4):
                        sc.op("pe", lambda h, j=j, cc=cc, pc=pc, xb=xb: h.matmul(
                            psum[pc][:], lhsT=dg[:, cc * 4 + j, :], rhs=xb[:, j:j + 512], start=(j == 0), stop=(j == 3)),
                            reads=[R_dg, Rxb], writes=[R_ps[pc]])
                    sc.op("act", lambda h, pc=pc, cc=cc, g=g: h.activation(
                        out=csT[cc][:, g * 512:(g + 1) * 512], in_=psum[pc][:], func=AF.Silu),
                        reads=[R_ps[pc]], writes=[R_csT[cc]])
                    yield
            for t4 in range(4):
                pz = 2 + (t4 % 2)
                for j in range(4):
                    t = t4 * 4 + j
                    for k in range(KC):
                        sc.op("pe", lambda h, t=t, j=j, k=k, pz=pz: h.matmul(
                            psum[pz][:, j * 128:(j + 1) * 128], lhsT=hT[:, k, t * 128:(t + 1) * 128],
                            rhs=Wd[:, k, 384:512], start=(k == 0), stop=(k == KC - 1)),
                            reads=[R_Wd, R_hT[t]], writes=[R_ps[pz]])
                sc.op("act", lambda h, t4=t4, pz=pz: h.activation(
                    out=zw[:, t4 * 4:(t4 + 1) * 4, :], in_=psum[pz][:].rearrange("p (j c) -> p j c", j=4), func=AF.Silu),
                    reads=[R_ps[pz]], writes=[R_zw])
                yield
            if hd < 3:
                load_wdn(hd + 1)
            pss = psum[4]
            for qi in range(2):
                if qi == 0:
                    sc.op("act", lambda h, qi=qi: h.activation(out=sq, in_=csT[qi][:], func=AF.Square),
                          reads=[R_csT[qi]], writes=R_sqall)
                else:
                    sc.op("dve", lambda h, qi=qi: h.tensor_tensor(out=sq, in0=csT[qi][:], in1=csT[qi][:], op=ALU.mult),
                          reads=[R_csT[qi]], writes=R_sqall)
                for t in range(NT):
                    sc.op("pe", lambda h, t=t, qi=qi: h.matmul(
                        pss[:, qi * 16 + t:qi * 16 + t + 1], lhsT=sq[:, t * 128:(t + 1) * 128], rhs=onesb[:],
                        start=True, stop=True), reads=R_sqall + [R_c2], writes=[R_ps[4]])
                yield
            rqk = HS[:, 0:2, :]
            sc.op("dve", lambda h: h.tensor_scalar(out=rqk, in0=pss[:, 0:32].rearrange("p (a t) -> p a t", a=2),
                                                   scalar1=EPS, scalar2=None, op0=ALU.add),
                  reads=[R_ps[4]], writes=[R_hs])
            sc.op("act", lambda h: h.activation(out=rqk, in_=rqk, func=AF.Sqrt), reads=[R_hs], writes=[R_hs])
            sc.op("dve", lambda h: h.reciprocal(out=rqk, in_=rqk), reads=[R_hs], writes=[R_hs])
            sc.op("dve", lambda h: h.tensor_scalar(out=HS[:, 0, :], in0=HS[:, 0, :], scalar1=float(128 ** -0.5),
                                                   scalar2=None, op0=ALU.mult), reads=[R_hs], writes=[R_hs])
            sc.op("dve", lambda h, hd=hd: h.tensor_tensor(out=HS[:, 4, :], in0=HS[:, 1, :], in1=beta[:, :, hd], op=ALU.mult),
                  reads=[R_hs, R_sc], writes=[R_hs])
            sc.op("dve", lambda h, hd=hd: h.tensor_tensor(out=HS[:, 2, :], in0=HS[:, 4, :], in1=egc[:, :, hd], op=ALU.mult),
                  reads=[R_hs, R_sc], writes=[R_hs])
            sc.op("dve", lambda h, hd=hd: h.tensor_tensor(out=HS[:, 3, :], in0=HS[:, 1, :], in1=ekd[:, :, hd], op=ALU.mult),
                  reads=[R_hs, R_sc], writes=[R_hs])
            sc.op("dve", lambda h, hd=hd: h.tensor_scalar(out=HS[:, 5, :], in0=beta[:, :, hd], scalar1=-1.0, scalar2=None,
                                                          op0=ALU.mult), reads=[R_sc], writes=[R_hs])
            yield

        a_rr = [0, 0]

        def stage_A(G, hd, half):
            par = G % 2
            M2g, Dm, DTm, Pa, Pb, Ra, Rb, Ya, Yb = GMh[half]
            R_M2g, R_Dm, R_DTm, R_Pa, R_Pb, R_Ra, R_Rb, R_Ya, R_Yb = R_GMh[half]
            KT, kbg, knt, qnt = ABh[half]
            R_KT, R_kbg, R_knt, R_qnt = R_ABh[half]
            TT, WnT, qkT, QT, kd0, kd1, vbt = HB[par][half]
            R_TT, R_WnT, R_qkT, R_QT, R_kd0, R_kd1, R_vbt = R_HB[par][half]
            T0 = G * 4 + half * NH
            NC2 = NH * 128

            def abank():
                b = half * 3 + a_rr[half] % 3
                a_rr[half] += 1
                return b
            for j in range(NH):
                t = T0 + j
                bk = abank()
                ptv = psum[bk][:].bitcast(BF16)
                Rp = R_ps[bk]
                for ci in range(3):
                    sc.op("pe", lambda h, ci=ci, t=t, ptv=ptv: h.transpose(
                        ptv[:, ci * 128:(ci + 1) * 128], csT[ci][:, t * 128:(t + 1) * 128], ident[:]),
                        reads=[R_csT[ci], R_const], writes=[Rp])
                qv, kv, vv = ptv[:, 0:128], ptv[:, 128:256], ptv[:, 256:384]
                sc.op("act", lambda h, j=j, t=t, qv=qv: h.activation(out=qnt[:, j, :], in_=qv, func=AF.Copy,
                                                                     scale=HS[:, 0, t:t + 1]),
                      reads=[Rp, R_hs], writes=[R_qnt])
                sc.op("dve", lambda h, j=j, t=t, kv=kv: h.tensor_scalar(out=knt[:, j, :], in0=kv, scalar1=HS[:, 1, t:t + 1],
                                                                        scalar2=None, op0=ALU.mult),
                      reads=[Rp, R_hs], writes=[R_knt])
                sc.op("act", lambda h, j=j, t=t, kv=kv: h.activation(out=kbg[:, j, :], in_=kv, func=AF.Copy,
                                                                     scale=HS[:, 2, t:t + 1]),
                      reads=[Rp, R_hs], writes=[R_kbg])
                sc.op("dve", lambda h, j=j, t=t, kv=kv: h.tensor_scalar(out=kd0[0:64, j, :], in0=kv[0:64, :],
                                                                        scalar1=HS[0:64, 3, t:t + 1], scalar2=None,
                                                                        op0=ALU.mult),
                      reads=[Rp, R_hs], writes=[R_kd0])
                sc.op("dve", lambda h, j=j, t=t, kv=kv: h.tensor_scalar(out=kd1[64:128, j, :], in0=kv[64:128, :],
                                                                        scalar1=HS[64:128, 3, t:t + 1], scalar2=None,
                                                                        op0=ALU.mult),
                      reads=[Rp, R_hs], writes=[R_kd1])
                sc.op("act", lambda h, j=j, t=t, vv=vv: h.activation(out=vbt[:, j, :], in_=vv, func=AF.Copy,
                                                                     scale=beta[:, t, hd:hd + 1]),
                      reads=[Rp, R_sc], writes=[R_vbt])
                yield
            bk = abank()
            pkq = psum[bk][:].bitcast(BF16)
            for j in range(NH):
                sc.op("pe", lambda h, j=j: h.transpose(pkq[:, j * 128:(j + 1) * 128], knt[:, j, :], ident[:]),
                      reads=[R_knt, R_const], writes=[R_ps[bk]])
                sc.op("pe", lambda h, j=j: h.transpose(pkq[:, NC2 + j * 128:NC2 + (j + 1) * 128], qnt[:, j, :], ident[:]),
                      reads=[R_qnt, R_const], writes=[R_ps[bk]])
            sc.op("act", lambda h: h.activation(out=KT[:].rearrange("p j c -> p (j c)"), in_=pkq[:, 0:NC2], func=AF.Copy),
                  reads=[R_ps[bk]], writes=[R_KT])
            sc.op("dve", lambda h: h.tensor_copy(out=QT[:].rearrange("p j c -> p (j c)"), in_=pkq[:, NC2:2 * NC2]),
                  reads=[R_ps[bk]], writes=[R_QT])
            yield
            sc.op("dve", lambda h: h.tensor_tensor(
                out=M2g[:], in0=M2.unsqueeze(1).to_broadcast([128, NH, 128]),
                in1=graw[:, T0:T0 + NH, hd:hd + 1].to_broadcast([128, NH, 128]), op=ALU.mult),
                reads=[R_c, R_sc], writes=[R_M2g])
            bg, bt = abank(), abank()
            sc.op("pe", lambda h: h.matmul(psum[bg][:, 0:NC2], lhsT=M1, rhs=M2g[:].rearrange("p j c -> p (j c)"),
                                           start=True, stop=False), reads=[R_c, R_M2g], writes=[R_ps[bg]])
            for j in range(NH):
                sc.op("pe", lambda h, j=j: h.matmul(psum[bg][:, j * 128:(j + 1) * 128], lhsT=identf, rhs=maskS,
                                                    start=False, stop=(j == NH - 1)), reads=[R_c], writes=[R_ps[bg]])
            for j in range(NH):
                sc.op("pe", lambda h, j=j: h.matmul(psum[bt][:, j * 128:(j + 1) * 128], lhsT=M2g[:, j, :], rhs=M1,
                                                    start=True, stop=False), reads=[R_c, R_M2g], writes=[R_ps[bt]])
                sc.op("pe", lambda h, j=j: h.matmul(psum[bt][:, j * 128:(j + 1) * 128], lhsT=identf, rhs=maskTu,
                                                    start=False, stop=True), reads=[R_c], writes=[R_ps[bt]])
            sc.op("act", lambda h: h.activation(out=Dm[:].rearrange("p j c -> p (j c)"), in_=psum[bg][:, 0:NC2], func=AF.Exp),
                  reads=[R_ps[bg]], writes=[R_Dm])
            sc.op("act", lambda h: h.activation(out=DTm[:].rearrange("p j c -> p (j c)"), in_=psum[bt][:, 0:NC2], func=AF.Exp),
                  reads=[R_ps[bt]], writes=[R_DTm])
            yield
            bs, bq = abank(), abank()
            for j in range(NH):
                sc.op("pe", lambda h, j=j: h.matmul(psum[bs][:, j * 128:(j + 1) * 128], lhsT=KT[:, j, :], rhs=KT[:, j, :],
                                                    start=True, stop=True), reads=[R_KT], writes=[R_ps[bs]])
            for j in range(NH):
                sc.op("pe", lambda h, j=j: h.matmul(psum[bq][:, j * 128:(j + 1) * 128], lhsT=KT[:, j, :], rhs=QT[:, j, :],
                                                    start=True, stop=True), reads=[R_KT, R_QT], writes=[R_ps[bq]])
            for j in range(NH):
                sc.op("dve", lambda h, j=j: h.scalar_tensor_tensor(
                    out=Pa[:, j, :], in0=psum[bs][:, j * 128:(j + 1) * 128], scalar=HS[:, 5, T0 + j:T0 + j + 1],
                    in1=Dm[:, j, :], op0=ALU.mult, op1=ALU.mult), reads=[R_ps[bs], R_Dm, R_hs], writes=[R_Pa])
            sc.op("dve", lambda h: h.tensor_tensor(out=qkT[:].rearrange("p j c -> p (j c)"), in0=psum[bq][:, 0:NC2],
                                                   in1=DTm[:].rearrange("p j c -> p (j c)"), op=ALU.mult),
                  reads=[R_ps[bq], R_DTm], writes=[R_qkT])
            yield
            br = abank()
            for j in range(NH):
                sc.op("pe", lambda h, j=j: h.transpose(psum[br][:, j * 128:(j + 1) * 128], Pa[:, j, :], identf),
                      reads=[R_Pa, R_c], writes=[R_ps[br]])
            sc.op("act", lambda h: h.activation(out=Ra[:].rearrange("p j c -> p (j c)"), in_=psum[br][:, 0:NC2], func=AF.Copy),
                  reads=[R_ps[br]], writes=[R_Ra])
            sc.op("dve", lambda h: h.tensor_tensor(out=Ya[:], in0=psum[br][:, 0:NC2].rearrange("p (j c) -> p j c", j=NH),
                                                   in1=identf.unsqueeze(1).to_broadcast([128, NH, 128]),
                                                   op=ALU.add), reads=[R_ps[br], R_c], writes=[R_Ya])
            yield
            Pc, Pn, Rc, Rn, Yc, Yn = Pa, Pb, Ra, Rb, Ya, Yb
            RPc, RPn, RRc, RRn, RYc, RYn = R_Pa, R_Pb, R_Ra, R_Rb, R_Ya, R_Yb
            NL = 5
            for lev in range(1, NL + 1):
                bp = abank()
                for j in range(NH):
                    sc.op("pe", lambda h, j=j, Rc=Rc, Pc=Pc, bp=bp: h.matmul(psum[bp][:, j * 128:(j + 1) * 128], lhsT=Rc[:, j, :],
                                                                             rhs=Pc[:, j, :], start=True, stop=True),
                          reads=[RRc, RPc], writes=[R_ps[bp]])
                if lev < NL:
                    brr = abank()
                    for j in range(NH):
                        sc.op("pe", lambda h, j=j, Rc=Rc, Pc=Pc, brr=brr: h.matmul(psum[brr][:, j * 128:(j + 1) * 128], lhsT=Pc[:, j, :],
                                                                                   rhs=Rc[:, j, :], start=True, stop=True),
                              reads=[RRc, RPc], writes=[R_ps[brr]])
                sc.op("act", lambda h, Pn=Pn, bp=bp: h.activation(out=Pn[:].rearrange("p j c -> p (j c)"), in_=psum[bp][:, 0:NC2], func=AF.Copy),
                      reads=[R_ps[bp]], writes=[RPn])
                if lev < NL:
                    sc.op("act", lambda h, Rn=Rn, brr=brr: h.activation(out=Rn[:].rearrange("p j c -> p (j c)"), in_=psum[brr][:, 0:NC2],
                                                                        func=AF.Copy), reads=[R_ps[brr]], writes=[RRn])
                yield
                by = abank()
                for j in range(NH):
                    sc.op("pe", lambda h, j=j, Pn=Pn, Yc=Yc, by=by: h.matmul(psum[by][:, j * 128:(j + 1) * 128], lhsT=Pn[:, j, :],
                                                                             rhs=Yc[:, j, :], start=True, stop=True),
                          reads=[RPn, RYc], writes=[R_ps[by]])
                if lev < NL:
                    sc.op("dve", lambda h, Yn=Yn, Yc=Yc, by=by: h.tensor_tensor(out=Yn[:].rearrange("p j c -> p (j c)"), in0=psum[by][:, 0:NC2],
                                                                                in1=Yc[:].rearrange("p j c -> p (j c)"), op=ALU.add),
                          reads=[R_ps[by], RYc], writes=[RYn])
                else:
                    sc.op("dve", lambda h, Yc=Yc, by=by: h.tensor_tensor(out=TT[:].rearrange("p j c -> p (j c)"), in0=psum[by][:, 0:NC2],
                                                                         in1=Yc[:].rearrange("p j c -> p (j c)"), op=ALU.add),
                          reads=[R_ps[by], RYc], writes=[R_TT])
                Pc, Pn, Rc, Rn, Yc, Yn = Pn, Pc, Rn, Rc, Yn, Yc
                RPc, RPn, RRc, RRn, RYc, RYn = RPn, RPc, RRn, RRc, RYn, RYc
                yield
            bw = abank()
            for j in range(NH):
                sc.op("pe", lambda h, j=j: h.matmul(psum[bw][:, j * 128:(j + 1) * 128], lhsT=kbg[:, j, :], rhs=TT[:, j, :],
                                                    start=True, stop=True), reads=[R_kbg, R_TT], writes=[R_ps[bw]])
            sc.op("act", lambda h: h.activation(out=WnT[:].rearrange("p j c -> p (j c)"), in_=psum[bw][:, 0:NC2], func=AF.Copy,
                                                scale=-1.0), reads=[R_ps[bw]], writes=[R_WnT])
            yield

        def stage_B(G, hd):
            zw = zw2[hd % 2]
            R_zw = R_zw2[hd % 2]
            if G == 0:
                sc.op("pool", lambda h: h.memset(Sf[:], 0.0), writes=[R_S])
                sc.op("pool", lambda h: h.memset(Sb[:], 0.0), writes=[R_S])
            par = G % 2
            T0 = G * 4
            pA, pB = psum[7], psum[6]
            pC = psum[7][:, 256:384]
            RA, RB, RC = R_ps[7], R_ps[6], R_ps[7]
            def tile_steps(j4):
                t = T0 + j4
                half, j = j4 // NH, j4 % NH
                TT, WnT, qkT, QT, kd0, kd1, vbt = HB[par][half]
                R_TT, R_WnT, R_qkT, R_QT, R_kd0, R_kd1, R_vbt = R_HB[par][half]
                for c in range(2):
                    kd = kd0 if c == 0 else kd1
                    Rkd = R_kd0 if c == 0 else R_kd1
                    sc.op("pe", lambda h, j=j, c=c: h.matmul(pA[:, c * 128:(c + 1) * 128], lhsT=TT[:, j, :], rhs=vbt[:, j, :],
                                                             start=True, stop=False),
                          reads=[R_TT, R_vbt], writes=[RA])
                    sc.op("pe", lambda h, j=j, c=c: h.matmul(pA[:, c * 128:(c + 1) * 128], lhsT=WnT[:, j, :], rhs=Sb[:],
                                                             start=False, stop=True),
                          reads=[R_WnT, R_S], writes=[RA])
                    if c == 0:
                        sc.op("dve", lambda h: h.tensor_copy(out=vnew[:], in_=pA[:, 0:128]),
                              reads=[RA], writes=[R_vnew])
                    else:
                        sc.op("dve", lambda h: h.tensor_copy(out=vnew[64:128, :], in_=pA[64:128, 128:256]),
                              reads=[RA], writes=[R_vnew])
                    yield
                    sc.op("pe", lambda h, j=j, c=c: h.matmul(pB[:, c * 128:(c + 1) * 128], lhsT=QT[:, j, :], rhs=Sb[:],
                                                             start=True, stop=True),
                          reads=[R_QT, R_S], writes=[RB])
                    sc.op("pe", lambda h, j=j, kd=kd: h.matmul(pC, lhsT=kd[:, j, :], rhs=vnew[:],
                                                               start=True, stop=True),
                          reads=[Rkd, R_vnew], writes=[RC])
                    if c == 1:
                        sc.op("pe", lambda h, j=j: h.matmul(pB[:, 256:384], lhsT=qkT[:, j, :], rhs=vnew[:],
                                                            start=True, stop=True),
                              reads=[R_qkT, R_vnew], writes=[RB])
                    sc.op("dve", lambda h, c=c, t=t: h.scalar_tensor_tensor(
                        out=Sb[:], in0=Sf[:], scalar=egl[:, c, t, hd:hd + 1], in1=pC, op0=ALU.mult, op1=ALU.add),
                        reads=[R_S, RC, R_sc], writes=[R_S])
                    sc.op("dve", lambda h, c=c, t=t: h.scalar_tensor_tensor(
                        out=Sf[:], in0=Sf[:], scalar=egl[:, c, t, hd:hd + 1], in1=pC, op0=ALU.mult, op1=ALU.add),
                        reads=[R_S, RC, R_sc], writes=[R_S])
                    yield
                p2 = p2s[j4 % 2]
                Rp2 = R_p2s[j4 % 2]
                sc.op("act", lambda h, p2=p2: h.activation(out=p2[:], in_=pB[:, 256:384], func=AF.Copy),
                      reads=[RB], writes=[Rp2])
                for c in range(2):
                    rs = slice(64 * c, 64 * c + 64)
                    sc.op("dve", lambda h, c=c, rs=rs, j4=j4, t=t, p2=p2: h.scalar_tensor_tensor(
                        out=osb[rs, j4, :], in0=pB[rs, c * 128:(c + 1) * 128], scalar=egc[rs, t, hd:hd + 1],
                        in1=p2[rs, :], op0=ALU.mult, op1=ALU.add), reads=[RB, Rp2, R_sc], writes=[R_osb])
                sc.op("act", lambda h, j4=j4: h.activation(out=junk[:], in_=osb[:, j4, :], func=AF.Square,
                                                           accum_out=oss[:, j4:j4 + 1]), reads=[R_osb], writes=[R_oss])
                yield
            for j4 in range(4):
                yield from tile_steps(j4)
            sc.op("dve", lambda h: h.tensor_scalar(out=oss[:, 4:8], in0=oss[:, 0:4], scalar1=1.0 / 128, scalar2=EPS,
                                                   op0=ALU.mult, op1=ALU.add), reads=[R_oss], writes=[R_oss])
            sc.op("act", lambda h: h.activation(out=oss[:, 4:8], in_=oss[:, 4:8], func=AF.Ln), reads=[R_oss], writes=[R_oss])
            sc.op("act", lambda h: h.activation(out=oss[:, 4:8], in_=oss[:, 4:8], func=AF.Exp, scale=-0.5),
                  reads=[R_oss], writes=[R_oss])
            pO = pB[:].bitcast(BF16)
            for j in range(4):
                t = T0 + j
                sc.op("dve", lambda h, j=j, t=t: h.scalar_tensor_tensor(
                    out=otk[:, j, :], in0=osb[:, j, :], scalar=oss[:, 4 + j:5 + j], in1=zw[:, t, :],
                    op0=ALU.mult, op1=ALU.mult), reads=[R_osb, R_oss, R_zw], writes=[R_otk])
                sc.op("pe", lambda h, j=j: h.transpose(pO[:, j * 128:(j + 1) * 128], otk[:, j, :], ident[:]),
                      reads=[R_otk, R_const], writes=[RB])
            sc.op("act", lambda h: h.activation(out=catD[:, hd, T0 * 128:(T0 + 4) * 128], in_=pO[:, 0:512],
                                                func=AF.Copy), reads=[RB], writes=[R_catD[hd]])
            sc.op("dve", lambda h: h.memset(oss[:, 0:4], 0.0), reads=[R_oss], writes=[R_oss])
            yield

        run_interleaved([stage_C(0)])
        for hd in range(4):
            run_interleaved([stage_A(0, hd, 0), stage_A(0, hd, 1)])
            for G in range(4):
                tasks = [stage_B(G, hd)]
                if G < 3:
                    tasks.append(stage_A(G + 1, hd, 0))
                    tasks.append(stage_A(G + 1, hd, 1))
                elif hd < 3:
                    tasks.append(stage_C(hd + 1))
                run_interleaved(tasks)
        end_phase()
        e.close()


    def postnorm_residual(t, pa, pb, wpost, R_wpost, tmpb, R_tmpb, ssb, R_ssb, junkp):
        sc.op("act", lambda h: h.activation(out=junkp[:], in_=psum[pa][:], func=AF.Square, accum_out=ssb[:, 0:1]),
              reads=[R_ps[pa]], writes=[R_ssb])
        sc.op("act", lambda h: h.activation(out=junkp[:], in_=psum[pb][:], func=AF.Square, accum_out=ssb[:, 1:2]),
              reads=[R_ps[pb]], writes=[R_ssb])
        sc.op("dve", lambda h: h.tensor_tensor(out=ssb[:, 2:3], in0=ssb[:, 0:1], in1=ssb[:, 1:2], op=ALU.add),
              reads=[R_ssb], writes=[R_ssb])
        sc.op("dve", lambda h: h.tensor_scalar(out=ssb[:, 2:3], in0=ssb[:, 2:3], scalar1=1.0 / D, scalar2=EPS,
                                               op0=ALU.mult, op1=ALU.add), reads=[R_ssb], writes=[R_ssb])
        sc.op("act", lambda h: h.activation(out=ssb[:, 2:3], in_=ssb[:, 2:3], func=AF.Sqrt), reads=[R_ssb], writes=[R_ssb])
        sc.op("dve", lambda h: h.reciprocal(out=ssb[:, 3:4], in_=ssb[:, 2:3]), reads=[R_ssb], writes=[R_ssb])
        sc.op("dve", lambda h: h.tensor_tensor(out=tmpb[:, 0:512], in0=psum[pa][:], in1=wpost[:, 0:512], op=ALU.mult),
              reads=[R_ps[pa], R_wpost], writes=[R_tmpb])
        sc.op("dve", lambda h: h.tensor_tensor(out=tmpb[:, 512:1024], in0=psum[pb][:], in1=wpost[:, 512:1024], op=ALU.mult),
              reads=[R_ps[pb], R_wpost], writes=[R_tmpb])
        sc.op("dve", lambda h: h.scalar_tensor_tensor(out=x_sb[:, t, :], in0=tmpb[:], scalar=ssb[:, 3:4], in1=x_sb[:, t, :],
                                                      op0=ALU.mult, op1=ALU.add),
              reads=[R_tmpb, R_ssb, R_x[t]], writes=[R_x[t]])
        sc.op("dve", lambda h: h.memset(ssb[:, 0:2], 0.0), reads=[R_ssb], writes=[R_ssb])

    def outproj_phase(l, catA, R_catA, catD, R_catD):
        e = ExitStack()
        Wo = sb(e, "w_o", [128, KC, D], BF16)
        R_Wo = Res("wo")
        phase_load(Wo[:], w_o_d[l], R_Wo, eng="pool")
        wpost = sb(e, "wpost", [128, D], F32)
        R_wpost = Res("wpost")
        phase_load(wpost[:], npost_d[:, l, 0, :], R_wpost)
        nwc = sb(e, "nwc", [128, 1], F32)
        R_nwc = Res("nwc")
        phase_load(nwc[:], dn_nwT_d[l], R_nwc)
        sc.op("dve", lambda h: h.tensor_scalar(out=Wo[:, 4:8, :], in0=Wo[:, 4:8, :], scalar1=nwc[:, 0:1], scalar2=None,
                                               op0=ALU.mult), reads=[R_Wo, R_nwc], writes=[R_Wo])
        tmpb = [sb(e, "tmpb%d" % i, [128, D], F32) for i in range(2)]
        R_tmpb = [Res("tmpb%d" % i) for i in range(2)]
        ssb = [sb(e, "ssb%d" % i, [128, 4], F32) for i in range(2)]
        R_ssb = [Res("ssb%d" % i) for i in range(2)]
        junkp = sb(e, "junkp", [128, 512], BF16)
        for i in range(2):
            sc.op("pool", lambda h, i=i: h.memset(ssb[i][:], 0.0), writes=[R_ssb[i]])
        for t in range(NT):
            pa, pb = (0, 1) if t % 2 == 0 else (2, 3)
            for n, pk in ((0, pa), (1, pb)):
                for k in range(KC):
                    src = catA if k < 4 else catD
                    Rsrc = R_catA[k] if k < 4 else R_catD[k - 4]
                    sc.op("pe", lambda h, k=k, n=n, pk=pk, t=t, src=src: h.matmul(
                        psum[pk][:], lhsT=src[:, k % 4, t * 128:(t + 1) * 128], rhs=Wo[:, k, n * 512:(n + 1) * 512],
                        start=(k == 0), stop=(k == KC - 1)), reads=[Rsrc, R_Wo], writes=[R_ps[pk]])
            postnorm_residual(t, pa, pb, wpost, R_wpost, tmpb[t % 2], R_tmpb[t % 2], ssb[t % 2], R_ssb[t % 2], junkp)
        end_phase()
        e.close()

    def ffn_phase(l):
        e = ExitStack()
        fcw = sb(e, "fcw", [128, 2 * NFC, 4], F32)
        R_fcw = Res("fcw")
        phase_load(fcw[:], fcw_d[:, l], R_fcw)
        halo = sb(e, "halo", [128, 2 * NFC, 2], F32)
        R_halo = Res("halo")
        hTh = sb(e, "hTh", [128, KC, 1024], BF16)
        R_hTh = [Res("hTh%d" % i) for i in range(8)]
        actT = sb(e, "actT", [128, NFC, 1024], BF16)
        R_act = [Res("act%d" % i) for i in range(NFC)]
        W2 = sb(e, "w_f2", [128, NFC, D], BF16)
        R_W2 = Res("w2")
        q_w2 = sc.dma_sem("wf2_%d" % l)
        q_w1 = [sc.dma_sem("wf1_%d_%d" % (l, i)) for i in range(2)]
        for hf in range(2):
            if hf == 0:
                sc.dma("pool", q_w2, lambda h: h.dma_start(out=W2[:, 0:11, :], in_=w_f2_d[l, :, 0:11, :]), writes=[R_W2])
                sc.dma("pool", q_w2, lambda h: h.dma_start(out=W2[:, 11:22, :], in_=w_f2_d[l, :, 11:22, :]), writes=[R_W2])
                R_W2.w = (q_w2, sc.cnt[q_w2])
            eu = ExitStack()
            W1 = [sb(eu, "w_f1_%d" % i, [128, KC, 256], BF16) for i in range(2)]
            R_W1 = [Res("w1_%d" % i) for i in range(2)]
            xpd = [[sb(eu, "fxp%d_%d" % (p, i), [128, 2 + 512], F32) for i in range(2)] for p in range(2)]
            R_xpd = [[Res("fxp%d_%d" % (p, i)) for i in range(2)] for p in range(2)]
            ycv = [[sb(eu, "fy%d_%d" % (p, i), [128, 512], F32) for i in range(2)] for p in range(2)]
            R_ycv = [[Res("fy%d_%d" % (p, i)) for i in range(2)] for p in range(2)]
            ggl = [sb(eu, "ggl%d" % i, [128, 512], F32) for i in range(2)]
            R_ggl = [Res("ggl%d" % i) for i in range(2)]

            def load_w1(jc):
                i = jc % 2
                sc.dma("pool", q_w1[i], lambda h, i=i, jc=jc: h.dma_start(out=W1[i][:], in_=w_f1_d[l, jc]), writes=[R_W1[i]])
            load_w1(0)
            prenorm(l, 1, hTh, list(range(hf * 8, hf * 8 + 8)), R_hTh)
            pend_glu = [None]

            def emit_glu(bi, jc, gi):
                gb, Rgb = ggl[bi], R_ggl[bi]
                sc.op("act", lambda h: h.activation(out=gb[:], in_=ycv[0][bi][:], func=AF.Gelu_apprx_tanh),
                      reads=[R_ycv[0][bi]], writes=[Rgb])
                sc.op("pool", lambda h: h.tensor_tensor(
                    out=actT[:, jc, gi * 512:(gi + 1) * 512], in0=gb[:], in1=ycv[1][bi][:], op=ALU.mult),
                    reads=[Rgb, R_ycv[1][bi]], writes=[R_act[jc]])

            for jc in range(NFC):
                if jc + 1 < NFC:
                    load_w1(jc + 1)
                Wc = W1[jc % 2]
                RWc = R_W1[jc % 2]
                for gi in range(2):
                    g = hf * 2 + gi
                    bi = (jc * 2 + gi) % 2
                    for part in range(2):
                        ch = part * NFC + jc
                        pk = part * 4 + (jc * 2 + gi) % 4
                        xb, Rxb = xpd[part][bi], R_xpd[part][bi]
                        xprev, Rxprev = xpd[part][1 - bi], R_xpd[part][1 - bi]
                        yb, Ryb = ycv[part][bi], R_ycv[part][bi]
                        for k in range(KC):
                            sc.op("pe", lambda h, k=k, part=part, pk=pk, gi=gi, Wc=Wc: h.matmul(
                                psum[pk][:], lhsT=Wc[:, k, part * 128:(part + 1) * 128], rhs=hTh[:, k, gi * 512:(gi + 1) * 512],
                                start=(k == 0), stop=(k == KC - 1)), reads=[RWc] + R_hTh[gi * 4:(gi + 1) * 4], writes=[R_ps[pk]])
                        sc.op("act", lambda h, pk=pk, xb=xb: h.activation(out=xb[:, 2:514], in_=psum[pk][:], func=AF.Copy),
                              reads=[R_ps[pk]], writes=[Rxb])
                        if g == 0:
                            sc.op("pool", lambda h, xb=xb: h.memset(xb[:, 0:2], 0.0), writes=[Rxb])
                        elif gi == 0:
                            sc.op("pool", lambda h, xb=xb, ch=ch: h.tensor_copy(out=xb[:, 0:2], in_=halo[:, ch, :]),
                                  reads=[R_halo], writes=[Rxb])
                        else:
                            sc.op("pool", lambda h, xb=xb, xprev=xprev: h.tensor_copy(out=xb[:, 0:2], in_=xprev[:, 512:514]),
                                  reads=[Rxprev], writes=[Rxb])
                        if hf == 0 and gi == 1:
                            sc.op("pool", lambda h, xb=xb, ch=ch: h.tensor_copy(out=halo[:, ch, :], in_=xb[:, 512:514]),
                                  reads=[Rxb], writes=[R_halo])
                        sc.op("act", lambda h, pk=pk, yb=yb, ch=ch: h.activation(
                            out=yb[:], in_=psum[pk][:], func=AF.Identity, scale=fcw[:, ch, 2:3], bias=fcw[:, ch, 3:4]),
                            reads=[R_ps[pk], R_fcw], writes=[Ryb])
                        for j in range(2):
                            sc.op("dve", lambda h, j=j, yb=yb, xb=xb, ch=ch: h.scalar_tensor_tensor(
                                out=yb[:], in0=xb[:, j:j + 512], scalar=fcw[:, ch, j:j + 1], in1=yb[:],
                                op0=ALU.mult, op1=ALU.add), reads=[Rxb, Ryb, R_fcw], writes=[Ryb])
                    if pend_glu[0] is not None:
                        emit_glu(*pend_glu[0])
                    pend_glu[0] = (bi, jc, gi)
            emit_glu(*pend_glu[0])
            pend_glu[0] = None
            sc.barrier()
            sc.emit_all()
            eu.close()
            ed = ExitStack()
            wpost = sb(ed, "wpost", [128, D], F32)
            R_wpost = Res("wpost")
            phase_load(wpost[:], npost_d[:, l, 1, :], R_wpost)
            tmpb = [sb(ed, "tmpb%d" % i, [128, D], F32) for i in range(2)]
            R_tmpb = [Res("tmpb%d" % i) for i in range(2)]
            ssb = [sb(ed, "ssb%d" % i, [128, 4], F32) for i in range(2)]
            R_ssb = [Res("ssb%d" % i) for i in range(2)]
            junkp = sb(ed, "junkp", [128, 512], BF16)
            for i in range(2):
                sc.op("pool", lambda h, i=i: h.memset(ssb[i][:], 0.0), writes=[R_ssb[i]])
            for tl in range(8):
                t = hf * 8 + tl
                pa, pb = (0, 1) if tl % 2 == 0 else (2, 3)
                for n, pk in ((0, pa), (1, pb)):
                    for jc in range(NFC):
                        sc.op("pe", lambda h, jc=jc, n=n, pk=pk, tl=tl: h.matmul(
                            psum[pk][:], lhsT=actT[:, jc, tl * 128:(tl + 1) * 128], rhs=W2[:, jc, n * 512:(n + 1) * 512],
                            start=(jc == 0), stop=(jc == NFC - 1)), reads=[R_act[jc], R_W2], writes=[R_ps[pk]])
                postnorm_residual(t, pa, pb, wpost, R_wpost, tmpb[tl % 2], R_tmpb[tl % 2], ssb[tl % 2], R_ssb[tl % 2], junkp)
                if l == depth - 1:
                    sc.dma("sp", q_out, lambda h, t=t: h.dma_start(out=y_d[t * 128:(t + 1) * 128, :], in_=x_sb[:, t, :]),
                           reads=[R_x[t]])
            sc.barrier()
            sc.emit_all()
            ed.close()
        e.close()

    for l in range(depth):
        e_mix = ExitStack()
        hT = sb(e_mix, "hT", [128, KC, S], BF16)
        hT_box[0] = hT
        R_hT[:] = [Res("hT%d" % t) for t in range(NT)]
        catA = sb(e_mix, "catA", [128, 4, S], BF16)
        R_catA = [Res("catA%d" % c) for c in range(4)]
        attention_phase(l, catA, R_catA, pre=lambda: prenorm(l, 0, hT, list(range(NT)), R_hT))
        if dbg and l == 0:
            sc.dma("sp", q_out, lambda h: h.dma_start(out=dbg_d["catA"], in_=catA[:]), reads=R_catA)
            end_phase()
        catD = sb(e_mix, "catD", [128, 4, S], BF16)
        R_catD = [Res("catD%d" % c) for c in range(4)]
        if stage >= 2:
            deltanet_phase(l, catD, R_catD)
        if dbg and l == 0:
            sc.dma("sp", q_out, lambda h: h.dma_start(out=dbg_d["catD"], in_=catD[:]), reads=R_catD)
            end_phase()
        if stage >= 3:
            outproj_phase(l, catA, R_catA, catD, R_catD)
        e_mix.close()
        if dbg and l == 0:
            sc.dma("sp", q_out, lambda h: h.dma_start(out=dbg_d["xmid"].rearrange("(t p) d -> p t d", p=128), in_=x_sb[:]),
                   reads=R_x)
            end_phase()
        if stage >= 4:
            ffn_phase(l)

    if stage < 4:
        yv = y_d.rearrange("(t p) d -> p t d", p=128)
        sc.dma("sp", q_out, lambda h: h.dma_start(out=yv, in_=x_sb[:]), reads=R_x)
    sc.wait_all("sp", [(q_out, sc.cnt[q_out])])
    sc.emit_all()
    es.close()
    return nc, sc


def host_prep(inputs, depth=DEPTH):
    f32 = np.float32
    w_in = np.asarray(inputs["w_in"], f32)
    out = {}
    w_att = np.empty((depth, 4, 128, KC, 640), f32)
    for l in range(depth):
        Wl = w_in[l].reshape(KC, 128, -1)
        for hp in range(4):
            cols = []
            for base in (0, 512):
                c = np.arange(base + hp * 128, base + hp * 128 + 128)
                cols.append(c)
                cs = c.reshape(2, 2, 32)[:, ::-1, :].reshape(-1)
                cols.append(cs)
            cols.append(np.arange(1024 + hp * 128, 1024 + hp * 128 + 128))
            cols = np.concatenate(cols)
            w_att[l, hp] = Wl[:, :, cols].transpose(1, 0, 2)
    out["w_att"] = w_att
    out["ident"] = np.eye(128, dtype=f32)
    inv = 1.0 / (10000.0 ** (np.arange(0, 64, 2, dtype=np.float32) / 64.0))
    ang = np.arange(S, dtype=np.float32)[None, :] * inv[:, None].astype(np.float32)
    cos = np.cos(ang).astype(f32)
    sin = np.sin(ang).astype(f32)
    rc = np.empty((128, S), f32)
    rs = np.empty((128, S), f32)
    for p in range(128):
        rc[p] = cos[p % 32]
        rs[p] = -sin[p % 32] if (p % 64) < 32 else sin[p % 32]
    out["rope_c"] = rc
    out["rope_s"] = rs
    c = np.arange(128)[:, None]
    a = np.arange(128)[None, :]
    Lm = (c >= a).astype(f32)
    Um = (c <= a).astype(f32)
    out["amask"] = np.concatenate([Lm, Um, Lm, Um, Lm, Um, Lm, Um], axis=1)
    npre = np.empty((128, depth, 2, KC), f32)
    for l in range(depth):
        npre[:, l, 0, :] = np.asarray(inputs["norm_pre_mix"], f32)[l].reshape(KC, 128).T
        npre[:, l, 1, :] = np.asarray(inputs["norm_pre_ffn"], f32)[l].reshape(KC, 128).T
    out["npre"] = npre
    w_dn = np.empty((depth, 4, 128, KC, 512), f32)
    w_ba = np.empty((depth, 128, KC, 8), f32)
    dn_cw = np.empty((128, depth, 4, 3, 4), f32)
    dn_hp = np.empty((128, depth, 2, 4), f32)
    dn_nw = np.empty((depth, 128, 1), f32)
    cwl = np.asarray(inputs["dn_conv_w"], f32)
    for l in range(depth):
        Wl = w_in[l].reshape(KC, 128, -1)
        for hd in range(4):
            cols = np.concatenate([np.arange(b + hd * 128, b + hd * 128 + 128) for b in (1536, 2048, 2560, 3072)])
            w_dn[l, hd] = Wl[:, :, cols].transpose(1, 0, 2)
            for cc in range(3):
                ch = cc * 512 + hd * 128 + np.arange(128)
                dn_cw[:, l, hd, cc, :] = cwl[l][:, ch].T
        w_ba[l] = Wl[:, :, 3584:3592].transpose(1, 0, 2)
        dn_hp[:, l, 0, :] = np.asarray(inputs["dn_a_log"], f32)[l][None, :]
        dn_hp[:, l, 1, :] = np.asarray(inputs["dn_dt_bias"], f32)[l][None, :]
        dn_nw[l, :, 0] = np.asarray(inputs["dn_norm_w"], f32)[l]
    out["w_dn"] = w_dn
    out["w_ba"] = w_ba
    out["dn_cw"] = dn_cw
    out["dn_hp"] = dn_hp
    out["dn_nwT"] = dn_nw
    ti = np.arange(128)[:, None]
    tj = np.arange(128)[None, :]
    same = (ti // 64) == (tj // 64)
    dn_c = np.zeros((128, 8, 128), f32)
    dn_c[:, 0] = (same & (ti <= tj))
    dn_c[:, 1] = (same & (ti > tj))
    dn_c[:, 2] = np.where(same & (ti > tj), 0.0, -30000.0)
    dn_c[:, 3] = np.where(same & (tj >= ti), 0.0, -30000.0)
    dn_c[:, 4] = same
    dn_c[:, 5] = (ti < 64) & (tj >= 0)
    dn_c[:, 6] = (ti >= 64) & (tj >= 0)
    dn_c[:, 7] = np.eye(128)
    out["dn_c"] = dn_c
    w_out = np.asarray(inputs["w_out"], f32)
    out["w_o"] = np.ascontiguousarray(w_out.reshape(depth, KC, 128, D).transpose(0, 2, 1, 3))
    npost = np.empty((128, depth, 2, D), f32)
    npost[:, :, 0, :] = np.asarray(inputs["norm_post_mix"], f32)[None]
    npost[:, :, 1, :] = np.asarray(inputs["norm_post_ffn"], f32)[None]
    out["npost"] = npost
    fw1 = np.asarray(inputs["ffn_w_in"], f32).reshape(depth, KC, 128, 2, NFC, 128)
    out["w_f1"] = np.ascontiguousarray(fw1.transpose(0, 4, 2, 1, 3, 5).reshape(depth, NFC, 128, KC, 256))
    fw2 = np.asarray(inputs["ffn_w_out"], f32).reshape(depth, NFC, 128, D)
    out["w_f2"] = np.ascontiguousarray(fw2.transpose(0, 2, 1, 3))
    fcw = np.empty((128, depth, 2 * NFC, 4), f32)
    cwf = np.asarray(inputs["ffn_conv_w"], f32).reshape(depth, 3, 2 * NFC, 128)
    cbf = np.asarray(inputs["ffn_conv_b"], f32).reshape(depth, 2 * NFC, 128)
    fcw[:, :, :, 0:3] = cwf.transpose(3, 0, 2, 1)
    fcw[:, :, :, 3] = cbf.transpose(2, 0, 1)
    out["fcw"] = fcw
    return out


_CACHE = {}


def kernel(**inputs):
    x = np.ascontiguousarray(np.asarray(inputs["x"], np.float32))
    shared = host_prep(inputs)
    if "nc" not in _CACHE:
        _CACHE["nc"] = build_program()[0]
    nc = _CACHE["nc"]
    in_maps = []
    for c in range(N_CORES):
        m = dict(shared)
        m["x"] = x[c]
        in_maps.append(m)
    res = run_bass_kernel_spmd(nc, in_maps, core_ids=list(range(N_CORES)))
    return np.stack([np.asarray(r["y"], np.float32) for r in res.results], axis=0)
```

```python
import numpy as np
import ml_dtypes
from contextlib import ExitStack
import concourse.bass as bass
import concourse.mybir as mybir
from concourse.bass_utils import run_bass_kernel_spmd

F32 = mybir.dt.float32
BF16 = mybir.dt.bfloat16
AF = mybir.ActivationFunctionType
ALU = mybir.AluOpType

S = 2048
D = 1024
NT = 16
KC = 8
DEPTH = 2
DFF = 2816
NFC = 22
EPS = 1e-6
N_CORES = 8


class Res:
    __slots__ = ("w", "r", "excl", "name")

    def __init__(self, name="", excl=False):
        self.w = None
        self.r = []
        self.excl = excl
        self.name = name


def run_interleaved(gens):
    active = list(gens)
    while active:
        for g in list(active):
            try:
                next(g)
            except StopIteration:
                active.remove(g)


class Sched:
    ENG = ("pe", "act", "dve", "pool", "sp")

    def __init__(self, nc, es):
        self.nc = nc
        self.es = es
        self.sems = {}
        self.cnt = {}
        for e in self.ENG:
            self.sems[e] = es.enter_context(nc.semaphore("sem_" + e))
            self.cnt[e] = 0
        self.waited = {e: {} for e in self.ENG}
        self.stream = {e: [] for e in self.ENG}
        self.nwaits = 0

    def dma_sem(self, name):
        self.sems[name] = self.es.enter_context(self.nc.semaphore("dq_" + name))
        self.cnt[name] = 0
        return name

    def _collect(self, eng, reads, writes):
        need = {}

        def add(tok, kind):
            if tok is None:
                return
            key, val = tok
            if key == eng:
                if eng in ("pe", "sp"):
                    return
            if self.waited[eng].get(key, 0) >= val:
                return
            if need.get(key, 0) < val:
                need[key] = val

        for res in reads:
            add(res.w, "raw")
            if res.excl:
                for t in res.r:
                    if t[0] != eng:
                        add(t, "war")
        for res in writes:
            add(res.w, "waw")
            for t in res.r:
                add(t, "war")
        for k, v in need.items():
            self.waited[eng][k] = v
        return list(need.items())

    def _commit(self, tok, reads, writes):
        for res in reads:
            res.r = [t for t in res.r if t[0] != tok[0]] + [tok]
        for res in writes:
            res.w = tok
            res.r = []

    def op(self, eng, fn, reads=(), writes=()):
        waits = self._collect(eng, reads, writes)
        self.cnt[eng] += 1
        tok = (eng, self.cnt[eng])
        sems = self.sems
        semh = sems[eng]
        self.nwaits += len(waits)

        def emit(h, waits=waits, fn=fn, semh=semh):
            for k, v in waits:
                h.wait_ge(sems[k], v)
            fn(h).then_inc(semh, 1)

        self.stream[eng].append(emit)
        self._commit(tok, reads, writes)
        return tok

    def dma(self, eng, semkey, fn, reads=(), writes=()):
        waits = self._collect(eng, reads, writes)
        self.cnt[semkey] += 16
        tok = (semkey, self.cnt[semkey])
        sems = self.sems
        semh = sems[semkey]

        def emit(h, waits=waits, fn=fn, semh=semh):
            for k, v in waits:
                h.wait_ge(sems[k], v)
            fn(h).then_inc(semh, 16)

        self.stream[eng].append(emit)
        self._commit(tok, reads, writes)
        return tok

    def wait_all(self, eng, toks):
        need = {}
        for key, val in toks:
            if need.get(key, 0) < val:
                need[key] = val
        waits = list(need.items())
        sems = self.sems

        def emit(h, waits=waits):
            for k, v in waits:
                h.wait_ge(sems[k], v)

        self.stream[eng].append(emit)

    def barrier(self, wait_dma=True):
        snap = dict(self.cnt)
        if not wait_dma:
            snap = {k: v for k, v in snap.items() if k in self.ENG}
        for e in self.ENG:
            waits = [(k, v) for k, v in snap.items() if v > 0 and k != e and self.waited[e].get(k, 0) < v]
            for k, v in waits:
                self.waited[e][k] = v
            if e in ("act", "dve", "pool") and snap[e] > 0:
                waits.append((e, snap[e]))
                self.waited[e][e] = snap[e]
            sems = self.sems

            def emit(h, waits=waits):
                for k, v in waits:
                    h.wait_ge(sems[k], v)

            self.stream[e].append(emit)

    def emit_all(self):
        nc = self.nc
        streams = self.stream
        self.stream = {e: [] for e in self.ENG}
        self._emit(streams)

    def _emit(self, streams):
        nc = self.nc
        self_stream = streams
        with nc.Block() as block:
            @block.tensor
            def _(h):
                for f in self_stream["pe"]:
                    f(h)

            @block.scalar
            def _(h):
                for f in self_stream["act"]:
                    f(h)

            @block.vector
            def _(h):
                for f in self_stream["dve"]:
                    f(h)

            @block.gpsimd
            def _(h):
                for f in self_stream["pool"]:
                    f(h)

            @block.sync
            def _(h):
                for f in self_stream["sp"]:
                    f(h)


def build_program(depth=DEPTH, stage=99, dbg=False):
    nc = bass.Bass("TRN2", target_bir_lowering=False)
    es = ExitStack()
    sc = Sched(nc, es)

    def dram_in(name, shape, dt=F32):
        return nc.dram_tensor(name, list(shape), dt, kind="ExternalInput").ap()

    def dram_out(name, shape, dt=F32):
        return nc.dram_tensor(name, list(shape), dt, kind="ExternalOutput").ap()

    uid = [0]

    def sb(e, name, shape, dt):
        uid[0] += 1
        return e.enter_context(nc.sbuf_tensor("sb%d_%s" % (uid[0], name), list(shape), dt))

    x_d = dram_in("x", [S, D])
    y_d = dram_out("y", [S, D])
    w_att_d = dram_in("w_att", [depth, 4, 128, KC, 640])
    ident_d = dram_in("ident", [128, 128])
    ropec_d = dram_in("rope_c", [128, S])
    ropes_d = dram_in("rope_s", [128, S])
    amask_d = dram_in("amask", [128, 1024])
    npre_d = dram_in("npre", [128, depth, 2, KC])
    w_dn_d = dram_in("w_dn", [depth, 4, 128, KC, 512])
    w_ba_d = dram_in("w_ba", [depth, 128, KC, 8])
    dn_cw_d = dram_in("dn_cw", [128, depth, 4, 3, 4])
    dn_hp_d = dram_in("dn_hp", [128, depth, 2, 4])
    dn_nwT_d = dram_in("dn_nwT", [depth, 128, 1])
    dn_c_d = dram_in("dn_c", [128, 8, 128])
    w_o_d = dram_in("w_o", [depth, 128, KC, D])
    npost_d = dram_in("npost", [128, depth, 2, D])
    w_f1_d = dram_in("w_f1", [depth, NFC, 128, KC, 256])
    w_f2_d = dram_in("w_f2", [depth, 128, NFC, D])
    fcw_d = dram_in("fcw", [128, depth, 2 * NFC, 4])
    dbg_d = {}
    if dbg:
        dbg_d["hT"] = dram_out("dbg_hT", [128, KC, S], BF16)
        dbg_d["catA"] = dram_out("dbg_catA", [128, 4, S], BF16)
        dbg_d["catD"] = dram_out("dbg_catD", [128, 4, S], BF16)
        dbg_d["xmid"] = dram_out("dbg_xmid", [S, D], F32)

    x_sb = sb(es, "x_sb", [128, NT, D], F32)
    hT_box = [None]
    ident = sb(es, "ident", [128, 128], BF16)
    npre = sb(es, "npre", [128, depth, 2, KC], F32)
    R_x = [Res("x%d" % t) for t in range(NT)]
    R_hT = [Res("hT%d" % t) for t in range(NT)]
    R_const = Res("const")

    psum = [es.enter_context(nc.psum_tensor("ps%d" % i, [128, 512], F32)) for i in range(8)]
    R_ps = [Res("ps%d" % i, excl=True) for i in range(8)]

    q_out = sc.dma_sem("out")
    q_ld = {"sp": [sc.dma_sem("ld%d" % i) for i in range(4)],
            "pool": [sc.dma_sem("ldp%d" % i) for i in range(3)]}
    ld_rr = {"sp": 0, "pool": 0}

    def phase_load(out_ap, in_ap, res, eng="sp"):
        qs = q_ld[eng]
        q = qs[ld_rr[eng] % len(qs)]
        ld_rr[eng] += 1
        return sc.dma(eng, q, lambda h: h.dma_start(out=out_ap, in_=in_ap), writes=[res])

    def end_phase(wait_dma=True):
        sc.barrier(wait_dma)
        sc.emit_all()

    xv = x_d.rearrange("(t p) d -> p t d", p=128)
    for t4 in range(4):
        qx = sc.dma_sem("xin%d" % t4)
        sc.dma("sp", qx, lambda h, t4=t4: h.dma_start(out=x_sb[:, t4 * 4:(t4 + 1) * 4, :], in_=xv[:, t4 * 4:(t4 + 1) * 4, :]),
               writes=R_x[t4 * 4:(t4 + 1) * 4])
    phase_load(ident[:], ident_d, R_const, eng="pool")
    phase_load(npre[:], npre_d, R_const)
    end_phase()

    def prenorm(l, which, hT, tiles, R_hTl):
        e = ExitStack()
        ss_all = sb(e, "ss_all", [128, NT], F32)
        rstd_all = sb(e, "rstd_all", [128, NT], F32)
        junk = sb(e, "junk", [128, D], BF16)
        hb = [sb(e, "hb%d" % i, [128, D], BF16) for i in range(2)]
        R_hb = [Res("hb%d" % i) for i in range(2)]
        R_ss = Res("ss")
        sc.op("dve", lambda h: h.memset(ss_all[:], 0.0), writes=[R_ss])
        for t in tiles:
            sc.op("act", lambda h, t=t: h.activation(out=junk[:], in_=x_sb[:, t, :], func=AF.Square,
                                                     accum_out=ss_all[:, t:t + 1]),
                  reads=[R_x[t]], writes=[R_ss])
        sc.op("dve", lambda h: h.tensor_scalar(out=rstd_all[:], in0=ss_all[:], scalar1=1.0 / D, scalar2=EPS,
                                               op0=ALU.mult, op1=ALU.add), reads=[R_ss], writes=[R_ss])
        sc.op("act", lambda h: h.activation(out=rstd_all[:], in_=rstd_all[:], func=AF.Sqrt),
              reads=[R_ss], writes=[R_ss])
        sc.op("dve", lambda h: h.reciprocal(out=rstd_all[:], in_=rstd_all[:]), reads=[R_ss], writes=[R_ss])
        for ti, t in enumerate(tiles):
            b = t % 2
            pb = 6 + (t % 2)
            sc.op("act", lambda h, t=t, b=b: h.activation(out=hb[b][:], in_=x_sb[:, t, :], func=AF.Copy,
                                                          scale=rstd_all[:, t:t + 1]),
                  reads=[R_x[t], R_ss], writes=[R_hb[b]])
            pst = psum[pb][:].bitcast(BF16)
            for k in range(KC):
                sc.op("pe", lambda h, k=k, b=b, pst=pst: h.transpose(pst[:, k * 128:(k + 1) * 128],
                                                                     hb[b][:, k * 128:(k + 1) * 128], ident[:]),
                      reads=[R_hb[b], R_const], writes=[R_ps[pb]])
            wv = npre[:, l, which, :].unsqueeze(2).to_broadcast([128, KC, 128])
            sc.op("dve", lambda h, ti=ti, pst=pst, wv=wv: h.tensor_tensor(
                out=hT[:, :, ti * 128:(ti + 1) * 128], in0=pst.rearrange("p (k c) -> p k c", k=KC), in1=wv, op=ALU.mult),
                reads=[R_ps[pb], R_const], writes=[R_hTl[ti]])
        end_phase(wait_dma=False)
        e.close()

    BR = ((1, 16), (4, 4), (16, 1))

    def attention_phase(l, catA, R_catA, pre=None):
        hT = hT_box[0]
        e = ExitStack()
        ropec = sb(e, "ropec", [128, S], F32)
        ropes = sb(e, "ropes", [128, S], F32)
        amask = sb(e, "amask", [128, 1024], BF16)
        R_tab = Res("tab")
        phase_load(ropec[:], ropec_d, R_tab)
        phase_load(ropes[:], ropes_d, R_tab)
        phase_load(amask[:], amask_d, R_tab, eng="pool")
        w_att = [sb(e, "w_att%d" % i, [128, KC, 640], BF16) for i in range(2)]
        R_watt = [Res("watt%d" % i) for i in range(2)]
        q_watt = [sc.dma_sem("watt%d_%d" % (l, i)) for i in range(2)]

        def load_watt(hp):
            i = hp % 2
            sc.dma("pool", q_watt[i], lambda h, i=i: h.dma_start(out=w_att[i][:], in_=w_att_d[l, hp]),
                   writes=[R_watt[i]])

        load_watt(0)
        if pre is not None:
            pre()
        qT = sb(e, "qT", [128, S], BF16)
        kTe = sb(e, "kTe", [128, S], BF16)
        kTo = sb(e, "kTo", [128, S], BF16)
        R_qT, R_kT = Res("qT"), Res("kT")
        vT = sb(e, "vT", [128, S], BF16)
        R_vT = Res("vT")
        rtmp = [sb(e, "rtmp%d" % i, [128, 2, 512], F32) for i in range(2)]
        R_rtmp = [Res("rtmp%d" % i) for i in range(2)]
        VX = sb(e, "vext", [128, 16, 2, 128], BF16)
        R_vext = Res("vext")
        pT = [sb(e, "pT%d" % i, [128, 512], BF16) for i in range(6)]
        R_pT = [Res("pT%d" % i) for i in range(6)]
        acc = sb(e, "acc", [128, 2, S], F32)
        R_acc = [Res("acc0"), Res("acc1")]

        sc.op("pool", lambda h: h.memset(kTe[64:128, :], 0.0), writes=[R_kT])
        sc.op("pool", lambda h: h.memset(kTo[0:64, :], 0.0), writes=[R_kT])
        sc.op("pool", lambda h: h.memset(VX[:, :, :, 64:128], 1.0), writes=[R_vext])

        def attention_unit(hp):
            W = w_att[hp % 2]
            RW = R_watt[hp % 2]
            for qk in range(2):
                for g in range(4):
                    pa, pb = (0, 1) if g % 2 == 0 else (2, 3)
                    for ci, pbk in ((0, pa), (1, pb)):
                        c0 = qk * 256 + ci * 128
                        for k in range(KC):
                            sc.op("pe", lambda h, k=k, c0=c0, pbk=pbk, g=g: h.matmul(
                                psum[pbk][:], lhsT=W[:, k, c0:c0 + 128], rhs=hT[:, k, g * 512:(g + 1) * 512],
                                start=(k == 0), stop=(k == KC - 1)),
                                reads=[RW] + R_hT[g * 4:(g + 1) * 4], writes=[R_ps[pbk]])
                    rb = g % 2
                    sc.op("dve", lambda h, pa=pa, g=g, rb=rb: h.tensor_tensor(
                        out=rtmp[rb][:, 0, :], in0=psum[pa][:], in1=ropec[:, g * 512:(g + 1) * 512], op=ALU.mult),
                        reads=[R_ps[pa], R_tab], writes=[R_rtmp[rb]])
                    sc.op("dve", lambda h, pb=pb, g=g, rb=rb: h.tensor_tensor(
                        out=rtmp[rb][:, 1, :], in0=psum[pb][:], in1=ropes[:, g * 512:(g + 1) * 512], op=ALU.mult),
                        reads=[R_ps[pb], R_tab], writes=[R_rtmp[rb]])
                    if qk == 0:
                        sc.op("pool", lambda h, g=g, rb=rb: h.tensor_tensor(
                            out=qT[:, g * 512:(g + 1) * 512], in0=rtmp[rb][:, 0, :], in1=rtmp[rb][:, 1, :],
                            op=ALU.add), reads=[R_rtmp[rb]], writes=[R_qT])
                    else:
                        sc.op("pool", lambda h, g=g, rb=rb: h.tensor_tensor(
                            out=kTe[0:64, g * 512:(g + 1) * 512], in0=rtmp[rb][0:64, 0, :],
                            in1=rtmp[rb][0:64, 1, :], op=ALU.add), reads=[R_rtmp[rb]], writes=[R_kT])
                        sc.op("pool", lambda h, g=g, rb=rb: h.tensor_tensor(
                            out=kTo[64:128, g * 512:(g + 1) * 512], in0=rtmp[rb][64:128, 0, :],
                            in1=rtmp[rb][64:128, 1, :], op=ALU.add), reads=[R_rtmp[rb]], writes=[R_kT])

            for g in range(4):
                pbk = g % 2
                for k in range(KC):
                    sc.op("pe", lambda h, k=k, pbk=pbk, g=g: h.matmul(
                        psum[pbk][:], lhsT=W[:, k, 512:640], rhs=hT[:, k, g * 512:(g + 1) * 512],
                        start=(k == 0), stop=(k == KC - 1)),
                        reads=[RW] + R_hT[g * 4:(g + 1) * 4], writes=[R_ps[pbk]])
                sc.op("act", lambda h, pbk=pbk, g=g: h.activation(out=vT[:, g * 512:(g + 1) * 512], in_=psum[pbk][:], func=AF.Copy),
                      reads=[R_ps[pbk]], writes=[R_vT])
            kTs = (kTe, kTo)

            def head_task(hh, bi, d, nb):
                kT_h = kTs[hh]
                sbank = (0, 1) if hh == 0 else (2, 3)
                obank = (4, 5) if hh == 0 else (6, 7)
                pTs = pT[hh * 3:hh * 3 + 3]
                RpTs = R_pT[hh * 3:hh * 3 + 3]
                steps = []
                oi = 0
                for r in range(d):
                    for ob in range(0, nb, 4):
                        nq = min(4, nb - ob)
                        for half in range(0, nq, 2):
                            steps.append((r, ob, nq, half, min(2, nq - half), oi, half + 2 >= nq))
                        oi += 1
                pending = None

                def emit_qk(i, st):
                    r, ob, nq, half, n2, oi, last = st
                    pss = sbank[i % 2]
                    pti = i % 3
                    for jj in range(n2):
                        bq = ob + half + jj
                        q0 = r + d * 128 * bq
                        qsl = qT[:, q0:q0 + d * 127 + 1:d]
                        if bq > 0:
                            k0 = r + d * 128 * (bq - 1)
                            sc.op("pe", lambda h, jj=jj, k0=k0, qsl=qsl, pss=pss: h.matmul(
                                psum[pss][:, jj * 256:jj * 256 + 128],
                                lhsT=kT_h[:, k0:k0 + d * 127 + 1:d], rhs=qsl, start=True, stop=False),
                                reads=[R_kT, R_qT], writes=[R_ps[pss]])
                            sc.op("pe", lambda h, jj=jj, pss=pss: h.matmul(
                                psum[pss][:, jj * 256:jj * 256 + 128], lhsT=ident[:], rhs=amask[:, 0:128],
                                start=False, stop=True), reads=[R_const, R_tab], writes=[R_ps[pss]])
                        k0 = r + d * 128 * bq
                        sc.op("pe", lambda h, jj=jj, k0=k0, qsl=qsl, pss=pss: h.matmul(
                            psum[pss][:, jj * 256 + 128:jj * 256 + 256],
                            lhsT=kT_h[:, k0:k0 + d * 127 + 1:d], rhs=qsl, start=True, stop=False),
                            reads=[R_kT, R_qT], writes=[R_ps[pss]])
                        sc.op("pe", lambda h, jj=jj, pss=pss: h.matmul(
                            psum[pss][:, jj * 256 + 128:jj * 256 + 256], lhsT=ident[:], rhs=amask[:, 128:256],
                            start=False, stop=True), reads=[R_const, R_tab], writes=[R_ps[pss]])
                    c_lo = 128 if (ob + half == 0) else 0
                    c_hi = n2 * 256
                    sc.op("act", lambda h, pss=pss, pti=pti, c_lo=c_lo, c_hi=c_hi: h.activation(
                        out=pTs[pti][:, c_lo:c_hi], in_=psum[pss][:, c_lo:c_hi], func=AF.Exp, scale=0.125),
                        reads=[R_ps[pss]], writes=[RpTs[pti]])

                def emit_pv(i, st):
                    r, ob, nq, half, n2, oi, last = st
                    pti = i % 3
                    pso = obank[oi % 2]
                    for jj in range(n2):
                        bq = ob + half + jj
                        oc = (half + jj) * 128
                        if bq > 0:
                            sc.op("pe", lambda h, jj=jj, bq=bq, oc=oc, pso=pso, pti=pti, r=r: h.matmul(
                                psum[pso][:, oc:oc + 128], lhsT=VX[:, r * nb + bq - 1, hh, :],
                                rhs=pTs[pti][:, jj * 256:jj * 256 + 128], start=True, stop=False),
                                reads=[R_vext, RpTs[pti]], writes=[R_ps[pso]])
                        sc.op("pe", lambda h, jj=jj, bq=bq, oc=oc, pso=pso, pti=pti, r=r: h.matmul(
                            psum[pso][:, oc:oc + 128], lhsT=VX[:, r * nb + bq, hh, :],
                            rhs=pTs[pti][:, jj * 256 + 128:jj * 256 + 256], start=(bq == 0), stop=True),
                            reads=[R_vext, RpTs[pti]], writes=[R_ps[pso]])
                    if last:
                        t0 = r + d * 128 * ob
                        n = nq * 128
                        dst = acc[:, hh, t0:t0 + d * (n - 1) + 1:d]
                        if bi == 0:
                            sc.op("act", lambda h, dst=dst, pso=pso, n=n: h.activation(
                                out=dst, in_=psum[pso][:, 0:n], func=AF.Copy),
                                reads=[R_ps[pso]], writes=[R_acc[hh]])
                        else:
                            sc.op("dve", lambda h, dst=dst, pso=pso, n=n: h.tensor_tensor(
                                out=dst, in0=psum[pso][:, 0:n], in1=dst, op=ALU.add),
                                reads=[R_ps[pso], R_acc[hh]], writes=[R_acc[hh]])

                for i, st in enumerate(steps):
                    emit_qk(i, st)
                    if pending is not None:
                        emit_pv(*pending)
                    pending = (i, st)
                    yield
                emit_pv(*pending)
                yield

            for bi, (d, nb) in enumerate(BR):
                for q4 in range(4):
                    pbk = q4 % 2
                    pvb = psum[pbk][:].bitcast(BF16)
                    for j in range(4):
                        blk = q4 * 4 + j
                        r, kb = blk // nb, blk % nb
                        t0 = r + d * 128 * kb
                        sc.op("pe", lambda h, j=j, t0=t0, pvb=pvb, d=d: h.transpose(
                            pvb[:, j * 128:(j + 1) * 128], vT[:, t0:t0 + d * 127 + 1:d], ident[:]),
                            reads=[R_vT, R_const], writes=[R_ps[pbk]])
                    sc.op("act", lambda h, q4=q4, pvb=pvb: h.activation(
                        out=VX[:, q4 * 4:(q4 + 1) * 4, :, 0:64],
                        in_=pvb[:, 0:512].rearrange("p (j h c) -> p j h c", j=4, h=2), func=AF.Copy),
                        reads=[R_ps[pbk]], writes=[R_vext])
                run_interleaved([head_task(0, bi, d, nb), head_task(1, bi, d, nb)])
            for hh in range(2):
                for g in range(4):
                    rb = g % 2
                    rlv = rtmp[rb][0:64, 0, :]
                    sc.op("act", lambda h, hh=hh, g=g, rlv=rlv: h.activation(
                        out=rlv, in_=acc[64:128, hh, g * 512:(g + 1) * 512], func=AF.Ln),
                        reads=[R_acc[hh]], writes=[R_rtmp[rb]])
                    sc.op("act", lambda h, rlv=rlv: h.activation(out=rlv, in_=rlv, func=AF.Exp, scale=-1.0),
                          reads=[R_rtmp[rb]], writes=[R_rtmp[rb]])
                    sc.op("pool", lambda h, hh=hh, g=g, rlv=rlv: h.tensor_tensor(
                        out=catA[64 * hh:64 * hh + 64, hp, g * 512:(g + 1) * 512],
                        in0=acc[0:64, hh, g * 512:(g + 1) * 512], in1=rlv, op=ALU.mult),
                        reads=[R_acc[hh], R_rtmp[rb]], writes=[R_catA[hp]])

        for hp in range(4):
            if hp < 3:
                load_watt(hp + 1)
            attention_unit(hp)
        end_phase()
        e.close()


    def deltanet_phase(l, catD, R_catD):
        hT = hT_box[0]
        e = ExitStack()
        C = sb(e, "dn_c", [128, 8, 128], F32)
        M1, M2, maskS, maskTu, Mblk, Mc0, Mc1, identf = [C[:, i, :] for i in range(8)]
        R_c = Res("dnc")
        phase_load(C[:], dn_c_d, R_c)
        cw = sb(e, "dn_cw", [128, 4, 3, 4], F32)
        hp = sb(e, "dn_hp", [128, 2, 4], F32)
        wba = sb(e, "w_ba", [128, KC, 8], BF16)
        R_c2 = Res("dnc2")
        phase_load(cw[:], dn_cw_d[:, l], R_c2)
        phase_load(hp[:], dn_hp_d[:, l], R_c2)
        R_wba = Res("wba")
        phase_load(wba[:], w_ba_d[l], R_wba, eng="pool")
        onesb = sb(e, "onesb", [128, 1], BF16)
        sc.op("pool", lambda h: h.memset(onesb[:], 1.0), writes=[R_c2])

        Wd = sb(e, "w_dn", [128, KC, 512], BF16)
        R_Wd = Res("wdn")
        q_wdn = sc.dma_sem("wdn_%d" % l)

        def load_wdn(hd):
            sc.dma("pool", q_wdn, lambda h: h.dma_start(out=Wd[:], in_=w_dn_d[l, hd]), writes=[R_Wd])

        SC = sb(e, "dn_sc", [128, 8, NT, 4], F32)
        beta, xa, graw, egc, ekd, tmpa = [SC[:, i] for i in range(6)]
        egl = SC[:, 6:8]
        R_sc = Res("dnsc")
        expA = sb(e, "expA", [128, 4], F32)
        pb = psum[0]
        for t in range(NT):
            for k in range(KC):
                sc.op("pe", lambda h, t=t, k=k: h.matmul(pb[:, t * 8:(t + 1) * 8], lhsT=hT[:, k, t * 128:(t + 1) * 128],
                                                         rhs=wba[:, k, :], start=(k == 0), stop=(k == KC - 1)),
                      reads=[R_hT[t], R_wba], writes=[R_ps[0]])
        bav = pb[:, 0:128].rearrange("p (t c) -> p t c", c=8)
        sc.op("act", lambda h: h.activation(out=beta, in_=bav[:, :, 0:4], func=AF.Sigmoid),
              reads=[R_ps[0]], writes=[R_sc])
        sc.op("dve", lambda h: h.tensor_tensor(out=xa, in0=bav[:, :, 4:8],
                                               in1=hp[:, 1, :].unsqueeze(1).to_broadcast([128, NT, 4]), op=ALU.add),
              reads=[R_ps[0], R_c2], writes=[R_sc])
        sc.op("act", lambda h: h.activation(out=xa, in_=xa, func=AF.Exp), reads=[R_sc], writes=[R_sc])
        sc.op("act", lambda h: h.activation(out=xa, in_=xa, func=AF.Ln, bias=1.0), reads=[R_sc], writes=[R_sc])
        sc.op("act", lambda h: h.activation(out=expA[:], in_=hp[:, 0, :], func=AF.Exp), reads=[R_c2], writes=[R_sc])
        sc.op("dve", lambda h: h.scalar_tensor_tensor(out=graw, in0=xa, scalar=-1.0,
                                                      in1=expA[:].unsqueeze(1).to_broadcast([128, NT, 4]),
                                                      op0=ALU.mult, op1=ALU.mult), reads=[R_sc], writes=[R_sc])
        grf = graw.rearrange("p t c -> p (t c)")
        pg = psum[1]
        for i, Mx in enumerate((M1, Mblk, Mc0, Mc1)):
            sc.op("pe", lambda h, i=i, Mx=Mx: h.matmul(pg[:, i * 64:(i + 1) * 64], lhsT=Mx, rhs=grf, start=True, stop=True),
                  reads=[R_c, R_sc], writes=[R_ps[1]])
        pgv = pg[:, 0:256].rearrange("p (i t c) -> p i t c", i=4, c=4)
        sc.op("act", lambda h: h.activation(out=egc, in_=pgv[:, 0], func=AF.Exp), reads=[R_ps[1]], writes=[R_sc])
        sc.op("dve", lambda h: h.tensor_copy(out=tmpa, in_=pgv[:, 0]), reads=[R_ps[1]], writes=[R_sc])
        sc.op("dve", lambda h: h.tensor_tensor(out=ekd, in0=pgv[:, 1], in1=tmpa, op=ALU.subtract),
              reads=[R_ps[1], R_sc], writes=[R_sc])
        sc.op("act", lambda h: h.activation(out=ekd, in_=ekd, func=AF.Exp), reads=[R_sc], writes=[R_sc])
        sc.op("act", lambda h: h.activation(out=SC[:, 6:8].rearrange("p a t c -> p (a t c)"), in_=pg[:, 128:256], func=AF.Exp),
              reads=[R_ps[1]], writes=[R_sc])

        xpad = [sb(e, "xpad%d" % i, [128, 3 + 512], BF16) for i in range(2)]
        R_xpad = [Res("xpad%d" % i) for i in range(2)]
        dg = sb(e, "dgw", [128, 12, 128], BF16)
        R_dg = Res("dg")
        csT = [sb(e, "csT%d" % i, [128, S], BF16) for i in range(3)]
        R_csT = [Res("csT%d" % i) for i in range(3)]
        sqt = sb(e, "sq", [128, S], BF16)
        sq = sqt[:]
        R_sqall = [Res("sq")]
        zw2 = [sb(e, "zw%d" % i, [128, NT, 128], BF16) for i in range(2)]
        R_zw2 = [Res("zw%d" % i) for i in range(2)]
        HS = sb(e, "dn_hs", [128, 6, NT], F32)
        R_hs = Res("hs")
        NH = 2
        GMh, R_GMh, ABh, R_ABh = [], [], [], []
        for hf_ in range(2):
            gmb = [sb(e, "gm%d_%d" % (hf_, i), [128, NH, 128], F32) for i in range(6)]
            rgm = [Res("gm%d_%d" % (hf_, i)) for i in range(6)]
            GMh.append([gmb[0], gmb[1], gmb[2], gmb[3], gmb[0], gmb[4], gmb[1], gmb[5], gmb[2]])
            R_GMh.append([rgm[0], rgm[1], rgm[2], rgm[3], rgm[0], rgm[4], rgm[1], rgm[5], rgm[2]])
            ABh.append([sb(e, "ab%d_%d" % (hf_, i), [128, NH, 128], BF16) for i in range(4)])
            R_ABh.append([Res("ab%d_%d" % (hf_, i)) for i in range(4)])
        HB = [[[sb(e, "hb%d_%d_%d" % (p, hf_, i), [128, NH, 128], BF16) for i in range(7)] for hf_ in range(2)]
              for p in range(2)]
        R_HB = [[[Res("hb%d_%d_%d" % (p, hf_, i)) for i in range(7)] for hf_ in range(2)] for p in range(2)]
        Sf = sb(e, "Sf", [128, 128], F32)
        Sb = sb(e, "Sb", [128, 128], BF16)
        R_S = Res("S")
        vnew = sb(e, "vnew", [128, 128], BF16)
        R_vnew = Res("vnew")
        p2s = [sb(e, "p2s%d" % i, [128, 128], F32) for i in range(2)]
        R_p2s = [Res("p2s%d" % i) for i in range(2)]
        osb = sb(e, "osb", [128, 4, 128], F32)
        R_osb = Res("osb")
        otk = sb(e, "otk", [128, 4, 128], BF16)
        R_otk = Res("otk")
        oss = sb(e, "oss", [128, 8], F32)
        R_oss = Res("oss")
        junk = sb(e, "junkd", [128, 128], BF16)
        sc.op("pool", lambda h: h.memset(oss[:], 0.0), writes=[R_oss])
        for p in range(2):
            for hf_ in range(2):
                sc.op("pool", lambda h, p=p, hf_=hf_: h.memset(HB[p][hf_][4][:], 0.0), writes=[R_HB[p][hf_][4]])
                sc.op("pool", lambda h, p=p, hf_=hf_: h.memset(HB[p][hf_][5][:], 0.0), writes=[R_HB[p][hf_][5]])

        load_wdn(0)

        def stage_C(hd):
            zw = zw2[hd % 2]
            R_zw = R_zw2[hd % 2]
            for cc in range(3):
                for j in range(4):
                    sc.op("dve", lambda h, cc=cc, j=j: h.tensor_scalar(
                        out=dg[:, cc * 4 + j, :], in0=ident[:], scalar1=cw[:, hd, cc, j:j + 1], scalar2=None, op0=ALU.mult),
                        reads=[R_const, R_c2], writes=[R_dg])
            yield
            for cc in range(3):
                for g in range(4):
                    pj = g % 2
                    pc = 2 + (g % 2)
                    xb = xpad[(cc * 4 + g) % 2]
                    Rxb = R_xpad[(cc * 4 + g) % 2]
                    xprev = xpad[(cc * 4 + g + 1) % 2]
                    Rxprev = R_xpad[(cc * 4 + g + 1) % 2]
                    for k in range(KC):
                        sc.op("pe", lambda h, k=k, cc=cc, g=g, pj=pj: h.matmul(
                            psum[pj][:], lhsT=Wd[:, k, cc * 128:(cc + 1) * 128], rhs=hT[:, k, g * 512:(g + 1) * 512],
                            start=(k == 0), stop=(k == KC - 1)), reads=[R_Wd] + R_hT[g * 4:(g + 1) * 4], writes=[R_ps[pj]])
                    sc.op("act", lambda h, pj=pj, xb=xb: h.activation(out=xb[:, 3:515], in_=psum[pj][:], func=AF.Copy),
                          reads=[R_ps[pj]], writes=[Rxb])
                    if g == 0:
                        sc.op("pool", lambda h, xb=xb: h.memset(xb[:, 0:3], 0.0), writes=[Rxb])
                    else:
                        sc.op("pool", lambda h, xb=xb, xprev=xprev: h.tensor_copy(out=xb[:, 0:3], in_=xprev[:, 512:515]),
                              reads=[Rxprev], writes=[Rxb])
                    for j in range(4):
                        sc.op("pe", lambda h, j=j, cc=cc, pc=pc, xb=xb: h.matmul(
                            psum[pc][:], lhsT=dg[:, cc * 4 + j, :], rhs=xb[:, j:j + 512], start=(j == 0), stop=(j == 3)),
                            reads=[R_dg, Rxb], writes=[R_ps[pc]])
                    sc.op("act", lambda h, pc=pc, cc=cc, g=g: h.activation(
                        out=csT[cc][:, g * 512:(g + 1) * 512], in_=psum[pc][:], func=AF.Silu),
                        reads=[R_ps[pc]], writes=[R_csT[cc]])
                    yield
            for t4 in range(4):
                pz = 2 + (t4 % 2)
                for j in range(4):
                    t = t4 * 4 + j
                    for k in range(KC):
                        sc.op("pe", lambda h, t=t, j=j, k=k, pz=pz: h.matmul(
                            psum[pz][:, j * 128:(j + 1) * 128], lhsT=hT[:, k, t * 128:(t + 1) * 128],
                            rhs=Wd[:, k, 384:512], start=(k == 0), stop=(k == KC - 1)),
                            reads=[R_Wd, R_hT[t]], writes=[R_ps[pz]])
                sc.op("act", lambda h, t4=t4, pz=pz: h.activation(
                    out=zw[:, t4 * 4:(t4 + 1) * 4, :], in_=psum[pz][:].rearrange("p (j c) -> p j c", j=4), func=AF.Silu),
                    reads=[R_ps[pz]], writes=[R_zw])
                yield
            if hd < 3:
                load_wdn(hd + 1)
            pss = psum[4]
            for qi in range(2):
                if qi == 0:
                    sc.op("act", lambda h, qi=qi: h.activation(out=sq, in_=csT[qi][:], func=AF.Square),
                          reads=[R_csT[qi]], writes=R_sqall)
                else:
                    sc.op("dve", lambda h, qi=qi: h.tensor_tensor(out=sq, in0=csT[qi][:], in1=csT[qi][:], op=ALU.mult),
                          reads=[R_csT[qi]], writes=R_sqall)
                for t in range(NT):
                    sc.op("pe", lambda h, t=t, qi=qi: h.matmul(
                        pss[:, qi * 16 + t:qi * 16 + t + 1], lhsT=sq[:, t * 128:(t + 1) * 128], rhs=onesb[:],
                        start=True, stop=True), reads=R_sqall + [R_c2], writes=[R_ps[4]])
                yield
            rqk = HS[:, 0:2, :]
            sc.op("dve", lambda h: h.tensor_scalar(out=rqk, in0=pss[:, 0:32].rearrange("p (a t) -> p a t", a=2),
                                                   scalar1=EPS, scalar2=None, op0=ALU.add),
                  reads=[R_ps[4]], writes=[R_hs])
            sc.op("act", lambda h: h.activation(out=rqk, in_=rqk, func=AF.Sqrt), reads=[R_hs], writes=[R_hs])
            sc.op("dve", lambda h: h.reciprocal(out=rqk, in_=rqk), reads=[R_hs], writes=[R_hs])
            sc.op("dve", lambda h: h.tensor_scalar(out=HS[:, 0, :], in0=HS[:, 0, :], scalar1=float(128 ** -0.5),
                                                   scalar2=None, op0=ALU.mult), reads=[R_hs], writes=[R_hs])
            sc.op("dve", lambda h, hd=hd: h.tensor_tensor(out=HS[:, 4, :], in0=HS[:, 1, :], in1=beta[:, :, hd], op=ALU.mult),
                  reads=[R_hs, R_sc], writes=[R_hs])
            sc.op("dve", lambda h, hd=hd: h.tensor_tensor(out=HS[:, 2, :], in0=HS[:, 4, :], in1=egc[:, :, hd], op=ALU.mult),
                  reads=[R_hs, R_sc], writes=[R_hs])
            sc.op("dve", lambda h, hd=hd: h.tensor_tensor(out=HS[:, 3, :], in0=HS[:, 1, :], in1=ekd[:, :, hd], op=ALU.mult),
                  reads=[R_hs, R_sc], writes=[R_hs])
            sc.op("dve", lambda h, hd=hd: h.tensor_scalar(out=HS[:, 5, :], in0=beta[:, :, hd], scalar1=-1.0, scalar2=None,
                                                          op0=ALU.mult), reads=[R_sc], writes=[R_hs])
            yield

        a_rr = [0, 0]

        def stage_A(G, hd, half):
            par = G % 2
            M2g, Dm, DTm, Pa, Pb, Ra, Rb, Ya, Yb = GMh[half]
            R_M2g, R_Dm, R_DTm, R_Pa, R_Pb, R_Ra, R_Rb, R_Ya, R_Yb = R_GMh[half]
            KT, kbg, knt, qnt = ABh[half]
            R_KT, R_kbg, R_knt, R_qnt = R_ABh[half]
            TT, WnT, qkT, QT, kd0, kd1, vbt = HB[par][half]
            R_TT, R_WnT, R_qkT, R_QT, R_kd0, R_kd1, R_vbt = R_HB[par][half]
            T0 = G * 4 + half * NH
            NC2 = NH * 128

            def abank():
                b = half * 3 + a_rr[half] % 3
                a_rr[half] += 1
                return b
            for j in range(NH):
                t = T0 + j
                bk = abank()
                ptv = psum[bk][:].bitcast(BF16)
                Rp = R_ps[bk]
                for ci in range(3):
                    sc.op("pe", lambda h, ci=ci, t=t, ptv=ptv: h.transpose(
                        ptv[:, ci * 128:(ci + 1) * 128], csT[ci][:, t * 128:(t + 1) * 128], ident[:]),
                        reads=[R_csT[ci], R_const], writes=[Rp])
                qv, kv, vv = ptv[:, 0:128], ptv[:, 128:256], ptv[:, 256:384]
                sc.op("act", lambda h, j=j, t=t, qv=qv: h.activation(out=qnt[:, j, :], in_=qv, func=AF.Copy,
                                                                     scale=HS[:, 0, t:t + 1]),
                      reads=[Rp, R_hs], writes=[R_qnt])
                sc.op("dve", lambda h, j=j, t=t, kv=kv: h.tensor_scalar(out=knt[:, j, :], in0=kv, scalar1=HS[:, 1, t:t + 1],
                                                                        scalar2=None, op0=ALU.mult),
                      reads=[Rp, R_hs], writes=[R_knt])
                sc.op("act", lambda h, j=j, t=t, kv=kv: h.activation(out=kbg[:, j, :], in_=kv, func=AF.Copy,
                                                                     scale=HS[:, 2, t:t + 1]),
                      reads=[Rp, R_hs], writes=[R_kbg])
                sc.op("dve", lambda h, j=j, t=t, kv=kv: h.tensor_scalar(out=kd0[0:64, j, :], in0=kv[0:64, :],
                                                                        scalar1=HS[0:64, 3, t:t + 1], scalar2=None,
                                                                        op0=ALU.mult),
                      reads=[Rp, R_hs], writes=[R_kd0])
                sc.op("dve", lambda h, j=j, t=t, kv=kv: h.tensor_scalar(out=kd1[64:128, j, :], in0=kv[64:128, :],
                                                                        scalar1=HS[64:128, 3, t:t + 1], scalar2=None,
                                                                        op0=ALU.mult),
                      reads=[Rp, R_hs], writes=[R_kd1])
                sc.op("act", lambda h, j=j, t=t, vv=vv: h.activation(out=vbt[:, j, :], in_=vv, func=AF.Copy,
                                                                     scale=beta[:, t, hd:hd + 1]),
                      reads=[Rp, R_sc], writes=[R_vbt])
                yield
            bk = abank()
            pkq = psum[bk][:].bitcast(BF16)
            for j in range(NH):
                sc.op("pe", lambda h, j=j: h.transpose(pkq[:, j * 128:(j + 1) * 128], knt[:, j, :], ident[:]),
                      reads=[R_knt, R_const], writes=[R_ps[bk]])
                sc.op("pe", lambda h, j=j: h.transpose(pkq[:, NC2 + j * 128:NC2 + (j + 1) * 128], qnt[:, j, :], ident[:]),
                      reads=[R_qnt, R_const], writes=[R_ps[bk]])
            sc.op("act", lambda h: h.activation(out=KT[:].rearrange("p j c -> p (j c)"), in_=pkq[:, 0:NC2], func=AF.Copy),
                  reads=[R_ps[bk]], writes=[R_KT])
            sc.op("dve", lambda h: h.tensor_copy(out=QT[:].rearrange("p j c -> p (j c)"), in_=pkq[:, NC2:2 * NC2]),
                  reads=[R_ps[bk]], writes=[R_QT])
            yield
            sc.op("dve", lambda h: h.tensor_tensor(
                out=M2g[:], in0=M2.unsqueeze(1).to_broadcast([128, NH, 128]),
                in1=graw[:, T0:T0 + NH, hd:hd + 1].to_broadcast([128, NH, 128]), op=ALU.mult),
                reads=[R_c, R_sc], writes=[R_M2g])
            bg, bt = abank(), abank()
            sc.op("pe", lambda h: h.matmul(psum[bg][:, 0:NC2], lhsT=M1, rhs=M2g[:].rearrange("p j c -> p (j c)"),
                                           start=True, stop=False), reads=[R_c, R_M2g], writes=[R_ps[bg]])
            for j in range(NH):
                sc.op("pe", lambda h, j=j: h.matmul(psum[bg][:, j * 128:(j + 1) * 128], lhsT=identf, rhs=maskS,
                                                    start=False, stop=(j == NH - 1)), reads=[R_c], writes=[R_ps[bg]])
            for j in range(NH):
                sc.op("pe", lambda h, j=j: h.matmul(psum[bt][:, j * 128:(j + 1) * 128], lhsT=M2g[:, j, :], rhs=M1,
                                                    start=True, stop=False), reads=[R_c, R_M2g], writes=[R_ps[bt]])
                sc.op("pe", lambda h, j=j: h.matmul(psum[bt][:, j * 128:(j + 1) * 128], lhsT=identf, rhs=maskTu,
                                                    start=False, stop=True), reads=[R_c], writes=[R_ps[bt]])
            sc.op("act", lambda h: h.activation(out=Dm[:].rearrange("p j c -> p (j c)"), in_=psum[bg][:, 0:NC2], func=AF.Exp),
                  reads=[R_ps[bg]], writes=[R_Dm])
            sc.op("act", lambda h: h.activation(out=DTm[:].rearrange("p j c -> p (j c)"), in_=psum[bt][:, 0:NC2], func=AF.Exp),
                  reads=[R_ps[bt]], writes=[R_DTm])
            yield
            bs, bq = abank(), abank()
            for j in range(NH):
                sc.op("pe", lambda h, j=j: h.matmul(psum[bs][:, j * 128:(j + 1) * 128], lhsT=KT[:, j, :], rhs=KT[:, j, :],
                                                    start=True, stop=True), reads=[R_KT], writes=[R_ps[bs]])
            for j in range(NH):
                sc.op("pe", lambda h, j=j: h.matmul(psum[bq][:, j * 128:(j + 1) * 128], lhsT=KT[:, j, :], rhs=QT[:, j, :],
                                                    start=True, stop=True), reads=[R_KT, R_QT], writes=[R_ps[bq]])
            for j in range(NH):
                sc.op("dve", lambda h, j=j: h.scalar_tensor_tensor(
                    out=Pa[:, j, :], in0=psum[bs][:, j * 128:(j + 1) * 128], scalar=HS[:, 5, T0 + j:T0 + j + 1],
                    in1=Dm[:, j, :], op0=ALU.mult, op1=ALU.mult), reads=[R_ps[bs], R_Dm, R_hs], writes=[R_Pa])
            sc.op("dve", lambda h: h.tensor_tensor(out=qkT[:].rearrange("p j c -> p (j c)"), in0=psum[bq][:, 0:NC2],
                                                   in1=DTm[:].rearrange("p j c -> p (j c)"), op=ALU.mult),
                  reads=[R_ps[bq], R_DTm], writes=[R_qkT])
            yield
            br = abank()
            for j in range(NH):
                sc.op("pe", lambda h, j=j: h.transpose(psum[br][:, j * 128:(j + 1) * 128], Pa[:, j, :], identf),
                      reads=[R_Pa, R_c], writes=[R_ps[br]])
            sc.op("act", lambda h: h.activation(out=Ra[:].rearrange("p j c -> p (j c)"), in_=psum[br][:, 0:NC2], func=AF.Copy),
                  reads=[R_ps[br]], writes=[R_Ra])
            sc.op("dve", lambda h: h.tensor_tensor(out=Ya[:], in0=psum[br][:, 0:NC2].rearrange("p (j c) -> p j c", j=NH),
                                                   in1=identf.unsqueeze(1).to_broadcast([128, NH, 128]),
                                                   op=ALU.add), reads=[R_ps[br], R_c], writes=[R_Ya])
            yield
            Pc, Pn, Rc, Rn, Yc, Yn = Pa, Pb, Ra, Rb, Ya, Yb
            RPc, RPn, RRc, RRn, RYc, RYn = R_Pa, R_Pb, R_Ra, R_Rb, R_Ya, R_Yb
            NL = 5
            for lev in range(1, NL + 1):
                bp = abank()
                for j in range(NH):
                    sc.op("pe", lambda h, j=j, Rc=Rc, Pc=Pc, bp=bp: h.matmul(psum[bp][:, j * 128:(j + 1) * 128], lhsT=Rc[:, j, :],
                                                                             rhs=Pc[:, j, :], start=True, stop=True),
                          reads=[RRc, RPc], writes=[R_ps[bp]])
                if lev < NL:
                    brr = abank()
                    for j in range(NH):
                        sc.op("pe", lambda h, j=j, Rc=Rc, Pc=Pc, brr=brr: h.matmul(psum[brr][:, j * 128:(j + 1) * 128], lhsT=Pc[:, j, :],
                                                                                   rhs=Rc[:, j, :], start=True, stop=True),
                              reads=[RRc, RPc], writes=[R_ps[brr]])
                sc.op("act", lambda h, Pn=Pn, bp=bp: h.activation(out=Pn[:].rearrange("p j c -> p (j c)"), in_=psum[bp][:, 0:NC2], func=AF.Copy),
                      reads=[R_ps[bp]], writes=[RPn])
                if lev < NL:
                    sc.op("act", lambda h, Rn=Rn, brr=brr: h.activation(out=Rn[:].rearrange("p j c -> p (j c)"), in_=psum[brr][:, 0:NC2],
                                                                        func=AF.Copy), reads=[R_ps[brr]], writes=[RRn])
                yield
                by = abank()
                for j in range(NH):
                    sc.op("pe", lambda h, j=j, Pn=Pn, Yc=Yc, by=by: h.matmul(psum[by][:, j * 128:(j + 1) * 128], lhsT=Pn[:, j, :],
                                                                             rhs=Yc[:, j, :], start=True, stop=True),
                          reads=[RPn, RYc], writes=[R_ps[by]])
                if lev < NL:
                    sc.op("dve", lambda h, Yn=Yn, Yc=Yc, by=by: h.tensor_tensor(out=Yn[:].rearrange("p j c -> p (j c)"), in0=psum[by][:, 0:NC2],
                                                                                in1=Yc[:].rearrange("p j c -> p (j c)"), op=ALU.add),
                          reads=[R_ps[by], RYc], writes=[RYn])
                else:
                    sc.op("dve", lambda h, Yc=Yc, by=by: h.tensor_tensor(out=TT[:].rearrange("p j c -> p (j c)"), in0=psum[by][:, 0:NC2],
                                                                         in1=Yc[:].rearrange("p j c -> p (j c)"), op=ALU.add),
                          reads=[R_ps[by], RYc], writes=[R_TT])
                Pc, Pn, Rc, Rn, Yc, Yn = Pn, Pc, Rn, Rc, Yn, Yc
                RPc, RPn, RRc, RRn, RYc, RYn = RPn, RPc, RRn, RRc, RYn, RYc
                yield
            bw = abank()
            for j in range(NH):
                sc.op("pe", lambda h, j=j: h.matmul(psum[bw][:, j * 128:(j + 1) * 128], lhsT=kbg[:, j, :], rhs=TT[:, j, :],
                                                    start=True, stop=True), reads=[R_kbg, R_TT], writes=[R_ps[bw]])
            sc.op("act", lambda h: h.activation(out=WnT[:].rearrange("p j c -> p (j c)"), in_=psum[bw][:, 0:NC2], func=AF.Copy,
                                                scale=-1.0), reads=[R_ps[bw]], writes=[R_WnT])
            yield

        def stage_B(G, hd):
            zw = zw2[hd % 2]
            R_zw = R_zw2[hd % 2]
            if G == 0:
                sc.op("pool", lambda h: h.memset(Sf[:], 0.0), writes=[R_S])
                sc.op("pool", lambda h: h.memset(Sb[:], 0.0), writes=[R_S])
            par = G % 2
            T0 = G * 4
            pA, pB = psum[7], psum[6]
            pC = psum[7][:, 256:384]
            RA, RB, RC = R_ps[7], R_ps[6], R_ps[7]
            def tile_steps(j4):
                t = T0 + j4
                half, j = j4 // NH, j4 % NH
                TT, WnT, qkT, QT, kd0, kd1, vbt = HB[par][half]
                R_TT, R_WnT, R_qkT, R_QT, R_kd0, R_kd1, R_vbt = R_HB[par][half]
                for c in range(2):
                    kd = kd0 if c == 0 else kd1
                    Rkd = R_kd0 if c == 0 else R_kd1
                    sc.op("pe", lambda h, j=j, c=c: h.matmul(pA[:, c * 128:(c + 1) * 128], lhsT=TT[:, j, :], rhs=vbt[:, j, :],
                                                             start=True, stop=False),
                          reads=[R_TT, R_vbt], writes=[RA])
                    sc.op("pe", lambda h, j=j, c=c: h.matmul(pA[:, c * 128:(c + 1) * 128], lhsT=WnT[:, j, :], rhs=Sb[:],
                                                             start=False, stop=True),
                          reads=[R_WnT, R_S], writes=[RA])
                    if c == 0:
                        sc.op("dve", lambda h: h.tensor_copy(out=vnew[:], in_=pA[:, 0:128]),
                              reads=[RA], writes=[R_vnew])
                    else:
                        sc.op("dve", lambda h: h.tensor_copy(out=vnew[64:128, :], in_=pA[64:128, 128:256]),
                              reads=[RA], writes=[R_vnew])
                    yield
                    sc.op("pe", lambda h, j=j, c=c: h.matmul(pB[:, c * 128:(c + 1) * 128], lhsT=QT[:, j, :], rhs=Sb[:],
                                                             start=True, stop=True),
                          reads=[R_QT, R_S], writes=[RB])
                    sc.op("pe", lambda h, j=j, kd=kd: h.matmul(pC, lhsT=kd[:, j, :], rhs=vnew[:],
                                                               start=True, stop=True),
                          reads=[Rkd, R_vnew], writes=[RC])
                    if c == 1:
                        sc.op("pe", lambda h, j=j: h.matmul(pB[:, 256:384], lhsT=qkT[:, j, :], rhs=vnew[:],
                                                            start=True, stop=True),
                              reads=[R_qkT, R_vnew], writes=[RB])
                    sc.op("dve", lambda h, c=c, t=t: h.scalar_tensor_tensor(
                        out=Sb[:], in0=Sf[:], scalar=egl[:, c, t, hd:hd + 1], in1=pC, op0=ALU.mult, op1=ALU.add),
                        reads=[R_S, RC, R_sc], writes=[R_S])
                    sc.op("dve", lambda h, c=c, t=t: h.scalar_tensor_tensor(
                        out=Sf[:], in0=Sf[:], scalar=egl[:, c, t, hd:hd + 1], in1=pC, op0=ALU.mult, op1=ALU.add),
                        reads=[R_S, RC, R_sc], writes=[R_S])
                    yield
                p2 = p2s[j4 % 2]
                Rp2 = R_p2s[j4 % 2]
                sc.op("act", lambda h, p2=p2: h.activation(out=p2[:], in_=pB[:, 256:384], func=AF.Copy),
                      reads=[RB], writes=[Rp2])
                for c in range(2):
                    rs = slice(64 * c, 64 * c + 64)
                    sc.op("dve", lambda h, c=c, rs=rs, j4=j4, t=t, p2=p2: h.scalar_tensor_tensor(
                        out=osb[rs, j4, :], in0=pB[rs, c * 128:(c + 1) * 128], scalar=egc[rs, t, hd:hd + 1],
                        in1=p2[rs, :], op0=ALU.mult, op1=ALU.add), reads=[RB, Rp2, R_sc], writes=[R_osb])
                sc.op("act", lambda h, j4=j4: h.activation(out=junk[:], in_=osb[:, j4, :], func=AF.Square,
                                                           accum_out=oss[:, j4:j4 + 1]), reads=[R_osb], writes=[R_oss])
                yield
            for j4 in range(4):
                yield from tile_steps(j4)
            sc.op("dve", lambda h: h.tensor_scalar(out=oss[:, 4:8], in0=oss[:, 0:4], scalar1=1.0 / 128, scalar2=EPS,
                                                   op0=ALU.mult, op1=ALU.add), reads=[R_oss], writes=[R_oss])
            sc.op("act", lambda h: h.activation(out=oss[:, 4:8], in_=oss[:, 4:8], func=AF.Ln), reads=[R_oss], writes=[R_oss])
            sc.op("act", lambda h: h.activation(out=oss[:, 4:8], in_=oss[:, 4:8], func=AF.Exp, scale=-0.5),
                  reads=[R_oss], writes=[R_oss])
            pO = pB[:].bitcast(BF16)
            for j in range(4):
                t = T0 + j
                sc.op("dve", lambda h, j=j, t=t: h.scalar_tensor_tensor(
                    out=otk[:, j, :], in0=osb[:, j, :], scalar=oss[:, 4 + j:5 + j], in1=zw[:, t, :],
                    op0=ALU.mult, op1=ALU.mult), reads=[R_osb, R_oss, R_zw], writes=[R_otk])
                sc.op("pe", lambda h, j=j: h.transpose(pO[:, j * 128:(j + 1) * 128], otk[:, j, :], ident[:]),
                      reads=[R_otk, R_const], writes=[RB])
            sc.op("act", lambda h: h.activation(out=catD[:, hd, T0 * 128:(T0 + 4) * 128], in_=pO[:, 0:512],
                                                func=AF.Copy), reads=[RB], writes=[R_catD[hd]])
            sc.op("dve", lambda h: h.memset(oss[:, 0:4], 0.0), reads=[R_oss], writes=[R_oss])
            yield

        run_interleaved([stage_C(0)])
        for hd in range(4):
            run_interleaved([stage_A(0, hd, 0), stage_A(0, hd, 1)])
            for G in range(4):
                tasks = [stage_B(G, hd)]
                if G < 3:
                    tasks.append(stage_A(G + 1, hd, 0))
                    tasks.append(stage_A(G + 1, hd, 1))
                elif hd < 3:
                    tasks.append(stage_C(hd + 1))
                run_interleaved(tasks)
        end_phase()
        e.close()


    def postnorm_residual(t, pa, pb, wpost, R_wpost, tmpb, R_tmpb, ssb, R_ssb, junkp):
        sc.op("act", lambda h: h.activation(out=junkp[:], in_=psum[pa][:], func=AF.Square, accum_out=ssb[:, 0:1]),
              reads=[R_ps[pa]], writes=[R_ssb])
        sc.op("act", lambda h: h.activation(out=junkp[:], in_=psum[pb][:], func=AF.Square, accum_out=ssb[:, 1:2]),
              reads=[R_ps[pb]], writes=[R_ssb])
        sc.op("dve", lambda h: h.tensor_tensor(out=ssb[:, 2:3], in0=ssb[:, 0:1], in1=ssb[:, 1:2], op=ALU.add),
              reads=[R_ssb], writes=[R_ssb])
        sc.op("dve", lambda h: h.tensor_scalar(out=ssb[:, 2:3], in0=ssb[:, 2:3], scalar1=1.0 / D, scalar2=EPS,
                                               op0=ALU.mult, op1=ALU.add), reads=[R_ssb], writes=[R_ssb])
        sc.op("act", lambda h: h.activation(out=ssb[:, 2:3], in_=ssb[:, 2:3], func=AF.Sqrt), reads=[R_ssb], writes=[R_ssb])
        sc.op("dve", lambda h: h.reciprocal(out=ssb[:, 3:4], in_=ssb[:, 2:3]), reads=[R_ssb], writes=[R_ssb])
        sc.op("dve", lambda h: h.tensor_tensor(out=tmpb[:, 0:512], in0=psum[pa][:], in1=wpost[:, 0:512], op=ALU.mult),
              reads=[R_ps[pa], R_wpost], writes=[R_tmpb])
        sc.op("dve", lambda h: h.tensor_tensor(out=tmpb[:, 512:1024], in0=psum[pb][:], in1=wpost[:, 512:1024], op=ALU.mult),
              reads=[R_ps[pb], R_wpost], writes=[R_tmpb])
        sc.op("dve", lambda h: h.scalar_tensor_tensor(out=x_sb[:, t, :], in0=tmpb[:], scalar=ssb[:, 3:4], in1=x_sb[:, t, :],
                                                      op0=ALU.mult, op1=ALU.add),
              reads=[R_tmpb, R_ssb, R_x[t]], writes=[R_x[t]])
        sc.op("dve", lambda h: h.memset(ssb[:, 0:2], 0.0), reads=[R_ssb], writes=[R_ssb])

    def outproj_phase(l, catA, R_catA, catD, R_catD):
        e = ExitStack()
        Wo = sb(e, "w_o", [128, KC, D], BF16)
        R_Wo = Res("wo")
        phase_load(Wo[:], w_o_d[l], R_Wo, eng="pool")
        wpost = sb(e, "wpost", [128, D], F32)
        R_wpost = Res("wpost")
        phase_load(wpost[:], npost_d[:, l, 0, :], R_wpost)
        nwc = sb(e, "nwc", [128, 1], F32)
        R_nwc = Res("nwc")
        phase_load(nwc[:], dn_nwT_d[l], R_nwc)
        sc.op("dve", lambda h: h.tensor_scalar(out=Wo[:, 4:8, :], in0=Wo[:, 4:8, :], scalar1=nwc[:, 0:1], scalar2=None,
                                               op0=ALU.mult), reads=[R_Wo, R_nwc], writes=[R_Wo])
        tmpb = [sb(e, "tmpb%d" % i, [128, D], F32) for i in range(2)]
        R_tmpb = [Res("tmpb%d" % i) for i in range(2)]
        ssb = [sb(e, "ssb%d" % i, [128, 4], F32) for i in range(2)]
        R_ssb = [Res("ssb%d" % i) for i in range(2)]
        junkp = sb(e, "junkp", [128, 512], BF16)
        for i in range(2):
            sc.op("pool", lambda h, i=i: h.memset(ssb[i][:], 0.0), writes=[R_ssb[i]])
        for t in range(NT):
            pa, pb = (0, 1) if t % 2 == 0 else (2, 3)
            for n, pk in ((0, pa), (1, pb)):
                for k in range(KC):
                    src = catA if k < 4 else catD
                    Rsrc = R_catA[k] if k < 4 else R_catD[k - 4]
                    sc.op("pe", lambda h, k=k, n=n, pk=pk, t=t, src=src: h.matmul(
                        psum[pk][:], lhsT=src[:, k % 4, t * 128:(t + 1) * 128], rhs=Wo[:, k, n * 512:(n + 1) * 512],
                        start=(k == 0), stop=(k == KC - 1)), reads=[Rsrc, R_Wo], writes=[R_ps[pk]])
            postnorm_residual(t, pa, pb, wpost, R_wpost, tmpb[t % 2], R_tmpb[t % 2], ssb[t % 2], R_ssb[t % 2], junkp)
        end_phase()
        e.close()

    def ffn_phase(l):
        e = ExitStack()
        fcw = sb(e, "fcw", [128, 2 * NFC, 4], F32)
        R_fcw = Res("fcw")
        phase_load(fcw[:], fcw_d[:, l], R_fcw)
        halo = sb(e, "halo", [128, 2 * NFC, 2], F32)
        R_halo = Res("halo")
        hTh = sb(e, "hTh", [128, KC, 1024], BF16)
        R_hTh = [Res("hTh%d" % i) for i in range(8)]
        actT = sb(e, "actT", [128, NFC, 1024], BF16)
        R_act = [Res("act%d" % i) for i in range(NFC)]
        W2 = sb(e, "w_f2", [128, NFC, D], BF16)
        R_W2 = Res("w2")
        q_w2 = sc.dma_sem("wf2_%d" % l)
        q_w1 = [sc.dma_sem("wf1_%d_%d" % (l, i)) for i in range(2)]
        for hf in range(2):
            if hf == 0:
                sc.dma("pool", q_w2, lambda h: h.dma_start(out=W2[:, 0:11, :], in_=w_f2_d[l, :, 0:11, :]), writes=[R_W2])
                sc.dma("pool", q_w2, lambda h: h.dma_start(out=W2[:, 11:22, :], in_=w_f2_d[l, :, 11:22, :]), writes=[R_W2])
                R_W2.w = (q_w2, sc.cnt[q_w2])
            eu = ExitStack()
            W1 = [sb(eu, "w_f1_%d" % i, [128, KC, 256], BF16) for i in range(2)]
            R_W1 = [Res("w1_%d" % i) for i in range(2)]
            xpd = [[sb(eu, "fxp%d_%d" % (p, i), [128, 2 + 512], F32) for i in range(2)] for p in range(2)]
            R_xpd = [[Res("fxp%d_%d" % (p, i)) for i in range(2)] for p in range(2)]
            ycv = [[sb(eu, "fy%d_%d" % (p, i), [128, 512], F32) for i in range(2)] for p in range(2)]
            R_ycv = [[Res("fy%d_%d" % (p, i)) for i in range(2)] for p in range(2)]
            ggl = [sb(eu, "ggl%d" % i, [128, 512], F32) for i in range(2)]
            R_ggl = [Res("ggl%d" % i) for i in range(2)]

            def load_w1(jc):
                i = jc % 2
                sc.dma("pool", q_w1[i], lambda h, i=i, jc=jc: h.dma_start(out=W1[i][:], in_=w_f1_d[l, jc]), writes=[R_W1[i]])
            load_w1(0)
            prenorm(l, 1, hTh, list(range(hf * 8, hf * 8 + 8)), R_hTh)
            pend_glu = [None]

            def emit_glu(bi, jc, gi):
                gb, Rgb = ggl[bi], R_ggl[bi]
                sc.op("act", lambda h: h.activation(out=gb[:], in_=ycv[0][bi][:], func=AF.Gelu_apprx_tanh),
                      reads=[R_ycv[0][bi]], writes=[Rgb])
                sc.op("pool", lambda h: h.tensor_tensor(
                    out=actT[:, jc, gi * 512:(gi + 1) * 512], in0=gb[:], in1=ycv[1][bi][:], op=ALU.mult),
                    reads=[Rgb, R_ycv[1][bi]], writes=[R_act[jc]])

            for jc in range(NFC):
                if jc + 1 < NFC:
                    load_w1(jc + 1)
                Wc = W1[jc % 2]
                RWc = R_W1[jc % 2]
                for gi in range(2):
                    g = hf * 2 + gi
                    bi = (jc * 2 + gi) % 2
                    for part in range(2):
                        ch = part * NFC + jc
                        pk = part * 4 + (jc * 2 + gi) % 4
                        xb, Rxb = xpd[part][bi], R_xpd[part][bi]
                        xprev, Rxprev = xpd[part][1 - bi], R_xpd[part][1 - bi]
                        yb, Ryb = ycv[part][bi], R_ycv[part][bi]
                        for k in range(KC):
                            sc.op("pe", lambda h, k=k, part=part, pk=pk, gi=gi, Wc=Wc: h.matmul(
                                psum[pk][:], lhsT=Wc[:, k, part * 128:(part + 1) * 128], rhs=hTh[:, k, gi * 512:(gi + 1) * 512],
                                start=(k == 0), stop=(k == KC - 1)), reads=[RWc] + R_hTh[gi * 4:(gi + 1) * 4], writes=[R_ps[pk]])
                        sc.op("act", lambda h, pk=pk, xb=xb: h.activation(out=xb[:, 2:514], in_=psum[pk][:], func=AF.Copy),
                              reads=[R_ps[pk]], writes=[Rxb])
                        if g == 0:
                            sc.op("pool", lambda h, xb=xb: h.memset(xb[:, 0:2], 0.0), writes=[Rxb])
                        elif gi == 0:
                            sc.op("pool", lambda h, xb=xb, ch=ch: h.tensor_copy(out=xb[:, 0:2], in_=halo[:, ch, :]),
                                  reads=[R_halo], writes=[Rxb])
                        else:
                            sc.op("pool", lambda h, xb=xb, xprev=xprev: h.tensor_copy(out=xb[:, 0:2], in_=xprev[:, 512:514]),
                                  reads=[Rxprev], writes=[Rxb])
                        if hf == 0 and gi == 1:
                            sc.op("pool", lambda h, xb=xb, ch=ch: h.tensor_copy(out=halo[:, ch, :], in_=xb[:, 512:514]),
                                  reads=[Rxb], writes=[R_halo])
                        sc.op("act", lambda h, pk=pk, yb=yb, ch=ch: h.activation(
                            out=yb[:], in_=psum[pk][:], func=AF.Identity, scale=fcw[:, ch, 2:3], bias=fcw[:, ch, 3:4]),
                            reads=[R_ps[pk], R_fcw], writes=[Ryb])
                        for j in range(2):
                            sc.op("dve", lambda h, j=j, yb=yb, xb=xb, ch=ch: h.scalar_tensor_tensor(
                                out=yb[:], in0=xb[:, j:j + 512], scalar=fcw[:, ch, j:j + 1], in1=yb[:],
                                op0=ALU.mult, op1=ALU.add), reads=[Rxb, Ryb, R_fcw], writes=[Ryb])
                    if pend_glu[0] is not None:
                        emit_glu(*pend_glu[0])
                    pend_glu[0] = (bi, jc, gi)
            emit_glu(*pend_glu[0])
            pend_glu[0] = None
            sc.barrier()
            sc.emit_all()
            eu.close()
            ed = ExitStack()
            wpost = sb(ed, "wpost", [128, D], F32)
            R_wpost = Res("wpost")
            phase_load(wpost[:], npost_d[:, l, 1, :], R_wpost)
            tmpb = [sb(ed, "tmpb%d" % i, [128, D], F32) for i in range(2)]
            R_tmpb = [Res("tmpb%d" % i) for i in range(2)]
            ssb = [sb(ed, "ssb%d" % i, [128, 4], F32) for i in range(2)]
            R_ssb = [Res("ssb%d" % i) for i in range(2)]
            junkp = sb(ed, "junkp", [128, 512], BF16)
            for i in range(2):
                sc.op("pool", lambda h, i=i: h.memset(ssb[i][:], 0.0), writes=[R_ssb[i]])
            for tl in range(8):
                t = hf * 8 + tl
                pa, pb = (0, 1) if tl % 2 == 0 else (2, 3)
                for n, pk in ((0, pa), (1, pb)):
                    for jc in range(NFC):
                        sc.op("pe", lambda h, jc=jc, n=n, pk=pk, tl=tl: h.matmul(
                            psum[pk][:], lhsT=actT[:, jc, tl * 128:(tl + 1) * 128], rhs=W2[:, jc, n * 512:(n + 1) * 512],
                            start=(jc == 0), stop=(jc == NFC - 1)), reads=[R_act[jc], R_W2], writes=[R_ps[pk]])
                postnorm_residual(t, pa, pb, wpost, R_wpost, tmpb[tl % 2], R_tmpb[tl % 2], ssb[tl % 2], R_ssb[tl % 2], junkp)
                if l == depth - 1:
                    sc.dma("sp", q_out, lambda h, t=t: h.dma_start(out=y_d[t * 128:(t + 1) * 128, :], in_=x_sb[:, t, :]),
                           reads=[R_x[t]])
            sc.barrier()
            sc.emit_all()
            ed.close()
        e.close()

    for l in range(depth):
        e_mix = ExitStack()
        hT = sb(e_mix, "hT", [128, KC, S], BF16)
        hT_box[0] = hT
        R_hT[:] = [Res("hT%d" % t) for t in range(NT)]
        catA = sb(e_mix, "catA", [128, 4, S], BF16)
        R_catA = [Res("catA%d" % c) for c in range(4)]
        attention_phase(l, catA, R_catA, pre=lambda: prenorm(l, 0, hT, list(range(NT)), R_hT))
        if dbg and l == 0:
            sc.dma("sp", q_out, lambda h: h.dma_start(out=dbg_d["catA"], in_=catA[:]), reads=R_catA)
            end_phase()
        catD = sb(e_mix, "catD", [128, 4, S], BF16)
        R_catD = [Res("catD%d" % c) for c in range(4)]
        if stage >= 2:
            deltanet_phase(l, catD, R_catD)
        if dbg and l == 0:
            sc.dma("sp", q_out, lambda h: h.dma_start(out=dbg_d["catD"], in_=catD[:]), reads=R_catD)
            end_phase()
        if stage >= 3:
            outproj_phase(l, catA, R_catA, catD, R_catD)
        e_mix.close()
        if dbg and l == 0:
            sc.dma("sp", q_out, lambda h: h.dma_start(out=dbg_d["xmid"].rearrange("(t p) d -> p t d", p=128), in_=x_sb[:]),
                   reads=R_x)
            end_phase()
        if stage >= 4:
            ffn_phase(l)

    if stage < 4:
        yv = y_d.rearrange("(t p) d -> p t d", p=128)
        sc.dma("sp", q_out, lambda h: h.dma_start(out=yv, in_=x_sb[:]), reads=R_x)
    sc.wait_all("sp", [(q_out, sc.cnt[q_out])])
    sc.emit_all()
    es.close()
    return nc, sc


def host_prep(inputs, depth=DEPTH):
    f32 = np.float32
    w_in = np.asarray(inputs["w_in"], f32)
    out = {}
    w_att = np.empty((depth, 4, 128, KC, 640), f32)
    for l in range(depth):
        Wl = w_in[l].reshape(KC, 128, -1)
        for hp in range(4):
            cols = []
            for base in (0, 512):
                c = np.arange(base + hp * 128, base + hp * 128 + 128)
                cols.append(c)
                cs = c.reshape(2, 2, 32)[:, ::-1, :].reshape(-1)
                cols.append(cs)
            cols.append(np.arange(1024 + hp * 128, 1024 + hp * 128 + 128))
            cols = np.concatenate(cols)
            w_att[l, hp] = Wl[:, :, cols].transpose(1, 0, 2)
    out["w_att"] = w_att
    out["ident"] = np.eye(128, dtype=f32)
    inv = 1.0 / (10000.0 ** (np.arange(0, 64, 2, dtype=np.float32) / 64.0))
    ang = np.arange(S, dtype=np.float32)[None, :] * inv[:, None].astype(np.float32)
    cos = np.cos(ang).astype(f32)
    sin = np.sin(ang).astype(f32)
    rc = np.empty((128, S), f32)
    rs = np.empty((128, S), f32)
    for p in range(128):
        rc[p] = cos[p % 32]
        rs[p] = -sin[p % 32] if (p % 64) < 32 else sin[p % 32]
    out["rope_c"] = rc
    out["rope_s"] = rs
    c = np.arange(128)[:, None]
    a = np.arange(128)[None, :]
    Lm = np.where(c >= a, 0.0, -30000.0).astype(f32)
    Um = np.where(c <= a, 0.0, -30000.0).astype(f32)
    out["amask"] = np.concatenate([Lm, Um, Lm, Um, Lm, Um, Lm, Um], axis=1)
    npre = np.empty((128, depth, 2, KC), f32)
    for l in range(depth):
        npre[:, l, 0, :] = np.asarray(inputs["norm_pre_mix"], f32)[l].reshape(KC, 128).T
        npre[:, l, 1, :] = np.asarray(inputs["norm_pre_ffn"], f32)[l].reshape(KC, 128).T
    out["npre"] = npre
    w_dn = np.empty((depth, 4, 128, KC, 512), f32)
    w_ba = np.empty((depth, 128, KC, 8), f32)
    dn_cw = np.empty((128, depth, 4, 3, 4), f32)
    dn_hp = np.empty((128, depth, 2, 4), f32)
    dn_nw = np.empty((depth, 128, 1), f32)
    cwl = np.asarray(inputs["dn_conv_w"], f32)
    for l in range(depth):
        Wl = w_in[l].reshape(KC, 128, -1)
        for hd in range(4):
            cols = np.concatenate([np.arange(b + hd * 128, b + hd * 128 + 128) for b in (1536, 2048, 2560, 3072)])
            w_dn[l, hd] = Wl[:, :, cols].transpose(1, 0, 2)
            for cc in range(3):
                ch = cc * 512 + hd * 128 + np.arange(128)
                dn_cw[:, l, hd, cc, :] = cwl[l][:, ch].T
        w_ba[l] = Wl[:, :, 3584:3592].transpose(1, 0, 2)
        dn_hp[:, l, 0, :] = np.asarray(inputs["dn_a_log"], f32)[l][None, :]
        dn_hp[:, l, 1, :] = np.asarray(inputs["dn_dt_bias"], f32)[l][None, :]
        dn_nw[l, :, 0] = np.asarray(inputs["dn_norm_w"], f32)[l]
    out["w_dn"] = w_dn
    out["w_ba"] = w_ba
    out["dn_cw"] = dn_cw
    out["dn_hp"] = dn_hp
    out["dn_nwT"] = dn_nw
    ti = np.arange(128)[:, None]
    tj = np.arange(128)[None, :]
    same = (ti // 64) == (tj // 64)
    dn_c = np.zeros((128, 8, 128), f32)
    dn_c[:, 0] = (same & (ti <= tj))
    dn_c[:, 1] = (same & (ti > tj))
    dn_c[:, 2] = np.where(same & (ti > tj), 0.0, -30000.0)
    dn_c[:, 3] = np.where(same & (tj >= ti), 0.0, -30000.0)
    dn_c[:, 4] = same
    dn_c[:, 5] = (ti < 64) & (tj >= 0)
    dn_c[:, 6] = (ti >= 64) & (tj >= 0)
    dn_c[:, 7] = np.eye(128)
    out["dn_c"] = dn_c
    w_out = np.asarray(inputs["w_out"], f32)
    out["w_o"] = np.ascontiguousarray(w_out.reshape(depth, KC, 128, D).transpose(0, 2, 1, 3))
    npost = np.empty((128, depth, 2, D), f32)
    npost[:, :, 0, :] = np.asarray(inputs["norm_post_mix"], f32)[None]
    npost[:, :, 1, :] = np.asarray(inputs["norm_post_ffn"], f32)[None]
    out["npost"] = npost
    fw1 = np.asarray(inputs["ffn_w_in"], f32).reshape(depth, KC, 128, 2, NFC, 128)
    out["w_f1"] = np.ascontiguousarray(fw1.transpose(0, 4, 2, 1, 3, 5).reshape(depth, NFC, 128, KC, 256))
    fw2 = np.asarray(inputs["ffn_w_out"], f32).reshape(depth, NFC, 128, D)
    out["w_f2"] = np.ascontiguousarray(fw2.transpose(0, 2, 1, 3))
    fcw = np.empty((128, depth, 2 * NFC, 4), f32)
    cwf = np.asarray(inputs["ffn_conv_w"], f32).reshape(depth, 3, 2 * NFC, 128)
    cbf = np.asarray(inputs["ffn_conv_b"], f32).reshape(depth, 2 * NFC, 128)
    fcw[:, :, :, 0:3] = cwf.transpose(3, 0, 2, 1)
    fcw[:, :, :, 3] = cbf.transpose(2, 0, 1)
    out["fcw"] = fcw
    return out


_CACHE = {}


def kernel(**inputs):
    x = np.ascontiguousarray(np.asarray(inputs["x"], np.float32))
    shared = host_prep(inputs)
    if "nc" not in _CACHE:
        _CACHE["nc"] = build_program()[0]
    nc = _CACHE["nc"]
    in_maps = []
    for c in range(N_CORES):
        m = dict(shared)
        m["x"] = x[c]
        in_maps.append(m)
    res = run_bass_kernel_spmd(nc, in_maps, core_ids=list(range(N_CORES)))
    return np.stack([np.asarray(r["y"], np.float32) for r in res.results], axis=0)
```

```python
import numpy as np
import ml_dtypes
from contextlib import ExitStack
import concourse.bass as bass
import concourse.mybir as mybir
from concourse.bass_utils import run_bass_kernel_spmd

F32 = mybir.dt.float32
BF16 = mybir.dt.bfloat16
AF = mybir.ActivationFunctionType
ALU = mybir.AluOpType

S = 2048
D = 1024
NT = 16
KC = 8
DEPTH = 2
DFF = 2816
NFC = 22
EPS = 1e-6
N_CORES = 8


class Res:
    __slots__ = ("w", "r", "excl", "name")

    def __init__(self, name="", excl=False):
        self.w = None
        self.r = []
        self.excl = excl
        self.name = name


def run_interleaved(gens):
    active = list(gens)
    while active:
        for g in list(active):
            try:
                next(g)
            except StopIteration:
                active.remove(g)


class Sched:
    ENG = ("pe", "act", "dve", "pool", "sp")

    def __init__(self, nc, es):
        self.nc = nc
        self.es = es
        self.sems = {}
        self.cnt = {}
        for e in self.ENG:
            self.sems[e] = es.enter_context(nc.semaphore("sem_" + e))
            self.cnt[e] = 0
        self.waited = {e: {} for e in self.ENG}
        self.stream = {e: [] for e in self.ENG}
        self.nwaits = 0

    def dma_sem(self, name):
        self.sems[name] = self.es.enter_context(self.nc.semaphore("dq_" + name))
        self.cnt[name] = 0
        return name

    def _collect(self, eng, reads, writes):
        need = {}

        def add(tok, kind):
            if tok is None:
                return
            key, val = tok
            if key == eng:
                if eng in ("pe", "sp"):
                    return
            if self.waited[eng].get(key, 0) >= val:
                return
            if need.get(key, 0) < val:
                need[key] = val

        for res in reads:
            add(res.w, "raw")
            if res.excl:
                for t in res.r:
                    if t[0] != eng:
                        add(t, "war")
        for res in writes:
            add(res.w, "waw")
            for t in res.r:
                add(t, "war")
        for k, v in need.items():
            self.waited[eng][k] = v
        return list(need.items())

    def _commit(self, tok, reads, writes):
        for res in reads:
            res.r = [t for t in res.r if t[0] != tok[0]] + [tok]
        for res in writes:
            res.w = tok
            res.r = []

    def op(self, eng, fn, reads=(), writes=()):
        waits = self._collect(eng, reads, writes)
        self.cnt[eng] += 1
        tok = (eng, self.cnt[eng])
        sems = self.sems
        semh = sems[eng]
        self.nwaits += len(waits)

        def emit(h, waits=waits, fn=fn, semh=semh):
            for k, v in waits:
                h.wait_ge(sems[k], v)
            fn(h).then_inc(semh, 1)

        self.stream[eng].append(emit)
        self._commit(tok, reads, writes)
        return tok

    def dma(self, eng, semkey, fn, reads=(), writes=()):
        waits = self._collect(eng, reads, writes)
        self.cnt[semkey] += 16
        tok = (semkey, self.cnt[semkey])
        sems = self.sems
        semh = sems[semkey]

        def emit(h, waits=waits, fn=fn, semh=semh):
            for k, v in waits:
                h.wait_ge(sems[k], v)
            fn(h).then_inc(semh, 16)

        self.stream[eng].append(emit)
        self._commit(tok, reads, writes)
        return tok

    def wait_all(self, eng, toks):
        need = {}
        for key, val in toks:
            if need.get(key, 0) < val:
                need[key] = val
        waits = list(need.items())
        sems = self.sems

        def emit(h, waits=waits):
            for k, v in waits:
                h.wait_ge(sems[k], v)

        self.stream[eng].append(emit)

    def barrier(self, wait_dma=True):
        snap = dict(self.cnt)
        if not wait_dma:
            snap = {k: v for k, v in snap.items() if k in self.ENG}
        for e in self.ENG:
            waits = [(k, v) for k, v in snap.items() if v > 0 and k != e and self.waited[e].get(k, 0) < v]
            for k, v in waits:
                self.waited[e][k] = v
            if e in ("act", "dve", "pool") and snap[e] > 0:
                waits.append((e, snap[e]))
                self.waited[e][e] = snap[e]
            sems = self.sems

            def emit(h, waits=waits):
                for k, v in waits:
                    h.wait_ge(sems[k], v)

            self.stream[e].append(emit)

    def emit_all(self):
        nc = self.nc
        streams = self.stream
        self.stream = {e: [] for e in self.ENG}
        self._emit(streams)

    def _emit(self, streams):
        nc = self.nc
        self_stream = streams
        with nc.Block() as block:
            @block.tensor
            def _(h):
                for f in self_stream["pe"]:
                    f(h)

            @block.scalar
            def _(h):
                for f in self_stream["act"]:
                    f(h)

            @block.vector
            def _(h):
                for f in self_stream["dve"]:
                    f(h)

            @block.gpsimd
            def _(h):
                for f in self_stream["pool"]:
                    f(h)

            @block.sync
            def _(h):
                for f in self_stream["sp"]:
                    f(h)


def build_program(depth=DEPTH, stage=99, dbg=False):
    nc = bass.Bass("TRN2", target_bir_lowering=False)
    es = ExitStack()
    sc = Sched(nc, es)

    def dram_in(name, shape, dt=F32):
        return nc.dram_tensor(name, list(shape), dt, kind="ExternalInput").ap()

    def dram_out(name, shape, dt=F32):
        return nc.dram_tensor(name, list(shape), dt, kind="ExternalOutput").ap()

    uid = [0]

    def sb(e, name, shape, dt):
        uid[0] += 1
        return e.enter_context(nc.sbuf_tensor("sb%d_%s" % (uid[0], name), list(shape), dt))

    x_d = dram_in("x", [S, D])
    y_d = dram_out("y", [S, D])
    w_att_d = dram_in("w_att", [depth, 4, 128, KC, 640])
    ident_d = dram_in("ident", [128, 128])
    ropec_d = dram_in("rope_c", [128, S])
    ropes_d = dram_in("rope_s", [128, S])
    amask_d = dram_in("amask", [128, 1024])
    npre_d = dram_in("npre", [128, depth, 2, KC])
    w_dn_d = dram_in("w_dn", [depth, 4, 128, KC, 512])
    w_ba_d = dram_in("w_ba", [depth, 128, KC, 8])
    dn_cw_d = dram_in("dn_cw", [128, depth, 4, 3, 4])
    dn_hp_d = dram_in("dn_hp", [128, depth, 2, 4])
    dn_nwT_d = dram_in("dn_nwT", [depth, 128, 1])
    dn_c_d = dram_in("dn_c", [128, 8, 128])
    w_o_d = dram_in("w_o", [depth, 128, KC, D])
    npost_d = dram_in("npost", [128, depth, 2, D])
    w_f1_d = dram_in("w_f1", [depth, NFC, 128, KC, 256])
    w_f2_d = dram_in("w_f2", [depth, 128, NFC, D])
    fcw_d = dram_in("fcw", [128, depth, 2 * NFC, 4])
    dbg_d = {}
    if dbg:
        dbg_d["hT"] = dram_out("dbg_hT", [128, KC, S], BF16)
        dbg_d["catA"] = dram_out("dbg_catA", [128, 4, S], BF16)
        dbg_d["catD"] = dram_out("dbg_catD", [128, 4, S], BF16)
        dbg_d["xmid"] = dram_out("dbg_xmid", [S, D], F32)

    x_sb = sb(es, "x_sb", [128, NT, D], F32)
    hT_box = [None]
    ident = sb(es, "ident", [128, 128], BF16)
    npre = sb(es, "npre", [128, depth, 2, KC], F32)
    R_x = [Res("x%d" % t) for t in range(NT)]
    R_hT = [Res("hT%d" % t) for t in range(NT)]
    R_const = Res("const")

    psum = [es.enter_context(nc.psum_tensor("ps%d" % i, [128, 512], F32)) for i in range(8)]
    R_ps = [Res("ps%d" % i, excl=True) for i in range(8)]

    q_out = sc.dma_sem("out")
    q_ld = {"sp": [sc.dma_sem("ld%d" % i) for i in range(4)],
            "pool": [sc.dma_sem("ldp%d" % i) for i in range(3)]}
    ld_rr = {"sp": 0, "pool": 0}

    def phase_load(out_ap, in_ap, res, eng="sp"):
        qs = q_ld[eng]
        q = qs[ld_rr[eng] % len(qs)]
        ld_rr[eng] += 1
        return sc.dma(eng, q, lambda h: h.dma_start(out=out_ap, in_=in_ap), writes=[res])

    def end_phase(wait_dma=True):
        sc.barrier(wait_dma)
        sc.emit_all()

    xv = x_d.rearrange("(t p) d -> p t d", p=128)
    for t4 in range(4):
        qx = sc.dma_sem("xin%d" % t4)
        sc.dma("sp", qx, lambda h, t4=t4: h.dma_start(out=x_sb[:, t4 * 4:(t4 + 1) * 4, :], in_=xv[:, t4 * 4:(t4 + 1) * 4, :]),
               writes=R_x[t4 * 4:(t4 + 1) * 4])
    phase_load(ident[:], ident_d, R_const, eng="pool")
    phase_load(npre[:], npre_d, R_const)
    end_phase()

    def prenorm(l, which, hT, tiles, R_hTl):
        e = ExitStack()
        ss_all = sb(e, "ss_all", [128, NT], F32)
        rstd_all = sb(e, "rstd_all", [128, NT], F32)
        junk = sb(e, "junk", [128, D], BF16)
        hb = [sb(e, "hb%d" % i, [128, D], BF16) for i in range(2)]
        R_hb = [Res("hb%d" % i) for i in range(2)]
        R_ss = Res("ss")
        sc.op("dve", lambda h: h.memset(ss_all[:], 0.0), writes=[R_ss])
        for t in tiles:
            sc.op("act", lambda h, t=t: h.activation(out=junk[:], in_=x_sb[:, t, :], func=AF.Square,
                                                     accum_out=ss_all[:, t:t + 1]),
                  reads=[R_x[t]], writes=[R_ss])
        sc.op("dve", lambda h: h.tensor_scalar(out=rstd_all[:], in0=ss_all[:], scalar1=1.0 / D, scalar2=EPS,
                                               op0=ALU.mult, op1=ALU.add), reads=[R_ss], writes=[R_ss])
        sc.op("act", lambda h: h.activation(out=rstd_all[:], in_=rstd_all[:], func=AF.Sqrt),
              reads=[R_ss], writes=[R_ss])
        sc.op("dve", lambda h: h.reciprocal(out=rstd_all[:], in_=rstd_all[:]), reads=[R_ss], writes=[R_ss])
        for ti, t in enumerate(tiles):
            b = t % 2
            pb = 6 + (t % 2)
            sc.op("act", lambda h, t=t, b=b: h.activation(out=hb[b][:], in_=x_sb[:, t, :], func=AF.Copy,
                                                          scale=rstd_all[:, t:t + 1]),
                  reads=[R_x[t], R_ss], writes=[R_hb[b]])
            pst = psum[pb][:].bitcast(BF16)
            for k in range(KC):
                sc.op("pe", lambda h, k=k, b=b, pst=pst: h.transpose(pst[:, k * 128:(k + 1) * 128],
                                                                     hb[b][:, k * 128:(k + 1) * 128], ident[:]),
                      reads=[R_hb[b], R_const], writes=[R_ps[pb]])
            wv = npre[:, l, which, :].unsqueeze(2).to_broadcast([128, KC, 128])
            sc.op("dve", lambda h, ti=ti, pst=pst, wv=wv: h.tensor_tensor(
                out=hT[:, :, ti * 128:(ti + 1) * 128], in0=pst.rearrange("p (k c) -> p k c", k=KC), in1=wv, op=ALU.mult),
                reads=[R_ps[pb], R_const], writes=[R_hTl[ti]])
        end_phase(wait_dma=False)
        e.close()

    BR = ((1, 16), (4, 4), (16, 1))

    def attention_phase(l, catA, R_catA, pre=None):
        hT = hT_box[0]
        e = ExitStack()
        ropec = sb(e, "ropec", [128, S], F32)
        ropes = sb(e, "ropes", [128, S], F32)
        amask = sb(e, "amask", [128, 1024], BF16)
        R_tab = Res("tab")
        phase_load(ropec[:], ropec_d, R_tab)
        phase_load(ropes[:], ropes_d, R_tab)
        phase_load(amask[:], amask_d, R_tab, eng="pool")
        w_att = [sb(e, "w_att%d" % i, [128, KC, 640], BF16) for i in range(2)]
        R_watt = [Res("watt%d" % i) for i in range(2)]
        q_watt = [sc.dma_sem("watt%d_%d" % (l, i)) for i in range(2)]

        def load_watt(hp):
            i = hp % 2
            sc.dma("pool", q_watt[i], lambda h, i=i: h.dma_start(out=w_att[i][:], in_=w_att_d[l, hp]),
                   writes=[R_watt[i]])

        load_watt(0)
        if pre is not None:
            pre()
        qT = sb(e, "qT", [128, S], BF16)
        kTe = sb(e, "kTe", [128, S], BF16)
        kTo = sb(e, "kTo", [128, S], BF16)
        R_qT, R_kT = Res("qT"), Res("kT")
        vT = sb(e, "vT", [128, S], BF16)
        R_vT = Res("vT")
        rtmp = [sb(e, "rtmp%d" % i, [128, 2, 512], F32) for i in range(2)]
        R_rtmp = [Res("rtmp%d" % i) for i in range(2)]
        VX = sb(e, "vext", [128, 16, 2, 128], BF16)
        R_vext = Res("vext")
        pT = [sb(e, "pT%d" % i, [128, 512], BF16) for i in range(6)]
        R_pT = [Res("pT%d" % i) for i in range(6)]
        acc = sb(e, "acc", [128, 2, S], F32)
        R_acc = [Res("acc0"), Res("acc1")]

        sc.op("pool", lambda h: h.memset(kTe[64:128, :], 0.0), writes=[R_kT])
        sc.op("pool", lambda h: h.memset(kTo[0:64, :], 0.0), writes=[R_kT])
        sc.op("pool", lambda h: h.memset(VX[:, :, :, 64:128], 1.0), writes=[R_vext])

        def attention_unit(hp):
            W = w_att[hp % 2]
            RW = R_watt[hp % 2]
            for qk in range(2):
                for g in range(4):
                    pa, pb = (0, 1) if g % 2 == 0 else (2, 3)
                    for ci, pbk in ((0, pa), (1, pb)):
                        c0 = qk * 256 + ci * 128
                        for k in range(KC):
                            sc.op("pe", lambda h, k=k, c0=c0, pbk=pbk, g=g: h.matmul(
                                psum[pbk][:], lhsT=W[:, k, c0:c0 + 128], rhs=hT[:, k, g * 512:(g + 1) * 512],
                                start=(k == 0), stop=(k == KC - 1)),
                                reads=[RW] + R_hT[g * 4:(g + 1) * 4], writes=[R_ps[pbk]])
                    rb = g % 2
                    sc.op("dve", lambda h, pa=pa, g=g, rb=rb: h.tensor_tensor(
                        out=rtmp[rb][:, 0, :], in0=psum[pa][:], in1=ropec[:, g * 512:(g + 1) * 512], op=ALU.mult),
                        reads=[R_ps[pa], R_tab], writes=[R_rtmp[rb]])
                    sc.op("dve", lambda h, pb=pb, g=g, rb=rb: h.tensor_tensor(
                        out=rtmp[rb][:, 1, :], in0=psum[pb][:], in1=ropes[:, g * 512:(g + 1) * 512], op=ALU.mult),
                        reads=[R_ps[pb], R_tab], writes=[R_rtmp[rb]])
                    if qk == 0:
                        sc.op("dve", lambda h, g=g, rb=rb: h.tensor_tensor(
                            out=qT[:, g * 512:(g + 1) * 512], in0=rtmp[rb][:, 0, :], in1=rtmp[rb][:, 1, :],
                            op=ALU.add), reads=[R_rtmp[rb]], writes=[R_qT])
                    else:
                        sc.op("dve", lambda h, g=g, rb=rb: h.tensor_tensor(
                            out=kTe[0:64, g * 512:(g + 1) * 512], in0=rtmp[rb][0:64, 0, :],
                            in1=rtmp[rb][0:64, 1, :], op=ALU.add), reads=[R_rtmp[rb]], writes=[R_kT])
                        sc.op("dve", lambda h, g=g, rb=rb: h.tensor_tensor(
                            out=kTo[64:128, g * 512:(g + 1) * 512], in0=rtmp[rb][64:128, 0, :],
                            in1=rtmp[rb][64:128, 1, :], op=ALU.add), reads=[R_rtmp[rb]], writes=[R_kT])

            for g in range(4):
                pbk = g % 2
                for k in range(KC):
                    sc.op("pe", lambda h, k=k, pbk=pbk, g=g: h.matmul(
                        psum[pbk][:], lhsT=W[:, k, 512:640], rhs=hT[:, k, g * 512:(g + 1) * 512],
                        start=(k == 0), stop=(k == KC - 1)),
                        reads=[RW] + R_hT[g * 4:(g + 1) * 4], writes=[R_ps[pbk]])
                sc.op("act", lambda h, pbk=pbk, g=g: h.activation(out=vT[:, g * 512:(g + 1) * 512], in_=psum[pbk][:], func=AF.Copy),
                      reads=[R_ps[pbk]], writes=[R_vT])
            kTs = (kTe, kTo)

            def head_task(hh, bi, d, nb):
                kT_h = kTs[hh]
                sbank = (0, 1) if hh == 0 else (2, 3)
                obank = (4, 5) if hh == 0 else (6, 7)
                pTs = pT[hh * 3:hh * 3 + 3]
                RpTs = R_pT[hh * 3:hh * 3 + 3]
                steps = []
                oi = 0
                for r in range(d):
                    for ob in range(0, nb, 4):
                        nq = min(4, nb - ob)
                        for half in range(0, nq, 2):
                            steps.append((r, ob, nq, half, min(2, nq - half), oi, half + 2 >= nq))
                        oi += 1
                pending = None

                def emit_qk(i, st):
                    r, ob, nq, half, n2, oi, last = st
                    pss = sbank[i % 2]
                    pti = i % 3
                    for jj in range(n2):
                        bq = ob + half + jj
                        q0 = r + d * 128 * bq
                        qsl = qT[:, q0:q0 + d * 127 + 1:d]
                        if bq > 0:
                            k0 = r + d * 128 * (bq - 1)
                            sc.op("pe", lambda h, jj=jj, k0=k0, qsl=qsl, pss=pss: h.matmul(
                                psum[pss][:, jj * 256:jj * 256 + 128],
                                lhsT=kT_h[:, k0:k0 + d * 127 + 1:d], rhs=qsl, start=True, stop=False),
                                reads=[R_kT, R_qT], writes=[R_ps[pss]])
                            sc.op("pe", lambda h, jj=jj, pss=pss: h.matmul(
                                psum[pss][:, jj * 256:jj * 256 + 128], lhsT=ident[:], rhs=amask[:, 0:128],
                                start=False, stop=True), reads=[R_const, R_tab], writes=[R_ps[pss]])
                        k0 = r + d * 128 * bq
                        sc.op("pe", lambda h, jj=jj, k0=k0, qsl=qsl, pss=pss: h.matmul(
                            psum[pss][:, jj * 256 + 128:jj * 256 + 256],
                            lhsT=kT_h[:, k0:k0 + d * 127 + 1:d], rhs=qsl, start=True, stop=False),
                            reads=[R_kT, R_qT], writes=[R_ps[pss]])
                        sc.op("pe", lambda h, jj=jj, pss=pss: h.matmul(
                            psum[pss][:, jj * 256 + 128:jj * 256 + 256], lhsT=ident[:], rhs=amask[:, 128:256],
                            start=False, stop=True), reads=[R_const, R_tab], writes=[R_ps[pss]])
                    c_lo = 128 if (ob + half == 0) else 0
                    c_hi = n2 * 256
                    sc.op("act", lambda h, pss=pss, pti=pti, c_lo=c_lo, c_hi=c_hi: h.activation(
                        out=pTs[pti][:, c_lo:c_hi], in_=psum[pss][:, c_lo:c_hi], func=AF.Exp, scale=0.125),
                        reads=[R_ps[pss]], writes=[RpTs[pti]])

                def emit_pv(i, st):
                    r, ob, nq, half, n2, oi, last = st
                    pti = i % 3
                    pso = obank[oi % 2]
                    for jj in range(n2):
                        bq = ob + half + jj
                        oc = (half + jj) * 128
                        if bq > 0:
                            sc.op("pe", lambda h, jj=jj, bq=bq, oc=oc, pso=pso, pti=pti, r=r: h.matmul(
                                psum[pso][:, oc:oc + 128], lhsT=VX[:, r * nb + bq - 1, hh, :],
                                rhs=pTs[pti][:, jj * 256:jj * 256 + 128], start=True, stop=False),
                                reads=[R_vext, RpTs[pti]], writes=[R_ps[pso]])
                        sc.op("pe", lambda h, jj=jj, bq=bq, oc=oc, pso=pso, pti=pti, r=r: h.matmul(
                            psum[pso][:, oc:oc + 128], lhsT=VX[:, r * nb + bq, hh, :],
                            rhs=pTs[pti][:, jj * 256 + 128:jj * 256 + 256], start=(bq == 0), stop=True),
                            reads=[R_vext, RpTs[pti]], writes=[R_ps[pso]])
                    if last:
                        t0 = r + d * 128 * ob
                        n = nq * 128
                        dst = acc[:, hh, t0:t0 + d * (n - 1) + 1:d]
                        if bi == 0:
                            sc.op("act", lambda h, dst=dst, pso=pso, n=n: h.activation(
                                out=dst, in_=psum[pso][:, 0:n], func=AF.Copy),
                                reads=[R_ps[pso]], writes=[R_acc[hh]])
                        else:
                            sc.op("dve", lambda h, dst=dst, pso=pso, n=n: h.tensor_tensor(
                                out=dst, in0=psum[pso][:, 0:n], in1=dst, op=ALU.add),
                                reads=[R_ps[pso], R_acc[hh]], writes=[R_acc[hh]])

                for i, st in enumerate(steps):
                    emit_qk(i, st)
                    if pending is not None:
                        emit_pv(*pending)
                    pending = (i, st)
                    yield
                emit_pv(*pending)
                yield

            for bi, (d, nb) in enumerate(BR):
                for q4 in range(4):
                    pbk = q4 % 2
                    pvb = psum[pbk][:].bitcast(BF16)
                    for j in range(4):
                        blk = q4 * 4 + j
                        r, kb = blk // nb, blk % nb
                        t0 = r + d * 128 * kb
                        sc.op("pe", lambda h, j=j, t0=t0, pvb=pvb, d=d: h.transpose(
                            pvb[:, j * 128:(j + 1) * 128], vT[:, t0:t0 + d * 127 + 1:d], ident[:]),
                            reads=[R_vT, R_const], writes=[R_ps[pbk]])
                    sc.op("act", lambda h, q4=q4, pvb=pvb: h.activation(
                        out=VX[:, q4 * 4:(q4 + 1) * 4, :, 0:64],
                        in_=pvb[:, 0:512].rearrange("p (j h c) -> p j h c", j=4, h=2), func=AF.Copy),
                        reads=[R_ps[pbk]], writes=[R_vext])
                run_interleaved([head_task(0, bi, d, nb), head_task(1, bi, d, nb)])
            for hh in range(2):
                for g in range(4):
                    rb = g % 2
                    rlv = rtmp[rb][0:64, 0, :]
                    sc.op("act", lambda h, hh=hh, g=g, rlv=rlv: h.activation(
                        out=rlv, in_=acc[64:128, hh, g * 512:(g + 1) * 512], func=AF.Ln),
                        reads=[R_acc[hh]], writes=[R_rtmp[rb]])
                    sc.op("act", lambda h, rlv=rlv: h.activation(out=rlv, in_=rlv, func=AF.Exp, scale=-1.0),
                          reads=[R_rtmp[rb]], writes=[R_rtmp[rb]])
                    sc.op("pool", lambda h, hh=hh, g=g, rlv=rlv: h.tensor_tensor(
                        out=catA[64 * hh:64 * hh + 64, hp, g * 512:(g + 1) * 512],
                        in0=acc[0:64, hh, g * 512:(g + 1) * 512], in1=rlv, op=ALU.mult),
                        reads=[R_acc[hh], R_rtmp[rb]], writes=[R_catA[hp]])

        for hp in range(4):
            if hp < 3:
                load_watt(hp + 1)
            attention_unit(hp)
        end_phase()
        e.close()


    def deltanet_phase(l, catD, R_catD):
        hT = hT_box[0]
        e = ExitStack()
        C = sb(e, "dn_c", [128, 8, 128], F32)
        M1, M2, maskS, maskTu, Mblk, Mc0, Mc1, identf = [C[:, i, :] for i in range(8)]
        R_c = Res("dnc")
        phase_load(C[:], dn_c_d, R_c)
        cw = sb(e, "dn_cw", [128, 4, 3, 4], F32)
        hp = sb(e, "dn_hp", [128, 2, 4], F32)
        wba = sb(e, "w_ba", [128, KC, 8], BF16)
        R_c2 = Res("dnc2")
        phase_load(cw[:], dn_cw_d[:, l], R_c2)
        phase_load(hp[:], dn_hp_d[:, l], R_c2)
        R_wba = Res("wba")
        phase_load(wba[:], w_ba_d[l], R_wba, eng="pool")
        onesb = sb(e, "onesb", [128, 1], BF16)
        sc.op("pool", lambda h: h.memset(onesb[:], 1.0), writes=[R_c2])

        Wd = sb(e, "w_dn", [128, KC, 512], BF16)
        R_Wd = Res("wdn")
        q_wdn = sc.dma_sem("wdn_%d" % l)

        def load_wdn(hd):
            sc.dma("pool", q_wdn, lambda h: h.dma_start(out=Wd[:], in_=w_dn_d[l, hd]), writes=[R_Wd])

        SC = sb(e, "dn_sc", [128, 8, NT, 4], F32)
        beta, xa, graw, egc, ekd, tmpa = [SC[:, i] for i in range(6)]
        egl = SC[:, 6:8]
        R_sc = Res("dnsc")
        expA = sb(e, "expA", [128, 4], F32)
        pb = psum[0]
        for t in range(NT):
            for k in range(KC):
                sc.op("pe", lambda h, t=t, k=k: h.matmul(pb[:, t * 8:(t + 1) * 8], lhsT=hT[:, k, t * 128:(t + 1) * 128],
                                                         rhs=wba[:, k, :], start=(k == 0), stop=(k == KC - 1)),
                      reads=[R_hT[t], R_wba], writes=[R_ps[0]])
        bav = pb[:, 0:128].rearrange("p (t c) -> p t c", c=8)
        sc.op("act", lambda h: h.activation(out=beta, in_=bav[:, :, 0:4], func=AF.Sigmoid),
              reads=[R_ps[0]], writes=[R_sc])
        sc.op("dve", lambda h: h.tensor_tensor(out=xa, in0=bav[:, :, 4:8],
                                               in1=hp[:, 1, :].unsqueeze(1).to_broadcast([128, NT, 4]), op=ALU.add),
              reads=[R_ps[0], R_c2], writes=[R_sc])
        sc.op("act", lambda h: h.activation(out=xa, in_=xa, func=AF.Exp), reads=[R_sc], writes=[R_sc])
        sc.op("act", lambda h: h.activation(out=xa, in_=xa, func=AF.Ln, bias=1.0), reads=[R_sc], writes=[R_sc])
        sc.op("act", lambda h: h.activation(out=expA[:], in_=hp[:, 0, :], func=AF.Exp), reads=[R_c2], writes=[R_sc])
        sc.op("dve", lambda h: h.scalar_tensor_tensor(out=graw, in0=xa, scalar=-1.0,
                                                      in1=expA[:].unsqueeze(1).to_broadcast([128, NT, 4]),
                                                      op0=ALU.mult, op1=ALU.mult), reads=[R_sc], writes=[R_sc])
        grf = graw.rearrange("p t c -> p (t c)")
        pg = psum[1]
        for i, Mx in enumerate((M1, Mblk, Mc0, Mc1)):
            sc.op("pe", lambda h, i=i, Mx=Mx: h.matmul(pg[:, i * 64:(i + 1) * 64], lhsT=Mx, rhs=grf, start=True, stop=True),
                  reads=[R_c, R_sc], writes=[R_ps[1]])
        pgv = pg[:, 0:256].rearrange("p (i t c) -> p i t c", i=4, c=4)
        sc.op("act", lambda h: h.activation(out=egc, in_=pgv[:, 0], func=AF.Exp), reads=[R_ps[1]], writes=[R_sc])
        sc.op("dve", lambda h: h.tensor_copy(out=tmpa, in_=pgv[:, 0]), reads=[R_ps[1]], writes=[R_sc])
        sc.op("dve", lambda h: h.tensor_tensor(out=ekd, in0=pgv[:, 1], in1=tmpa, op=ALU.subtract),
              reads=[R_ps[1], R_sc], writes=[R_sc])
        sc.op("act", lambda h: h.activation(out=ekd, in_=ekd, func=AF.Exp), reads=[R_sc], writes=[R_sc])
        sc.op("act", lambda h: h.activation(out=SC[:, 6:8].rearrange("p a t c -> p (a t c)"), in_=pg[:, 128:256], func=AF.Exp),
              reads=[R_ps[1]], writes=[R_sc])

        xpad = [sb(e, "xpad%d" % i, [128, 3 + 512], BF16) for i in range(2)]
        R_xpad = [Res("xpad%d" % i) for i in range(2)]
        dg = sb(e, "dgw", [128, 12, 128], BF16)
        R_dg = Res("dg")
        csT = [sb(e, "csT%d" % i, [128, S], BF16) for i in range(3)]
        R_csT = [Res("csT%d" % i) for i in range(3)]
        sqt = sb(e, "sq", [128, S], BF16)
        sq = sqt[:]
        R_sqall = [Res("sq")]
        zw2 = [sb(e, "zw%d" % i, [128, NT, 128], BF16) for i in range(2)]
        R_zw2 = [Res("zw%d" % i) for i in range(2)]
        HS = sb(e, "dn_hs", [128, 6, NT], F32)
        R_hs = Res("hs")
        NH = 2
        GMh, R_GMh, ABh, R_ABh = [], [], [], []
        for hf_ in range(2):
            gmb = [sb(e, "gm%d_%d" % (hf_, i), [128, NH, 128], F32) for i in range(6)]
            rgm = [Res("gm%d_%d" % (hf_, i)) for i in range(6)]
            GMh.append([gmb[0], gmb[1], gmb[2], gmb[3], gmb[0], gmb[4], gmb[1], gmb[5], gmb[2]])
            R_GMh.append([rgm[0], rgm[1], rgm[2], rgm[3], rgm[0], rgm[4], rgm[1], rgm[5], rgm[2]])
            ABh.append([sb(e, "ab%d_%d" % (hf_, i), [128, NH, 128], BF16) for i in range(4)])
            R_ABh.append([Res("ab%d_%d" % (hf_, i)) for i in range(4)])
        HB = [[[sb(e, "hb%d_%d_%d" % (p, hf_, i), [128, NH, 128], BF16) for i in range(7)] for hf_ in range(2)]
              for p in range(2)]
        R_HB = [[[Res("hb%d_%d_%d" % (p, hf_, i)) for i in range(7)] for hf_ in range(2)] for p in range(2)]
        Sf = sb(e, "Sf", [128, 128], F32)
        Sb = sb(e, "Sb", [128, 128], BF16)
        R_S = Res("S")
        vnew = sb(e, "vnew", [128, 128], BF16)
        R_vnew = Res("vnew")
        p2s = [sb(e, "p2s%d" % i, [128, 128], F32) for i in range(2)]
        R_p2s = [Res("p2s%d" % i) for i in range(2)]
        osb = sb(e, "osb", [128, 4, 128], F32)
        R_osb = Res("osb")
        otk = sb(e, "otk", [128, 4, 128], BF16)
        R_otk = Res("otk")
        oss = sb(e, "oss", [128, 8], F32)
        R_oss = Res("oss")
        junk = sb(e, "junkd", [128, 128], BF16)
        sc.op("pool", lambda h: h.memset(oss[:], 0.0), writes=[R_oss])
        for p in range(2):
            for hf_ in range(2):
                sc.op("pool", lambda h, p=p, hf_=hf_: h.memset(HB[p][hf_][4][:], 0.0), writes=[R_HB[p][hf_][4]])
                sc.op("pool", lambda h, p=p, hf_=hf_: h.memset(HB[p][hf_][5][:], 0.0), writes=[R_HB[p][hf_][5]])

        load_wdn(0)

        def stage_C(hd):
            zw = zw2[hd % 2]
            R_zw = R_zw2[hd % 2]
            for cc in range(3):
                for j in range(4):
                    sc.op("dve", lambda h, cc=cc, j=j: h.tensor_scalar(
                        out=dg[:, cc * 4 + j, :], in0=ident[:], scalar1=cw[:, hd, cc, j:j + 1], scalar2=None, op0=ALU.mult),
                        reads=[R_const, R_c2], writes=[R_dg])
            yield
            for cc in range(3):
                for g in range(4):
                    pj = g % 2
                    pc = 2 + (g % 2)
                    xb = xpad[(cc * 4 + g) % 2]
                    Rxb = R_xpad[(cc * 4 + g) % 2]
                    xprev = xpad[(cc * 4 + g + 1) % 2]
                    Rxprev = R_xpad[(cc * 4 + g + 1) % 2]
                    for k in range(KC):
                        sc.op("pe", lambda h, k=k, cc=cc, g=g, pj=pj: h.matmul(
                            psum[pj][:], lhsT=Wd[:, k, cc * 128:(cc + 1) * 128], rhs=hT[:, k, g * 512:(g + 1) * 512],
                            start=(k == 0), stop=(k == KC - 1)), reads=[R_Wd] + R_hT[g * 4:(g + 1) * 4], writes=[R_ps[pj]])
                    sc.op("act", lambda h, pj=pj, xb=xb: h.activation(out=xb[:, 3:515], in_=psum[pj][:], func=AF.Copy),
                          reads=[R_ps[pj]], writes=[Rxb])
                    if g == 0:
                        sc.op("pool", lambda h, xb=xb: h.memset(xb[:, 0:3], 0.0), writes=[Rxb])
                    else:
                        sc.op("pool", lambda h, xb=xb, xprev=xprev: h.tensor_copy(out=xb[:, 0:3], in_=xprev[:, 512:515]),
                              reads=[Rxprev], writes=[Rxb])
                    for j in range(4):
                        sc.op("pe", lambda h, j=j, cc=cc, pc=pc, xb=xb: h.matmul(
                            psum[pc][:], lhsT=dg[:, cc * 4 + j, :], rhs=xb[:, j:j + 512], start=(j == 0), stop=(j == 3)),
                            reads=[R_dg, Rxb], writes=[R_ps[pc]])
                    sc.op("act", lambda h, pc=pc, cc=cc, g=g: h.activation(
                        out=csT[cc][:, g * 512:(g + 1) * 512], in_=psum[pc][:], func=AF.Silu),
                        reads=[R_ps[pc]], writes=[R_csT[cc]])
                    yield
            for t4 in range(4):
                pz = 2 + (t4 % 2)
                for j in range(4):
                    t = t4 * 4 + j
                    for k in range(KC):
                        sc.op("pe", lambda h, t=t, j=j, k=k, pz=pz: h.matmul(
                            psum[pz][:, j * 128:(j + 1) * 128], lhsT=hT[:, k, t * 128:(t + 1) * 128],
                            rhs=Wd[:, k, 384:512], start=(k == 0), stop=(k == KC - 1)),
                            reads=[R_Wd, R_hT[t]], writes=[R_ps[pz]])
                sc.op("act", lambda h, t4=t4, pz=pz: h.activation(
                    out=zw[:, t4 * 4:(t4 + 1) * 4, :], in_=psum[pz][:].rearrange("p (j c) -> p j c", j=4), func=AF.Silu),
                    reads=[R_ps[pz]], writes=[R_zw])
                yield
            if hd < 3:
                load_wdn(hd + 1)
            pss = psum[4]
            for qi in range(2):
                if qi == 0:
                    sc.op("act", lambda h, qi=qi: h.activation(out=sq, in_=csT[qi][:], func=AF.Square),
                          reads=[R_csT[qi]], writes=R_sqall)
                else:
                    sc.op("dve", lambda h, qi=qi: h.tensor_tensor(out=sq, in0=csT[qi][:], in1=csT[qi][:], op=ALU.mult),
                          reads=[R_csT[qi]], writes=R_sqall)
                for t in range(NT):
                    sc.op("pe", lambda h, t=t, qi=qi: h.matmul(
                        pss[:, qi * 16 + t:qi * 16 + t + 1], lhsT=sq[:, t * 128:(t + 1) * 128], rhs=onesb[:],
                        start=True, stop=True), reads=R_sqall + [R_c2], writes=[R_ps[4]])
                yield
            rqk = HS[:, 0:2, :]
            sc.op("dve", lambda h: h.tensor_scalar(out=rqk, in0=pss[:, 0:32].rearrange("p (a t) -> p a t", a=2),
                                                   scalar1=EPS, scalar2=None, op0=ALU.add),
                  reads=[R_ps[4]], writes=[R_hs])
            sc.op("act", lambda h: h.activation(out=rqk, in_=rqk, func=AF.Sqrt), reads=[R_hs], writes=[R_hs])
            sc.op("dve", lambda h: h.reciprocal(out=rqk, in_=rqk), reads=[R_hs], writes=[R_hs])
            sc.op("dve", lambda h: h.tensor_scalar(out=HS[:, 0, :], in0=HS[:, 0, :], scalar1=float(128 ** -0.5),
                                                   scalar2=None, op0=ALU.mult), reads=[R_hs], writes=[R_hs])
            sc.op("dve", lambda h, hd=hd: h.tensor_tensor(out=HS[:, 4, :], in0=HS[:, 1, :], in1=beta[:, :, hd], op=ALU.mult),
                  reads=[R_hs, R_sc], writes=[R_hs])
            sc.op("dve", lambda h, hd=hd: h.tensor_tensor(out=HS[:, 2, :], in0=HS[:, 4, :], in1=egc[:, :, hd], op=ALU.mult),
                  reads=[R_hs, R_sc], writes=[R_hs])
            sc.op("dve", lambda h, hd=hd: h.tensor_tensor(out=HS[:, 3, :], in0=HS[:, 1, :], in1=ekd[:, :, hd], op=ALU.mult),
                  reads=[R_hs, R_sc], writes=[R_hs])
            sc.op("dve", lambda h, hd=hd: h.tensor_scalar(out=HS[:, 5, :], in0=beta[:, :, hd], scalar1=-1.0, scalar2=None,
                                                          op0=ALU.mult), reads=[R_sc], writes=[R_hs])
            yield

        a_rr = [0, 0]

        def stage_A(G, hd, half):
            par = G % 2
            M2g, Dm, DTm, Pa, Pb, Ra, Rb, Ya, Yb = GMh[half]
            R_M2g, R_Dm, R_DTm, R_Pa, R_Pb, R_Ra, R_Rb, R_Ya, R_Yb = R_GMh[half]
            KT, kbg, knt, qnt = ABh[half]
            R_KT, R_kbg, R_knt, R_qnt = R_ABh[half]
            TT, WnT, qkT, QT, kd0, kd1, vbt = HB[par][half]
            R_TT, R_WnT, R_qkT, R_QT, R_kd0, R_kd1, R_vbt = R_HB[par][half]
            T0 = G * 4 + half * NH
            NC2 = NH * 128

            def abank():
                b = half * 3 + a_rr[half] % 3
                a_rr[half] += 1
                return b
            for j in range(NH):
                t = T0 + j
                bk = abank()
                ptv = psum[bk][:].bitcast(BF16)
                Rp = R_ps[bk]
                for ci in range(3):
                    sc.op("pe", lambda h, ci=ci, t=t, ptv=ptv: h.transpose(
                        ptv[:, ci * 128:(ci + 1) * 128], csT[ci][:, t * 128:(t + 1) * 128], ident[:]),
                        reads=[R_csT[ci], R_const], writes=[Rp])
                qv, kv, vv = ptv[:, 0:128], ptv[:, 128:256], ptv[:, 256:384]
                sc.op("act", lambda h, j=j, t=t, qv=qv: h.activation(out=qnt[:, j, :], in_=qv, func=AF.Copy,
                                                                     scale=HS[:, 0, t:t + 1]),
                      reads=[Rp, R_hs], writes=[R_qnt])
                sc.op("dve", lambda h, j=j, t=t, kv=kv: h.tensor_scalar(out=knt[:, j, :], in0=kv, scalar1=HS[:, 1, t:t + 1],
                                                                        scalar2=None, op0=ALU.mult),
                      reads=[Rp, R_hs], writes=[R_knt])
                sc.op("act", lambda h, j=j, t=t, kv=kv: h.activation(out=kbg[:, j, :], in_=kv, func=AF.Copy,
                                                                     scale=HS[:, 2, t:t + 1]),
                      reads=[Rp, R_hs], writes=[R_kbg])
                sc.op("dve", lambda h, j=j, t=t, kv=kv: h.tensor_scalar(out=kd0[0:64, j, :], in0=kv[0:64, :],
                                                                        scalar1=HS[0:64, 3, t:t + 1], scalar2=None,
                                                                        op0=ALU.mult),
                      reads=[Rp, R_hs], writes=[R_kd0])
                sc.op("dve", lambda h, j=j, t=t, kv=kv: h.tensor_scalar(out=kd1[64:128, j, :], in0=kv[64:128, :],
                                                                        scalar1=HS[64:128, 3, t:t + 1], scalar2=None,
                                                                        op0=ALU.mult),
                      reads=[Rp, R_hs], writes=[R_kd1])
                sc.op("act", lambda h, j=j, t=t, vv=vv: h.activation(out=vbt[:, j, :], in_=vv, func=AF.Copy,
                                                                     scale=beta[:, t, hd:hd + 1]),
                      reads=[Rp, R_sc], writes=[R_vbt])
                yield
            bk = abank()
            pkq = psum[bk][:].bitcast(BF16)
            for j in range(NH):
                sc.op("pe", lambda h, j=j: h.transpose(pkq[:, j * 128:(j + 1) * 128], knt[:, j, :], ident[:]),
                      reads=[R_knt, R_const], writes=[R_ps[bk]])
                sc.op("pe", lambda h, j=j: h.transpose(pkq[:, NC2 + j * 128:NC2 + (j + 1) * 128], qnt[:, j, :], ident[:]),
                      reads=[R_qnt, R_const], writes=[R_ps[bk]])
            sc.op("act", lambda h: h.activation(out=KT[:].rearrange("p j c -> p (j c)"), in_=pkq[:, 0:NC2], func=AF.Copy),
                  reads=[R_ps[bk]], writes=[R_KT])
            sc.op("dve", lambda h: h.tensor_copy(out=QT[:].rearrange("p j c -> p (j c)"), in_=pkq[:, NC2:2 * NC2]),
                  reads=[R_ps[bk]], writes=[R_QT])
            yield
            sc.op("dve", lambda h: h.tensor_tensor(
                out=M2g[:], in0=M2.unsqueeze(1).to_broadcast([128, NH, 128]),
                in1=graw[:, T0:T0 + NH, hd:hd + 1].to_broadcast([128, NH, 128]), op=ALU.mult),
                reads=[R_c, R_sc], writes=[R_M2g])
            bg, bt = abank(), abank()
            sc.op("pe", lambda h: h.matmul(psum[bg][:, 0:NC2], lhsT=M1, rhs=M2g[:].rearrange("p j c -> p (j c)"),
                                           start=True, stop=False), reads=[R_c, R_M2g], writes=[R_ps[bg]])
            for j in range(NH):
                sc.op("pe", lambda h, j=j: h.matmul(psum[bg][:, j * 128:(j + 1) * 128], lhsT=identf, rhs=maskS,
                                                    start=False, stop=(j == NH - 1)), reads=[R_c], writes=[R_ps[bg]])
            for j in range(NH):
                sc.op("pe", lambda h, j=j: h.matmul(psum[bt][:, j * 128:(j + 1) * 128], lhsT=M2g[:, j, :], rhs=M1,
                                                    start=True, stop=False), reads=[R_c, R_M2g], writes=[R_ps[bt]])
                sc.op("pe", lambda h, j=j: h.matmul(psum[bt][:, j * 128:(j + 1) * 128], lhsT=identf, rhs=maskTu,
                                                    start=False, stop=True), reads=[R_c], writes=[R_ps[bt]])
            sc.op("act", lambda h: h.activation(out=Dm[:].rearrange("p j c -> p (j c)"), in_=psum[bg][:, 0:NC2], func=AF.Exp),
                  reads=[R_ps[bg]], writes=[R_Dm])
            sc.op("act", lambda h: h.activation(out=DTm[:].rearrange("p j c -> p (j c)"), in_=psum[bt][:, 0:NC2], func=AF.Exp),
                  reads=[R_ps[bt]], writes=[R_DTm])
            yield
            bs, bq = abank(), abank()
            for j in range(NH):
                sc.op("pe", lambda h, j=j: h.matmul(psum[bs][:, j * 128:(j + 1) * 128], lhsT=KT[:, j, :], rhs=KT[:, j, :],
                                                    start=True, stop=True), reads=[R_KT], writes=[R_ps[bs]])
            for j in range(NH):
                sc.op("pe", lambda h, j=j: h.matmul(psum[bq][:, j * 128:(j + 1) * 128], lhsT=KT[:, j, :], rhs=QT[:, j, :],
                                                    start=True, stop=True), reads=[R_KT, R_QT], writes=[R_ps[bq]])
            for j in range(NH):
                sc.op("dve", lambda h, j=j: h.scalar_tensor_tensor(
                    out=Pa[:, j, :], in0=psum[bs][:, j * 128:(j + 1) * 128], scalar=HS[:, 5, T0 + j:T0 + j + 1],
                    in1=Dm[:, j, :], op0=ALU.mult, op1=ALU.mult), reads=[R_ps[bs], R_Dm, R_hs], writes=[R_Pa])
            sc.op("dve", lambda h: h.tensor_tensor(out=qkT[:].rearrange("p j c -> p (j c)"), in0=psum[bq][:, 0:NC2],
                                                   in1=DTm[:].rearrange("p j c -> p (j c)"), op=ALU.mult),
                  reads=[R_ps[bq], R_DTm], writes=[R_qkT])
            yield
            br = abank()
            for j in range(NH):
                sc.op("pe", lambda h, j=j: h.transpose(psum[br][:, j * 128:(j + 1) * 128], Pa[:, j, :], identf),
                      reads=[R_Pa, R_c], writes=[R_ps[br]])
            sc.op("act", lambda h: h.activation(out=Ra[:].rearrange("p j c -> p (j c)"), in_=psum[br][:, 0:NC2], func=AF.Copy),
                  reads=[R_ps[br]], writes=[R_Ra])
            sc.op("dve", lambda h: h.tensor_tensor(out=Ya[:], in0=psum[br][:, 0:NC2].rearrange("p (j c) -> p j c", j=NH),
                                                   in1=identf.unsqueeze(1).to_broadcast([128, NH, 128]),
                                                   op=ALU.add), reads=[R_ps[br], R_c], writes=[R_Ya])
            yield
            Pc, Pn, Rc, Rn, Yc, Yn = Pa, Pb, Ra, Rb, Ya, Yb
            RPc, RPn, RRc, RRn, RYc, RYn = R_Pa, R_Pb, R_Ra, R_Rb, R_Ya, R_Yb
            NL = 5
            for lev in range(1, NL + 1):
                bp = abank()
                for j in range(NH):
                    sc.op("pe", lambda h, j=j, Rc=Rc, Pc=Pc, bp=bp: h.matmul(psum[bp][:, j * 128:(j + 1) * 128], lhsT=Rc[:, j, :],
                                                                             rhs=Pc[:, j, :], start=True, stop=True),
                          reads=[RRc, RPc], writes=[R_ps[bp]])
                if lev < NL:
                    brr = abank()
                    for j in range(NH):
                        sc.op("pe", lambda h, j=j, Rc=Rc, Pc=Pc, brr=brr: h.matmul(psum[brr][:, j * 128:(j + 1) * 128], lhsT=Pc[:, j, :],
                                                                                   rhs=Rc[:, j, :], start=True, stop=True),
                              reads=[RRc, RPc], writes=[R_ps[brr]])
                sc.op("act", lambda h, Pn=Pn, bp=bp: h.activation(out=Pn[:].rearrange("p j c -> p (j c)"), in_=psum[bp][:, 0:NC2], func=AF.Copy),
                      reads=[R_ps[bp]], writes=[RPn])
                if lev < NL:
                    sc.op("act", lambda h, Rn=Rn, brr=brr: h.activation(out=Rn[:].rearrange("p j c -> p (j c)"), in_=psum[brr][:, 0:NC2],
                                                                        func=AF.Copy), reads=[R_ps[brr]], writes=[RRn])
                yield
                by = abank()
                for j in range(NH):
                    sc.op("pe", lambda h, j=j, Pn=Pn, Yc=Yc, by=by: h.matmul(psum[by][:, j * 128:(j + 1) * 128], lhsT=Pn[:, j, :],
                                                                             rhs=Yc[:, j, :], start=True, stop=True),
                          reads=[RPn, RYc], writes=[R_ps[by]])
                if lev < NL:
                    sc.op("dve", lambda h, Yn=Yn, Yc=Yc, by=by: h.tensor_tensor(out=Yn[:].rearrange("p j c -> p (j c)"), in0=psum[by][:, 0:NC2],
                                                                                in1=Yc[:].rearrange("p j c -> p (j c)"), op=ALU.add),
                          reads=[R_ps[by], RYc], writes=[RYn])
                else:
                    sc.op("dve", lambda h, Yc=Yc, by=by: h.tensor_tensor(out=TT[:].rearrange("p j c -> p (j c)"), in0=psum[by][:, 0:NC2],
                                                                         in1=Yc[:].rearrange("p j c -> p (j c)"), op=ALU.add),
                          reads=[R_ps[by], RYc], writes=[R_TT])
                Pc, Pn, Rc, Rn, Yc, Yn = Pn, Pc, Rn, Rc, Yn, Yc
                RPc, RPn, RRc, RRn, RYc, RYn = RPn, RPc, RRn, RRc, RYn, RYc
                yield
            bw = abank()
            for j in range(NH):
                sc.op("pe", lambda h, j=j: h.matmul(psum[bw][:, j * 128:(j + 1) * 128], lhsT=kbg[:, j, :], rhs=TT[:, j, :],
                                                    start=True, stop=True), reads=[R_kbg, R_TT], writes=[R_ps[bw]])
            sc.op("act", lambda h: h.activation(out=WnT[:].rearrange("p j c -> p (j c)"), in_=psum[bw][:, 0:NC2], func=AF.Copy,
                                                scale=-1.0), reads=[R_ps[bw]], writes=[R_WnT])
            yield

        def stage_B(G, hd):
            zw = zw2[hd % 2]
            R_zw = R_zw2[hd % 2]
            if G == 0:
                sc.op("pool", lambda h: h.memset(Sf[:], 0.0), writes=[R_S])
                sc.op("pool", lambda h: h.memset(Sb[:], 0.0), writes=[R_S])
            par = G % 2
            T0 = G * 4
            pA, pB = psum[7], psum[6]
            pC = psum[7][:, 256:384]
            RA, RB, RC = R_ps[7], R_ps[6], R_ps[7]
            def tile_steps(j4):
                t = T0 + j4
                half, j = j4 // NH, j4 % NH
                TT, WnT, qkT, QT, kd0, kd1, vbt = HB[par][half]
                R_TT, R_WnT, R_qkT, R_QT, R_kd0, R_kd1, R_vbt = R_HB[par][half]
                for c in range(2):
                    kd = kd0 if c == 0 else kd1
                    Rkd = R_kd0 if c == 0 else R_kd1
                    sc.op("pe", lambda h, j=j, c=c: h.matmul(pA[:, c * 128:(c + 1) * 128], lhsT=TT[:, j, :], rhs=vbt[:, j, :],
                                                             start=True, stop=False),
                          reads=[R_TT, R_vbt], writes=[RA])
                    sc.op("pe", lambda h, j=j, c=c: h.matmul(pA[:, c * 128:(c + 1) * 128], lhsT=WnT[:, j, :], rhs=Sb[:],
                                                             start=False, stop=True),
                          reads=[R_WnT, R_S], writes=[RA])
                    if c == 0:
                        sc.op("dve", lambda h: h.tensor_copy(out=vnew[:], in_=pA[:, 0:128]),
                              reads=[RA], writes=[R_vnew])
                    else:
                        sc.op("dve", lambda h: h.tensor_copy(out=vnew[64:128, :], in_=pA[64:128, 128:256]),
                              reads=[RA], writes=[R_vnew])
                    yield
                    sc.op("pe", lambda h, j=j, c=c: h.matmul(pB[:, c * 128:(c + 1) * 128], lhsT=QT[:, j, :], rhs=Sb[:],
                                                             start=True, stop=True),
                          reads=[R_QT, R_S], writes=[RB])
                    sc.op("pe", lambda h, j=j, kd=kd: h.matmul(pC, lhsT=kd[:, j, :], rhs=vnew[:],
                                                               start=True, stop=True),
                          reads=[Rkd, R_vnew], writes=[RC])
                    if c == 1:
                        sc.op("pe", lambda h, j=j: h.matmul(pB[:, 256:384], lhsT=qkT[:, j, :], rhs=vnew[:],
                                                            start=True, stop=True),
                              reads=[R_qkT, R_vnew], writes=[RB])
                    sc.op("dve", lambda h, c=c, t=t: h.scalar_tensor_tensor(
                        out=Sb[:], in0=Sf[:], scalar=egl[:, c, t, hd:hd + 1], in1=pC, op0=ALU.mult, op1=ALU.add),
                        reads=[R_S, RC, R_sc], writes=[R_S])
                    sc.op("dve", lambda h, c=c, t=t: h.scalar_tensor_tensor(
                        out=Sf[:], in0=Sf[:], scalar=egl[:, c, t, hd:hd + 1], in1=pC, op0=ALU.mult, op1=ALU.add),
                        reads=[R_S, RC, R_sc], writes=[R_S])
                    yield
                p2 = p2s[j4 % 2]
                Rp2 = R_p2s[j4 % 2]
                sc.op("act", lambda h, p2=p2: h.activation(out=p2[:], in_=pB[:, 256:384], func=AF.Copy),
                      reads=[RB], writes=[Rp2])
                for c in range(2):
                    rs = slice(64 * c, 64 * c + 64)
                    sc.op("dve", lambda h, c=c, rs=rs, j4=j4, t=t, p2=p2: h.scalar_tensor_tensor(
                        out=osb[rs, j4, :], in0=pB[rs, c * 128:(c + 1) * 128], scalar=egc[rs, t, hd:hd + 1],
                        in1=p2[rs, :], op0=ALU.mult, op1=ALU.add), reads=[RB, Rp2, R_sc], writes=[R_osb])
                sc.op("act", lambda h, j4=j4: h.activation(out=junk[:], in_=osb[:, j4, :], func=AF.Square,
                                                           accum_out=oss[:, j4:j4 + 1]), reads=[R_osb], writes=[R_oss])
                yield
            for j4 in range(4):
                yield from tile_steps(j4)
            sc.op("dve", lambda h: h.tensor_scalar(out=oss[:, 4:8], in0=oss[:, 0:4], scalar1=1.0 / 128, scalar2=EPS,
                                                   op0=ALU.mult, op1=ALU.add), reads=[R_oss], writes=[R_oss])
            sc.op("act", lambda h: h.activation(out=oss[:, 4:8], in_=oss[:, 4:8], func=AF.Ln), reads=[R_oss], writes=[R_oss])
            sc.op("act", lambda h: h.activation(out=oss[:, 4:8], in_=oss[:, 4:8], func=AF.Exp, scale=-0.5),
                  reads=[R_oss], writes=[R_oss])
            pO = pB[:].bitcast(BF16)
            for j in range(4):
                t = T0 + j
                sc.op("dve", lambda h, j=j, t=t: h.scalar_tensor_tensor(
                    out=otk[:, j, :], in0=osb[:, j, :], scalar=oss[:, 4 + j:5 + j], in1=zw[:, t, :],
                    op0=ALU.mult, op1=ALU.mult), reads=[R_osb, R_oss, R_zw], writes=[R_otk])
                sc.op("pe", lambda h, j=j: h.transpose(pO[:, j * 128:(j + 1) * 128], otk[:, j, :], ident[:]),
                      reads=[R_otk, R_const], writes=[RB])
            sc.op("act", lambda h: h.activation(out=catD[:, hd, T0 * 128:(T0 + 4) * 128], in_=pO[:, 0:512],
                                                func=AF.Copy), reads=[RB], writes=[R_catD[hd]])
            sc.op("dve", lambda h: h.memset(oss[:, 0:4], 0.0), reads=[R_oss], writes=[R_oss])
            yield

        run_interleaved([stage_C(0)])
        for hd in range(4):
            run_interleaved([stage_A(0, hd, 0), stage_A(0, hd, 1)])
            for G in range(4):
                tasks = [stage_B(G, hd)]
                if G < 3:
                    tasks.append(stage_A(G + 1, hd, 0))
                    tasks.append(stage_A(G + 1, hd, 1))
                elif hd < 3:
                    tasks.append(stage_C(hd + 1))
                run_interleaved(tasks)
        end_phase()
        e.close()


    def postnorm_residual(t, pa, pb, wpost, R_wpost, tmpb, R_tmpb, ssb, R_ssb, junkp):
        sc.op("act", lambda h: h.activation(out=junkp[:], in_=psum[pa][:], func=AF.Square, accum_out=ssb[:, 0:1]),
              reads=[R_ps[pa]], writes=[R_ssb])
        sc.op("act", lambda h: h.activation(out=junkp[:], in_=psum[pb][:], func=AF.Square, accum_out=ssb[:, 1:2]),
              reads=[R_ps[pb]], writes=[R_ssb])
        sc.op("dve", lambda h: h.tensor_tensor(out=ssb[:, 2:3], in0=ssb[:, 0:1], in1=ssb[:, 1:2], op=ALU.add),
              reads=[R_ssb], writes=[R_ssb])
        sc.op("dve", lambda h: h.tensor_scalar(out=ssb[:, 2:3], in0=ssb[:, 2:3], scalar1=1.0 / D, scalar2=EPS,
                                               op0=ALU.mult, op1=ALU.add), reads=[R_ssb], writes=[R_ssb])
        sc.op("act", lambda h: h.activation(out=ssb[:, 2:3], in_=ssb[:, 2:3], func=AF.Sqrt), reads=[R_ssb], writes=[R_ssb])
        sc.op("dve", lambda h: h.reciprocal(out=ssb[:, 3:4], in_=ssb[:, 2:3]), reads=[R_ssb], writes=[R_ssb])
        sc.op("dve", lambda h: h.tensor_tensor(out=tmpb[:, 0:512], in0=psum[pa][:], in1=wpost[:, 0:512], op=ALU.mult),
              reads=[R_ps[pa], R_wpost], writes=[R_tmpb])
        sc.op("dve", lambda h: h.tensor_tensor(out=tmpb[:, 512:1024], in0=psum[pb][:], in1=wpost[:, 512:1024], op=ALU.mult),
              reads=[R_ps[pb], R_wpost], writes=[R_tmpb])
        sc.op("dve", lambda h: h.scalar_tensor_tensor(out=x_sb[:, t, :], in0=tmpb[:], scalar=ssb[:, 3:4], in1=x_sb[:, t, :],
                                                      op0=ALU.mult, op1=ALU.add),
              reads=[R_tmpb, R_ssb, R_x[t]], writes=[R_x[t]])
        sc.op("dve", lambda h: h.memset(ssb[:, 0:2], 0.0), reads=[R_ssb], writes=[R_ssb])

    def outproj_phase(l, catA, R_catA, catD, R_catD):
        e = ExitStack()
        Wo = sb(e, "w_o", [128, KC, D], BF16)
        R_Wo = Res("wo")
        phase_load(Wo[:], w_o_d[l], R_Wo, eng="pool")
        wpost = sb(e, "wpost", [128, D], F32)
        R_wpost = Res("wpost")
        phase_load(wpost[:], npost_d[:, l, 0, :], R_wpost)
        nwc = sb(e, "nwc", [128, 1], F32)
        R_nwc = Res("nwc")
        phase_load(nwc[:], dn_nwT_d[l], R_nwc)
        sc.op("dve", lambda h: h.tensor_scalar(out=Wo[:, 4:8, :], in0=Wo[:, 4:8, :], scalar1=nwc[:, 0:1], scalar2=None,
                                               op0=ALU.mult), reads=[R_Wo, R_nwc], writes=[R_Wo])
        tmpb = [sb(e, "tmpb%d" % i, [128, D], F32) for i in range(2)]
        R_tmpb = [Res("tmpb%d" % i) for i in range(2)]
        ssb = [sb(e, "ssb%d" % i, [128, 4], F32) for i in range(2)]
        R_ssb = [Res("ssb%d" % i) for i in range(2)]
        junkp = sb(e, "junkp", [128, 512], BF16)
        for i in range(2):
            sc.op("pool", lambda h, i=i: h.memset(ssb[i][:], 0.0), writes=[R_ssb[i]])
        for t in range(NT):
            pa, pb = (0, 1) if t % 2 == 0 else (2, 3)
            for n, pk in ((0, pa), (1, pb)):
                for k in range(KC):
                    src = catA if k < 4 else catD
                    Rsrc = R_catA[k] if k < 4 else R_catD[k - 4]
                    sc.op("pe", lambda h, k=k, n=n, pk=pk, t=t, src=src: h.matmul(
                        psum[pk][:], lhsT=src[:, k % 4, t * 128:(t + 1) * 128], rhs=Wo[:, k, n * 512:(n + 1) * 512],
                        start=(k == 0), stop=(k == KC - 1)), reads=[Rsrc, R_Wo], writes=[R_ps[pk]])
            postnorm_residual(t, pa, pb, wpost, R_wpost, tmpb[t % 2], R_tmpb[t % 2], ssb[t % 2], R_ssb[t % 2], junkp)
        end_phase()
        e.close()

    def ffn_phase(l):
        e = ExitStack()
        fcw = sb(e, "fcw", [128, 2 * NFC, 4], F32)
        R_fcw = Res("fcw")
        phase_load(fcw[:], fcw_d[:, l], R_fcw)
        halo = sb(e, "halo", [128, 2 * NFC, 2], F32)
        R_halo = Res("halo")
        hTh = sb(e, "hTh", [128, KC, 1024], BF16)
        R_hTh = [Res("hTh%d" % i) for i in range(8)]
        actT = sb(e, "actT", [128, NFC, 1024], BF16)
        R_act = [Res("act%d" % i) for i in range(NFC)]
        W2 = sb(e, "w_f2", [128, NFC, D], BF16)
        R_W2 = Res("w2")
        q_w2 = sc.dma_sem("wf2_%d" % l)
        q_w1 = [sc.dma_sem("wf1_%d_%d" % (l, i)) for i in range(2)]
        for hf in range(2):
            if hf == 0:
                sc.dma("pool", q_w2, lambda h: h.dma_start(out=W2[:, 0:11, :], in_=w_f2_d[l, :, 0:11, :]), writes=[R_W2])
                sc.dma("pool", q_w2, lambda h: h.dma_start(out=W2[:, 11:22, :], in_=w_f2_d[l, :, 11:22, :]), writes=[R_W2])
                R_W2.w = (q_w2, sc.cnt[q_w2])
            eu = ExitStack()
            W1 = [sb(eu, "w_f1_%d" % i, [128, KC, 256], BF16) for i in range(2)]
            R_W1 = [Res("w1_%d" % i) for i in range(2)]
            xpd = [[sb(eu, "fxp%d_%d" % (p, i), [128, 2 + 512], F32) for i in range(2)] for p in range(2)]
            R_xpd = [[Res("fxp%d_%d" % (p, i)) for i in range(2)] for p in range(2)]
            ycv = [[sb(eu, "fy%d_%d" % (p, i), [128, 512], F32) for i in range(2)] for p in range(2)]
            R_ycv = [[Res("fy%d_%d" % (p, i)) for i in range(2)] for p in range(2)]
            ggl = [sb(eu, "ggl%d" % i, [128, 512], F32) for i in range(2)]
            R_ggl = [Res("ggl%d" % i) for i in range(2)]

            def load_w1(jc):
                i = jc % 2
                sc.dma("pool", q_w1[i], lambda h, i=i, jc=jc: h.dma_start(out=W1[i][:], in_=w_f1_d[l, jc]), writes=[R_W1[i]])
            load_w1(0)
            prenorm(l, 1, hTh, list(range(hf * 8, hf * 8 + 8)), R_hTh)
            pend_glu = [None]

            def emit_glu(bi, jc, gi):
                gb, Rgb = ggl[bi], R_ggl[bi]
                sc.op("act", lambda h: h.activation(out=gb[:], in_=ycv[0][bi][:], func=AF.Gelu_apprx_tanh),
                      reads=[R_ycv[0][bi]], writes=[Rgb])
                sc.op("pool", lambda h: h.tensor_tensor(
                    out=actT[:, jc, gi * 512:(gi + 1) * 512], in0=gb[:], in1=ycv[1][bi][:], op=ALU.mult),
                    reads=[Rgb, R_ycv[1][bi]], writes=[R_act[jc]])

            for jc in range(NFC):
                if jc + 1 < NFC:
                    load_w1(jc + 1)
                Wc = W1[jc % 2]
                RWc = R_W1[jc % 2]
                for gi in range(2):
                    g = hf * 2 + gi
                    bi = (jc * 2 + gi) % 2
                    for part in range(2):
                        ch = part * NFC + jc
                        pk = part * 4 + (jc * 2 + gi) % 4
                        xb, Rxb = xpd[part][bi], R_xpd[part][bi]
                        xprev, Rxprev = xpd[part][1 - bi], R_xpd[part][1 - bi]
                        yb, Ryb = ycv[part][bi], R_ycv[part][bi]
                        for k in range(KC):
                            sc.op("pe", lambda h, k=k, part=part, pk=pk, gi=gi, Wc=Wc: h.matmul(
                                psum[pk][:], lhsT=Wc[:, k, part * 128:(part + 1) * 128], rhs=hTh[:, k, gi * 512:(gi + 1) * 512],
                                start=(k == 0), stop=(k == KC - 1)), reads=[RWc] + R_hTh[gi * 4:(gi + 1) * 4], writes=[R_ps[pk]])
                        sc.op("act", lambda h, pk=pk, xb=xb: h.activation(out=xb[:, 2:514], in_=psum[pk][:], func=AF.Copy),
                              reads=[R_ps[pk]], writes=[Rxb])
                        if g == 0:
                            sc.op("pool", lambda h, xb=xb: h.memset(xb[:, 0:2], 0.0), writes=[Rxb])
                        elif gi == 0:
                            sc.op("pool", lambda h, xb=xb, ch=ch: h.tensor_copy(out=xb[:, 0:2], in_=halo[:, ch, :]),
                                  reads=[R_halo], writes=[Rxb])
                        else:
                            sc.op("pool", lambda h, xb=xb, xprev=xprev: h.tensor_copy(out=xb[:, 0:2], in_=xprev[:, 512:514]),
                                  reads=[Rxprev], writes=[Rxb])
                        if hf == 0 and gi == 1:
                            sc.op("pool", lambda h, xb=xb, ch=ch: h.tensor_copy(out=halo[:, ch, :], in_=xb[:, 512:514]),
                                  reads=[Rxb], writes=[R_halo])
                        sc.op("act", lambda h, pk=pk, yb=yb, ch=ch: h.activation(
                            out=yb[:], in_=psum[pk][:], func=AF.Identity, scale=fcw[:, ch, 2:3], bias=fcw[:, ch, 3:4]),
                            reads=[R_ps[pk], R_fcw], writes=[Ryb])
                        for j in range(2):
                            sc.op("dve", lambda h, j=j, yb=yb, xb=xb, ch=ch: h.scalar_tensor_tensor(
                                out=yb[:], in0=xb[:, j:j + 512], scalar=fcw[:, ch, j:j + 1], in1=yb[:],
                                op0=ALU.mult, op1=ALU.add), reads=[Rxb, Ryb, R_fcw], writes=[Ryb])
                    if pend_glu[0] is not None:
                        emit_glu(*pend_glu[0])
                    pend_glu[0] = (bi, jc, gi)
            emit_glu(*pend_glu[0])
            pend_glu[0] = None
            sc.barrier()
            sc.emit_all()
            eu.close()
            ed = ExitStack()
            wpost = sb(ed, "wpost", [128, D], F32)
            R_wpost = Res("wpost")
            phase_load(wpost[:], npost_d[:, l, 1, :], R_wpost)
            tmpb = [sb(ed, "tmpb%d" % i, [128, D], F32) for i in range(2)]
            R_tmpb = [Res("tmpb%d" % i) for i in range(2)]
            ssb = [sb(ed, "ssb%d" % i, [128, 4], F32) for i in range(2)]
            R_ssb = [Res("ssb%d" % i) for i in range(2)]
            junkp = sb(ed, "junkp", [128, 512], BF16)
            for i in range(2):
                sc.op("pool", lambda h, i=i: h.memset(ssb[i][:], 0.0), writes=[R_ssb[i]])
            for tl in range(8):
                t = hf * 8 + tl
                pa, pb = (0, 1) if tl % 2 == 0 else (2, 3)
                for n, pk in ((0, pa), (1, pb)):
                    for jc in range(NFC):
                        sc.op("pe", lambda h, jc=jc, n=n, pk=pk, tl=tl: h.matmul(
                            psum[pk][:], lhsT=actT[:, jc, tl * 128:(tl + 1) * 128], rhs=W2[:, jc, n * 512:(n + 1) * 512],
                            start=(jc == 0), stop=(jc == NFC - 1)), reads=[R_act[jc], R_W2], writes=[R_ps[pk]])
                postnorm_residual(t, pa, pb, wpost, R_wpost, tmpb[tl % 2], R_tmpb[tl % 2], ssb[tl % 2], R_ssb[tl % 2], junkp)
                if l == depth - 1:
                    sc.dma("sp", q_out, lambda h, t=t: h.dma_start(out=y_d[t * 128:(t + 1) * 128, :], in_=x_sb[:, t, :]),
                           reads=[R_x[t]])
            sc.barrier()
            sc.emit_all()
            ed.close()
        e.close()

    for l in range(depth):
        e_mix = ExitStack()
        hT = sb(e_mix, "hT", [128, KC, S], BF16)
        hT_box[0] = hT
        R_hT[:] = [Res("hT%d" % t) for t in range(NT)]
        catA = sb(e_mix, "catA", [128, 4, S], BF16)
        R_catA = [Res("catA%d" % c) for c in range(4)]
        attention_phase(l, catA, R_catA, pre=lambda: prenorm(l, 0, hT, list(range(NT)), R_hT))
        if dbg and l == 0:
            sc.dma("sp", q_out, lambda h: h.dma_start(out=dbg_d["catA"], in_=catA[:]), reads=R_catA)
            end_phase()
        catD = sb(e_mix, "catD", [128, 4, S], BF16)
        R_catD = [Res("catD%d" % c) for c in range(4)]
        if stage >= 2:
            deltanet_phase(l, catD, R_catD)
        if dbg and l == 0:
            sc.dma("sp", q_out, lambda h: h.dma_start(out=dbg_d["catD"], in_=catD[:]), reads=R_catD)
            end_phase()
        if stage >= 3:
            outproj_phase(l, catA, R_catA, catD, R_catD)
        e_mix.close()
        if dbg and l == 0:
            sc.dma("sp", q_out, lambda h: h.dma_start(out=dbg_d["xmid"].rearrange("(t p) d -> p t d", p=128), in_=x_sb[:]),
                   reads=R_x)
            end_phase()
        if stage >= 4:
            ffn_phase(l)

    if stage < 4:
        yv = y_d.rearrange("(t p) d -> p t d", p=128)
        sc.dma("sp", q_out, lambda h: h.dma_start(out=yv, in_=x_sb[:]), reads=R_x)
    sc.wait_all("sp", [(q_out, sc.cnt[q_out])])
    sc.emit_all()
    es.close()
    return nc, sc


def host_prep(inputs, depth=DEPTH):
    f32 = np.float32
    w_in = np.asarray(inputs["w_in"], f32)
    out = {}
    w_att = np.empty((depth, 4, 128, KC, 640), f32)
    for l in range(depth):
        Wl = w_in[l].reshape(KC, 128, -1)
        for hp in range(4):
            cols = []
            for base in (0, 512):
                c = np.arange(base + hp * 128, base + hp * 128 + 128)
                cols.append(c)
                cs = c.reshape(2, 2, 32)[:, ::-1, :].reshape(-1)
                cols.append(cs)
            cols.append(np.arange(1024 + hp * 128, 1024 + hp * 128 + 128))
            cols = np.concatenate(cols)
            w_att[l, hp] = Wl[:, :, cols].transpose(1, 0, 2)
    out["w_att"] = w_att
    out["ident"] = np.eye(128, dtype=f32)
    inv = 1.0 / (10000.0 ** (np.arange(0, 64, 2, dtype=np.float32) / 64.0))
    ang = np.arange(S, dtype=np.float32)[None, :] * inv[:, None].astype(np.float32)
    cos = np.cos(ang).astype(f32)
    sin = np.sin(ang).astype(f32)
    rc = np.empty((128, S), f32)
    rs = np.empty((128, S), f32)
    for p in range(128):
        rc[p] = cos[p % 32]
        rs[p] = -sin[p % 32] if (p % 64) < 32 else sin[p % 32]
    out["rope_c"] = rc
    out["rope_s"] = rs
    c = np.arange(128)[:, None]
    a = np.arange(128)[None, :]
    Lm = np.where(c >= a, 0.0, -30000.0).astype(f32)
    Um = np.where(c <= a, 0.0, -30000.0).astype(f32)
    out["amask"] = np.concatenate([Lm, Um, Lm, Um, Lm, Um, Lm, Um], axis=1)
    npre = np.empty((128, depth, 2, KC), f32)
    for l in range(depth):
        npre[:, l, 0, :] = np.asarray(inputs["norm_pre_mix"], f32)[l].reshape(KC, 128).T
        npre[:, l, 1, :] = np.asarray(inputs["norm_pre_ffn"], f32)[l].reshape(KC, 128).T
    out["npre"] = npre
    w_dn = np.empty((depth, 4, 128, KC, 512), f32)
    w_ba = np.empty((depth, 128, KC, 8), f32)
    dn_cw = np.empty((128, depth, 4, 3, 4), f32)
    dn_hp = np.empty((128, depth, 2, 4), f32)
    dn_nw = np.empty((depth, 128, 1), f32)
    cwl = np.asarray(inputs["dn_conv_w"], f32)
    for l in range(depth):
        Wl = w_in[l].reshape(KC, 128, -1)
        for hd in range(4):
            cols = np.concatenate([np.arange(b + hd * 128, b + hd * 128 + 128) for b in (1536, 2048, 2560, 3072)])
            w_dn[l, hd] = Wl[:, :, cols].transpose(1, 0, 2)
            for cc in range(3):
                ch = cc * 512 + hd * 128 + np.arange(128)
                dn_cw[:, l, hd, cc, :] = cwl[l][:, ch].T
        w_ba[l] = Wl[:, :, 3584:3592].transpose(1, 0, 2)
        dn_hp[:, l, 0, :] = np.asarray(inputs["dn_a_log"], f32)[l][None, :]
        dn_hp[:, l, 1, :] = np.asarray(inputs["dn_dt_bias"], f32)[l][None, :]
        dn_nw[l, :, 0] = np.asarray(inputs["dn_norm_w"], f32)[l]
    out["w_dn"] = w_dn
    out["w_ba"] = w_ba
    out["dn_cw"] = dn_cw
    out["dn_hp"] = dn_hp
    out["dn_nwT"] = dn_nw
    ti = np.arange(128)[:, None]
    tj = np.arange(128)[None, :]
    same = (ti // 64) == (tj // 64)
    dn_c = np.zeros((128, 8, 128), f32)
    dn_c[:, 0] = (same & (ti <= tj))
    dn_c[:, 1] = (same & (ti > tj))
    dn_c[:, 2] = np.where(same & (ti > tj), 0.0, -30000.0)
    dn_c[:, 3] = np.where(same & (tj >= ti), 0.0, -30000.0)
    dn_c[:, 4] = same
    dn_c[:, 5] = (ti < 64) & (tj >= 0)
    dn_c[:, 6] = (ti >= 64) & (tj >= 0)
    dn_c[:, 7] = np.eye(128)
    out["dn_c"] = dn_c
    w_out = np.asarray(inputs["w_out"], f32)
    out["w_o"] = np.ascontiguousarray(w_out.reshape(depth, KC, 128, D).transpose(0, 2, 1, 3))
    npost = np.empty((128, depth, 2, D), f32)
    npost[:, :, 0, :] = np.asarray(inputs["norm_post_mix"], f32)[None]
    npost[:, :, 1, :] = np.asarray(inputs["norm_post_ffn"], f32)[None]
    out["npost"] = npost
    fw1 = np.asarray(inputs["ffn_w_in"], f32).reshape(depth, KC, 128, 2, NFC, 128)
    out["w_f1"] = np.ascontiguousarray(fw1.transpose(0, 4, 2, 1, 3, 5).reshape(depth, NFC, 128, KC, 256))
    fw2 = np.asarray(inputs["ffn_w_out"], f32).reshape(depth, NFC, 128, D)
    out["w_f2"] = np.ascontiguousarray(fw2.transpose(0, 2, 1, 3))
    fcw = np.empty((128, depth, 2 * NFC, 4), f32)
    cwf = np.asarray(inputs["ffn_conv_w"], f32).reshape(depth, 3, 2 * NFC, 128)
    cbf = np.asarray(inputs["ffn_conv_b"], f32).reshape(depth, 2 * NFC, 128)
    fcw[:, :, :, 0:3] = cwf.transpose(3, 0, 2, 1)
    fcw[:, :, :, 3] = cbf.transpose(2, 0, 1)
    out["fcw"] = fcw
    return out


_CACHE = {}


def kernel(**inputs):
    x = np.ascontiguousarray(np.asarray(inputs["x"], np.float32))
    shared = host_prep(inputs)
    if "nc" not in _CACHE:
        _CACHE["nc"] = build_program()[0]
    nc = _CACHE["nc"]
    in_maps = []
    for c in range(N_CORES):
        m = dict(shared)
        m["x"] = x[c]
        in_maps.append(m)
    res = run_bass_kernel_spmd(nc, in_maps, core_ids=list(range(N_CORES)))
    return np.stack([np.asarray(r["y"], np.float32) for r in res.results], axis=0)
```
